# Optimizing a Trainium2 kernel written in Bass

```python
import math
import jax
import jax.numpy as jnp
from jax import lax
import numpy as np

D_MODEL = 1024
BATCH = 8
SEQ = 4096
DEPTH = 2

CTX_LEN = 256
GRID_W = 64
EPS = 1e-6
ROPE_THETA = 10000.0
Q_BLOCK = 128
WINDOW = 128
NEG = -1e30

HA = 8
A_NOPE = 64
A_ROPE = 32
A_V = 64
A_Q_RANK = 256
A_KV_RANK = 128
HB = 8
HB_KV = 2
B_HD = 64
HC = 8
HC_KV = 2
C_HD = 64

BRANCH_W = 512
N_BRANCH = 3
D_FF = 2816
CONV_W = 3

IN_SIZES = (A_Q_RANK, A_KV_RANK, A_ROPE, HB * B_HD, HB_KV * B_HD, HB_KV * B_HD, HC * C_HD, HC_KV * C_HD, HC_KV * C_HD, N_BRANCH * D_MODEL)
IN_COLS = sum(IN_SIZES)

kernel_name = 'hybrid_mla_gqa_swa_convffn_trunk'


def rms_norm(x, g):
    xf = x.astype(jnp.float32)
    y = xf * lax.rsqrt(jnp.mean(xf * xf, axis=-1, keepdims=True) + EPS)
    return (y * g.astype(jnp.float32)).astype(x.dtype)


def modulate(x, g, shift, scale):
    return rms_norm(x, g) * (1 + scale) + shift


def axial_rope_table(rows, head_dim):
    n_freq = head_dim // 4
    row = jnp.repeat(jnp.arange(rows, dtype=jnp.float32), GRID_W)
    col = jnp.tile(jnp.arange(GRID_W, dtype=jnp.float32), rows)
    inv = ROPE_THETA ** (-jnp.arange(n_freq, dtype=jnp.float32) / n_freq)
    ang = jnp.stack([row[:, None] * inv, col[:, None] * inv], axis=1)
    return jnp.cos(ang), jnp.sin(ang)


def apply_rope(x, table):
    if table is None:
        return x
    cos, sin = table
    shp = x.shape
    nf = shp[-1] // 4
    xr = x.reshape(shp[:-1] + (2, 2, nf))
    bshape = (1, shp[1]) + (1,) * (x.ndim - 3) + (2, nf)
    c = cos.reshape(bshape).astype(x.dtype)
    s = sin.reshape(bshape).astype(x.dtype)
    x0 = xr[..., 0, :]
    x1 = xr[..., 1, :]
    return jnp.stack([x0 * c - x1 * s, x1 * c + x0 * s], axis=-2).reshape(shp)


def blocked_attention(q, k, v, scale, sink=None):
    b, lq, hkv, g, dk = q.shape
    nblk = lq // Q_BLOCK
    n_keys = k.shape[1]
    qb = q.reshape(b, nblk, Q_BLOCK, hkv, g, dk).transpose(1, 0, 2, 3, 4, 5)

    def one_block(qblk):
        s = jnp.einsum('bqhgd,bkhd->bhgqk', qblk, k).astype(jnp.float32) * scale
        if sink is not None:
            sk = jnp.broadcast_to(sink.astype(jnp.float32).reshape(1, hkv, g, 1, 1), s.shape[:-1] + (1,))
            s = jnp.concatenate([s, sk], axis=-1)
        p = jax.nn.softmax(s, axis=-1)[..., :n_keys].astype(v.dtype)
        return jnp.einsum('bhgqk,bkhd->bqhgd', p, v)

    out = lax.map(one_block, qb)
    return out.transpose(1, 0, 2, 3, 4, 5).reshape(b, lq, hkv * g * v.shape[-1])


def window_attention(q, k, v, k_ctx, v_ctx, sink, scale):
    b, n, hkv, g, d = q.shape
    nblk = n // Q_BLOCK
    n_ctx = k_ctx.shape[1]
    band = 3 * Q_BLOCK
    qb = q.reshape(b, nblk, Q_BLOCK, hkv, g, d).transpose(1, 0, 2, 3, 4, 5)

    def banded(t):
        tp = jnp.pad(t, ((0, 0), (Q_BLOCK, Q_BLOCK), (0, 0), (0, 0)))
        tp = tp.reshape(b, nblk + 2, Q_BLOCK, hkv, t.shape[-1])
        return jnp.concatenate([tp[:, :-2], tp[:, 1:-1], tp[:, 2:]], axis=2).transpose(1, 0, 2, 3, 4)

    kw = banded(k)
    vw = banded(v)
    blk = jnp.arange(nblk)[:, None, None]
    qi = jnp.arange(Q_BLOCK)[None, :, None]
    kj = jnp.arange(band)[None, None, :]
    rel = kj - qi
    j_abs = blk * Q_BLOCK - Q_BLOCK + kj
    valid = (rel >= Q_BLOCK - WINDOW) & (rel <= Q_BLOCK + WINDOW) & (j_abs >= 0) & (j_abs < n)
    sink_f = sink.astype(jnp.float32).reshape(1, hkv, g, 1, 1)

    def one_block(args):
        qblk, kblk, vblk, mask = args
        s_ctx = jnp.einsum('bqhgd,bkhd->bhgqk', qblk, k_ctx).astype(jnp.float32) * scale
        s_win = jnp.einsum('bqhgd,bkhd->bhgqk', qblk, kblk).astype(jnp.float32) * scale
        s_win = jnp.where(mask, s_win, NEG)
        s_sink = jnp.broadcast_to(sink_f, s_win.shape[:-1] + (1,))
        p = jax.nn.softmax(jnp.concatenate([s_ctx, s_win, s_sink], axis=-1), axis=-1).astype(v.dtype)
        return (jnp.einsum('bhgqk,bkhd->bqhgd', p[..., :n_ctx], v_ctx)
                + jnp.einsum('bhgqk,bkhd->bqhgd', p[..., n_ctx:n_ctx + band], vblk))

    out = lax.map(one_block, (qb, kw, vw, valid))
    return out.transpose(1, 0, 2, 3, 4, 5).reshape(b, n, hkv * g * v.shape[-1])


def project_mixers(h, w_in, b_gate, g_q_a, w_q_b, g_kv_a, w_kv_b, g_qn, g_kn, rope_a, rope_h):
    b, n, _ = h.shape
    cuts = []
    acc = 0
    for size in IN_SIZES[:-1]:
        acc += size
        cuts.append(acc)
    aq, akv, akr, bq, bk, bv, cq, ck, cv, gl = jnp.split(h @ w_in, cuts, axis=-1)
    q = (rms_norm(aq, g_q_a) @ w_q_b).reshape(b, n, HA, A_NOPE + A_ROPE)
    kv = (rms_norm(akv, g_kv_a) @ w_kv_b).reshape(b, n, HA, A_NOPE + A_V)
    k_rope = apply_rope(akr, rope_a)
    q_a = jnp.concatenate([q[..., :A_NOPE], apply_rope(q[..., A_NOPE:], rope_a)], axis=-1)[:, :, :, None, :]
    k_a = jnp.concatenate([kv[..., :A_NOPE], jnp.broadcast_to(k_rope[:, :, None, :], (b, n, HA, A_ROPE))], axis=-1)
    v_a = kv[..., A_NOPE:]
    q_b = apply_rope(rms_norm(bq.reshape(b, n, HB_KV, HB // HB_KV, B_HD), g_qn), rope_h)
    k_b = apply_rope(rms_norm(bk.reshape(b, n, HB_KV, B_HD), g_kn), rope_h)
    v_b = bv.reshape(b, n, HB_KV, B_HD)
    q_c = apply_rope(cq.reshape(b, n, HC_KV, HC // HC_KV, C_HD), rope_h)
    k_c = apply_rope(ck.reshape(b, n, HC_KV, C_HD), rope_h)
    v_c = cv.reshape(b, n, HC_KV, C_HD)
    gates = jax.nn.sigmoid(gl + b_gate).reshape(b, n, N_BRANCH, D_MODEL)
    return (q_a, k_a, v_a, q_b, k_b, v_b, q_c, k_c, v_c, gates)


def merge_branches(outs, gates, w_branch, w_out):
    y = sum(gates[:, :, i] * (o @ w_branch[i]) for i, o in enumerate(outs))
    return y @ w_out


def conv_ffn(h, w_up, w_conv, b_conv, w_down):
    n = h.shape[1]
    half = CONV_W // 2
    u = jnp.pad(h @ w_up, ((0, 0), (half, half), (0, 0)))
    u = sum(u[:, j:j + n] * w_conv[j] for j in range(CONV_W)) + b_conv
    a, gv = jnp.split(u, 2, axis=-1)
    return (jax.nn.silu(a) * gv) @ w_down


def setup_inputs(seed: int = 0) -> dict:
    key = jax.random.key(seed)
    ks = jax.random.split(key, 24)

    def nrm(k, shape, scale):
        return jax.random.normal(k, shape, jnp.float32) * scale

    def gain(k, shape):
        return 1.0 + 0.02 * jax.random.normal(k, shape, jnp.float32)

    return {
        'x': nrm(ks[0], (BATCH, SEQ, D_MODEL), 1.0),
        'c': nrm(ks[1], (BATCH, D_MODEL), 1.0),
        'ctx': nrm(ks[2], (BATCH, CTX_LEN, D_MODEL), 1.0),
        'c_ctx': nrm(ks[3], (D_MODEL,), 1.0),
        'w_mod': nrm(ks[4], (DEPTH, D_MODEL, 6 * D_MODEL), 0.5 * D_MODEL ** -0.5),
        'b_mod': nrm(ks[5], (DEPTH, 6 * D_MODEL), 0.01),
        'g_norm1': gain(ks[6], (DEPTH, D_MODEL)),
        'w_in': nrm(ks[7], (DEPTH, D_MODEL, IN_COLS), D_MODEL ** -0.5),
        'b_gate': nrm(ks[8], (DEPTH, N_BRANCH * D_MODEL), 0.02),
        'g_q_a': gain(ks[9], (DEPTH, A_Q_RANK)),
        'w_q_b': nrm(ks[10], (DEPTH, A_Q_RANK, HA * (A_NOPE + A_ROPE)), A_Q_RANK ** -0.5),
        'g_kv_a': gain(ks[11], (DEPTH, A_KV_RANK)),
        'w_kv_b': nrm(ks[12], (DEPTH, A_KV_RANK, HA * (A_NOPE + A_V)), A_KV_RANK ** -0.5),
        'g_qn': gain(ks[13], (DEPTH, B_HD)),
        'g_kn': gain(ks[14], (DEPTH, B_HD)),
        'sink': nrm(ks[15], (DEPTH, HC), 0.5),
        'w_branch': nrm(ks[16], (DEPTH, N_BRANCH, BRANCH_W, D_MODEL), BRANCH_W ** -0.5),
        'w_out': nrm(ks[17], (DEPTH, D_MODEL, D_MODEL), D_MODEL ** -0.5),
        'g_norm2': gain(ks[18], (DEPTH, D_MODEL)),
        'w_up': nrm(ks[19], (DEPTH, D_MODEL, 2 * D_FF), D_MODEL ** -0.5),
        'w_conv': nrm(ks[20], (DEPTH, CONV_W, 2 * D_FF), CONV_W ** -0.5),
        'b_conv': nrm(ks[21], (DEPTH, 2 * D_FF), 0.02),
        'w_down': nrm(ks[22], (DEPTH, D_FF, D_MODEL), D_FF ** -0.5),
        'g_final': gain(ks[23], (D_MODEL,)),
    }


def reference(x, c, ctx, c_ctx, w_mod, b_mod, g_norm1, w_in, b_gate, g_q_a, w_q_b, g_kv_a, w_kv_b, g_qn, g_kn, sink, w_branch, w_out, g_norm2, w_up, w_conv, b_conv, w_down, g_final):
    n = x.shape[1]
    rows = n // GRID_W
    rope_a = axial_rope_table(rows, A_ROPE)
    rope_h = axial_rope_table(rows, B_HD)
    scale_a = 1.0 / math.sqrt(A_NOPE + A_ROPE)
    scale_h = 1.0 / math.sqrt(B_HD)
    x_lat = x
    x_ctx = ctx
    for l in range(DEPTH):
        m_lat = jnp.split((jax.nn.silu(c) @ w_mod[l] + b_mod[l])[:, None, :], 6, axis=-1)
        m_ctx = jnp.split(jax.nn.silu(c_ctx) @ w_mod[l] + b_mod[l], 6, axis=-1)
        lw = (w_in[l], b_gate[l], g_q_a[l], w_q_b[l], g_kv_a[l], w_kv_b[l], g_qn[l], g_kn[l])
        qa, ka, va, qb, kb, vb, qc, kc, vc, gates = project_mixers(
            modulate(x_lat, g_norm1[l], m_lat[0], m_lat[1]), *lw, rope_a, rope_h)
        cqa, cka, cva, cqb, ckb, cvb, cqc, ckc, cvc, cgates = project_mixers(
            modulate(x_ctx, g_norm1[l], m_ctx[0], m_ctx[1]), *lw, None, None)
        o_a = blocked_attention(qa, jnp.concatenate([cka, ka], axis=1), jnp.concatenate([cva, va], axis=1), scale_a)
        o_b = blocked_attention(qb, jnp.concatenate([ckb, kb], axis=1), jnp.concatenate([cvb, vb], axis=1), scale_h)
        o_c = window_attention(qc, kc, vc, ckc, cvc, sink[l], scale_h)
        x_lat = x_lat + m_lat[2] * merge_branches((o_a, o_b, o_c), gates, w_branch[l], w_out[l])
        x_lat = x_lat + m_lat[5] * conv_ffn(modulate(x_lat, g_norm2[l], m_lat[3], m_lat[4]), w_up[l], w_conv[l], b_conv[l], w_down[l])
        if l < DEPTH - 1:
            co_a = blocked_attention(cqa, cka, cva, scale_a)
            co_b = blocked_attention(cqb, ckb, cvb, scale_h)
            co_c = blocked_attention(cqc, ckc, cvc, scale_h, sink[l])
            x_ctx = x_ctx + m_ctx[2] * merge_branches((co_a, co_b, co_c), cgates, w_branch[l], w_out[l])
            x_ctx = x_ctx + m_ctx[5] * conv_ffn(modulate(x_ctx, g_norm2[l], m_ctx[3], m_ctx[4]), w_up[l], w_conv[l], b_conv[l], w_down[l])
    return rms_norm(x_lat, g_final)
```

```python
import contextlib
import math
import numpy as np
import ml_dtypes
import concourse.bass as bass
import concourse.mybir as mybir
from concourse.bass_utils import run_bass_kernel_spmd
from concourse.alu_op_type import AluOpType as ALU

F32 = mybir.dt.float32
BF16 = mybir.dt.bfloat16
AF = mybir.ActivationFunctionType
AX = mybir.AxisListType

D = 1024
CTX = 256
L = 2
EPS = 1e-6
DFF = 2816
NA = 1952
SC_A = 1.0 / math.sqrt(96.0)
SC_H = 1.0 / 8.0


class Res:
    __slots__ = ("name", "w", "r", "dsem")

    def __init__(self, name):
        self.name = name
        self.w = None
        self.r = {}
        self.dsem = None


class SemObj:
    __slots__ = ("h", "cnt")

    def __init__(self, h):
        self.h = h
        self.cnt = 0


class TT:
    __slots__ = ("t", "r")

    def __init__(self, t, r):
        self.t = t
        self.r = r


class Sched:
    def __init__(self, nc, es, nsem=90):
        self.nc = nc
        self.E = {"pe": nc.tensor, "act": nc.scalar, "dve": nc.vector, "pool": nc.gpsimd, "sp": nc.sync}
        self.pool = [SemObj(es.enter_context(nc.semaphore("sm%d" % i))) for i in range(nsem)]
        self.free = list(self.pool)
        self.esem = {k: self.free.pop() for k in self.E}
        self.waited = {k: {} for k in self.E}
        self.res = []
        self.dma_sems = []
        self.held = []
        self.dma_free = {"sp": [], "pool": [], "act": []}

    def newres(self, name):
        r = Res(name)
        self.res.append(r)
        return r

    def _wait(self, eng, stamps):
        need = {}
        for st in stamps:
            if st is None:
                continue
            so, c, ek = st
            k = id(so)
            if k not in need or need[k][1] < c:
                need[k] = st
        w = self.waited[eng]
        for k, (so, c, ek) in need.items():
            if w.get(k, 0) < c:
                self.E[eng].wait_ge(so.h, c)
                w[k] = c

    def _hazards(self, eng, reads, writes):
        st = []
        for R in reads:
            if R.w is not None:
                if not (R.w[2] == eng and eng == "pe"):
                    st.append(R.w)
        for R in writes:
            if R.w is not None and R.w[2] != eng:
                st.append(R.w)
            for s in R.r.values():
                if s[2] != eng:
                    st.append(s)
        return st

    def op(self, eng, fn, reads=(), writes=()):
        self._wait(eng, self._hazards(eng, reads, writes))
        inst = fn(self.E[eng])
        so = self.esem[eng]
        so.cnt += 1
        inst.then_inc(so.h, 1)
        stamp = (so, so.cnt, eng)
        for R in reads:
            R.r[id(so)] = stamp
        for R in writes:
            R.w = stamp
            R.r = {}
        return inst

    def dma(self, eng, pairs, sb, load, extra_reads=(), kw=None):
        if load:
            hz = self._hazards("dma", list(extra_reads), [sb])
        else:
            hz = self._hazards("dma", [sb] + list(extra_reads), [])
        self._wait(eng, hz)
        if sb.dsem is None:
            sb.dsem = {}
        if eng not in sb.dsem:
            if self.dma_free[eng]:
                sb.dsem[eng] = self.dma_free[eng].pop()
            else:
                sb.dsem[eng] = self.free.pop()
                self.dma_sems.append(sb.dsem[eng])
            self.held.append((eng, sb.dsem[eng]))
        so = sb.dsem[eng]
        for (o, i) in pairs:
            self.E[eng].dma_start(out=o, in_=i, **(kw or {})).then_inc(so.h, 16)
            so.cnt += 16
        stamp = (so, so.cnt, "dma")
        if load:
            sb.w = stamp
            sb.r = {}
        else:
            sb.r[id(so)] = stamp

    def barrier(self):
        stamps = []
        for k, so in self.esem.items():
            if so.cnt > 0:
                stamps.append((so, so.cnt, k))
        for so in self.dma_sems:
            if so.cnt > 0:
                stamps.append((so, so.cnt, "dma"))
        for eng in self.E:
            self._wait(eng, [s for s in stamps if s[2] != eng or eng in ("act", "dve", "pool")])
        for R in self.res:
            R.w = None
            R.r = {}
            R.dsem = None
        for (eng_, so_) in self.held:
            self.dma_free[eng_].append(so_)
        self.held = []
        self.res = []
        for k in ("pe", "act", "dve"):
            if self.esem[k].cnt > 14000:
                self.esem[k] = self.free.pop()


def build_nc(SEQ, dbg=False):
    T = CTX + SEQ
    NT = T // 128
    NLT = SEQ // 128
    nc = bass.Bass("TRN2", target_bir_lowering=False)

    def din(name, shape, dt=F32):
        return nc.dram_tensor(name, list(shape), dt, kind="ExternalInput").ap()

    def dscr(name, shape, dt=BF16):
        return nc.dram_tensor(name, list(shape), dt, kind=("ExternalOutput" if dbg else "Internal")).ap()

    xin = din("xin", [T, D])
    cvec = din("cvec", [2, D])
    w_mod = din("w_mod", [L, D, 6 * D])
    b_mod = din("b_mod", [L, 6 * D])
    g_norm1 = din("g_norm1", [L, D])
    g_norm2 = din("g_norm2", [L, D])
    w_inA = din("w_inA", [L, D, NA])
    w_inG = din("w_inG", [L, D, 3 * D])
    b_gate = din("b_gate", [L, 3 * D])
    g_q_a = din("g_q_a", [L, 256])
    w_q_b = din("w_q_b", [L, 256, 768])
    g_kv_a = din("g_kv_a", [L, 128])
    w_kv_b = din("w_kv_b", [L, 128, 1024])
    g_qk = din("g_qk", [L, 4, 64])
    sink = din("sink", [L, 8])
    w_branch = din("w_branch", [L, 3, 512, D])
    w_out = din("w_out", [L, D, D])
    w_up = din("w_up", [L, D, 2 * DFF])
    w_conv = din("w_conv", [L, 3, 2 * DFF])
    b_conv = din("b_conv", [L, 2 * DFF])
    w_down = din("w_down", [L, DFF, D])
    g_final = din("g_final", [1, D])
    ident_d = din("ident", [128, 128], BF16)
    masks_d = din("masks", [2, 128, 128], BF16)
    ropeH = din("ropeH", [T, 2, 64])
    ropeA = din("ropeA", [T, 2, 32])
    out_d = nc.dram_tensor("out", [SEQ, D], F32, kind="ExternalOutput").ap()

    mscr = dscr("mscr", [L, 2, 6 * D], F32)
    xres = dscr("xres", [T, D], F32)
    x1s = dscr("x1s", [T, D], F32)
    QTA = dscr("QTA", [8, 96, T])
    KTA = dscr("KTA", [8, 96, T])
    VA = dscr("VA", [T, 512])
    QTB = dscr("QTB", [4, 128, T])
    KTB = dscr("KTB", [128, T])
    VB = dscr("VB", [T, 128])
    QTC = dscr("QTC", [4, 128, T])
    KTC = dscr("KTC", [128, T])
    VC = dscr("VC", [T, 128])
    OA = dscr("OA", [T, 512])
    OB = dscr("OB", [T, 512])
    OC = dscr("OC", [T, 512])
    H2T = dscr("H2T", [8, 128, T])
    wAb = dscr("wAb", [L, D, NA])
    wqbb = dscr("wqbb", [L, 256, 768])
    wkvbb = dscr("wkvbb", [L, 128, 1024])
    wGb = dscr("wGb", [L, D, 3 * D])
    wbrb = dscr("wbrb", [L, 3, 512, D])
    wob = dscr("wob", [L, D, D])
    wub = dscr("wub", [L, D, 2 * DFF])
    wdb = dscr("wdb", [L, DFF, D])

    with contextlib.ExitStack() as es0:
        S = Sched(nc, es0)

        uid = [0]

        def mk(es, name, shape, dt):
            uid[0] += 1
            t = es.enter_context(nc.sbuf_tensor("sb_%s_%d" % (name, uid[0]), list(shape), dt))
            return TT(t, S.newres(name))

        def mkp(es, name, shape, dt):
            uid[0] += 1
            t = es.enter_context(nc.psum_tensor("ps_%s_%d" % (name, uid[0]), list(shape), dt))
            return TT(t, S.newres(name))

        def precast(pairs, name, after=()):
            S.dma("pool", pairs, S.newres(name), True, extra_reads=after)

        def rows(ap2, n):
            return [ap2[j * 128:(j + 1) * 128] for j in range(n)]

        def precast_A(l_, after=()):
            prs = []
            for (c0, cw) in ((0, 976), (976, 976)):
                prs += [(o[:, c0:c0 + cw], i[:, c0:c0 + cw]) for o, i in zip(rows(wAb[l_], 8), rows(w_inA[l_], 8))]
            prs += list(zip(rows(wqbb[l_], 2), rows(w_q_b[l_], 2)))
            prs += [(wkvbb[l_], w_kv_b[l_])]
            precast(prs, "pcA%d" % l_, after)

        def precast_C(l_, after=()):
            prs = []
            for c in range(2):
                prs += [(o[:, c * 1536:(c + 1) * 1536], i[:, c * 1536:(c + 1) * 1536]) for o, i in zip(rows(wGb[l_], 8), rows(w_inG[l_], 8))]
            for i_ in range(3):
                prs += list(zip(rows(wbrb[l_, i_], 4), rows(w_branch[l_, i_], 4)))
            prs += list(zip(rows(wob[l_], 8), rows(w_out[l_], 8)))
            precast(prs, "pcC1%d" % l_, after)
            prs = []
            for c in range(4):
                prs += [(o[:, c * 1408:(c + 1) * 1408], i[:, c * 1408:(c + 1) * 1408]) for o, i in zip(rows(wub[l_], 8), rows(w_up[l_], 8))]
            prs += list(zip(rows(wdb[l_], 22), rows(w_down[l_], 22)))
            precast(prs, "pcC2%d" % l_, after)

        precast_A(0)

        ident = mk(es0, "ident", [128, 128], BF16)
        masks = mk(es0, "masks", [128, 2, 128], BF16)
        MF = mk(es0, "MF", [128, L, 2, 48], F32)
        G1 = mk(es0, "G1", [128, L, 2, 8], F32)
        G2 = mk(es0, "G2", [128, L, 2, 8], F32)
        gn = mk(es0, "gn", [128, 2, L, 8], F32)
        bgT = mk(es0, "bgT", [128, L, 24], F32)
        cvw = mk(es0, "cvw", [128, L, 4, 44], F32)
        gqa = mk(es0, "gqa", [128, L, 2], F32)
        gkva = mk(es0, "gkva", [128, L, 1], F32)
        gqkb = mk(es0, "gqkb", [128, L, 4, 64], F32)
        esink = mk(es0, "esink", [128, L, 8], F32)
        gfb = mk(es0, "gfb", [128, D], F32)
        invn = mk(es0, "invn", [128, 12], F32)
        epsb = mk(es0, "epsb", [128, 1], F32)

        S.dma("sp", [(ident.t[:], ident_d[:, :])], ident.r, True)
        S.dma("sp", [(masks.t[:], masks_d.rearrange("m p f -> p m f"))], masks.r, True)
        es0.enter_context(nc.allow_non_contiguous_dma(reason="strided parameter / staging DMAs"))
        fm = lambda v: v.rearrange("(j p) -> p j", p=128)
        S.dma("sp", [(gn.t[:, 0, l_, :], fm(g_norm1[l_])) for l_ in range(L)] + [(gn.t[:, 1, l_, :], fm(g_norm2[l_])) for l_ in range(L)], gn.r, True)
        S.dma("sp", [(bgT.t[:, l_, :], fm(b_gate[l_])) for l_ in range(L)], bgT.r, True)
        S.dma("sp", [(cvw.t[:, l_, c_, :], fm(w_conv[l_, c_])) for l_ in range(L) for c_ in range(3)]
              + [(cvw.t[:, l_, 3, :], fm(b_conv[l_])) for l_ in range(L)], cvw.r, True)
        S.dma("sp", [(gqa.t[:, l_, :], fm(g_q_a[l_])) for l_ in range(L)], gqa.r, True)
        S.dma("sp", [(gkva.t[:, l_, :], fm(g_kv_a[l_])) for l_ in range(L)], gkva.r, True)
        S.dma("sp", [(gqkb.t[:].rearrange("p l a d -> p (l a d)"),
                      g_qk.rearrange("l a d -> (l a d)").unsqueeze(0).partition_broadcast(128))], gqkb.r, True)
        S.dma("sp", [(esink.t[:].rearrange("p l h -> p (l h)"),
                      sink.rearrange("l h -> (l h)").unsqueeze(0).partition_broadcast(128))], esink.r, True)
        S.dma("sp", [(gfb.t[:], g_final.partition_broadcast(128))], gfb.r, True)
        S.op("act", lambda e: e.activation(out=esink.t[:], in_=esink.t[:], func=AF.Exp), [esink.r], [esink.r])
        S.op("dve", lambda e: e.memset(invn.t[:, 0:1], 1.0 / 256), [], [invn.r])
        S.op("dve", lambda e: e.memset(invn.t[:, 1:2], 1.0 / 128), [], [invn.r])
        S.op("dve", lambda e: e.memset(invn.t[:, 2:12], 1.0 / 64), [], [invn.r])
        S.op("dve", lambda e: e.memset(epsb.t[:], EPS), [], [epsb.r])

        with contextlib.ExitStack() as es:
            cT = mk(es, "cT", [128, 8, 2], F32)
            LT = mk(es, "LT", [128, 8, 33], F32)
            wm = [mk(es, "wm%d" % i, [128, 8, 512], F32) for i in range(2)]
            bm = mk(es, "bm", [33, L, 6 * D], F32)
            mrow = mk(es, "mrow", [33, L, 6 * D], F32)
            pm = [mkp(es, "pm%d" % i, [128, 512], F32) for i in range(2)]
            S.dma("sp", [(cT.t[:, :, r_], fm(cvec[r_])) for r_ in range(2)], cT.r, True)
            S.dma("sp", [(bm.t[p_:p_ + 1, l_, :], b_mod[l_].unsqueeze(0)) for p_ in (0, 32) for l_ in range(L)], bm.r, True)
            S.op("dve", lambda e: e.memset(LT.t[:], 0.0), [], [LT.r])
            S.op("act", lambda e: e.activation(out=LT.t[:, :, 0:1], in_=cT.t[:, :, 0:1], func=AF.Silu), [cT.r, LT.r], [LT.r])
            S.op("act", lambda e: e.activation(out=LT.t[:, :, 32:33], in_=cT.t[:, :, 1:2], func=AF.Silu), [cT.r, LT.r], [LT.r])
            i = 0
            for l in range(L):
                for cc in range(12):
                    wb_ = wm[i % 2]
                    pb_ = pm[i % 2]
                    S.dma("sp", [(wb_.t[:], w_mod[l].rearrange("(j p) c -> p j c", p=128)[:, :, cc * 512:(cc + 1) * 512])], wb_.r, True)

                    def mm(e, wb_=wb_, pb_=pb_):
                        for j in range(8):
                            ins = e.matmul(pb_.t[0:33, :], lhsT=LT.t[:, j, :], rhs=wb_.t[:, j, :], start=(j == 0), stop=(j == 7))
                        return ins
                    S.op("pe", mm, [LT.r, wb_.r], [pb_.r])
                    for r0 in (0, 32):
                        S.op("dve", lambda e, r0=r0, pb_=pb_, l=l, cc=cc: e.tensor_tensor(
                            out=mrow.t[r0:r0 + 1, l, cc * 512:(cc + 1) * 512], in0=pb_.t[r0:r0 + 1, :],
                            in1=bm.t[r0:r0 + 1, l, cc * 512:(cc + 1) * 512], op=ALU.add), [pb_.r, bm.r], [mrow.r])
                    i += 1
            S.dma("sp", [(mscr[l_, r_, :].unsqueeze(0), mrow.t[32 * r_:32 * r_ + 1, l_, :]) for l_ in range(L) for r_ in range(2)], mrow.r, False)
            S.barrier()
        S.dma("sp", [(MF.t[:, l, r, :], mscr[l, r].rearrange("(j p) -> p j", p=128)) for l in range(L) for r in range(2)], MF.r, True)
        for l in range(L):
            for r in range(2):
                S.op("dve", lambda e, l=l, r=r: e.scalar_tensor_tensor(
                    out=G1.t[:, l, r, :], in0=MF.t[:, l, r, 8:16], scalar=1.0, in1=gn.t[:, 0, l, :], op0=ALU.add, op1=ALU.mult),
                    [MF.r, gn.r], [G1.r])
                S.op("dve", lambda e, l=l, r=r: e.scalar_tensor_tensor(
                    out=G2.t[:, l, r, :], in0=MF.t[:, l, r, 32:40], scalar=1.0, in1=gn.t[:, 1, l, :], op0=ALU.add, op1=ALU.mult),
                    [MF.r, gn.r], [G2.r])

        def norm_mod_gen(xt_ap, xr, tmp, hT, col0, Gt, shift_ap_fn, gsel, pT, pad=0):
            junk, ss, sd, rs, xs = tmp
            S.op("act", lambda e: e.activation(out=junk.t[:], in_=xt_ap, func=AF.Square, accum_out=ss.t[:]), [xr], [junk.r, ss.r])
            yield
            for _p in range(pad):
                yield
            S.op("dve", lambda e: e.tensor_scalar(out=ss.t[:], in0=ss.t[:], scalar1=1.0 / D, scalar2=EPS, op0=ALU.mult, op1=ALU.add),
                 [ss.r], [ss.r])
            yield
            for _p in range(pad):
                yield
            S.op("act", lambda e: e.activation(out=sd.t[:], in_=ss.t[:], func=AF.Sqrt), [ss.r], [sd.r])
            yield
            for _p in range(pad):
                yield
            S.op("dve", lambda e: e.reciprocal(out=rs.t[:], in_=sd.t[:]), [sd.r], [rs.r])
            yield
            for _p in range(pad):
                yield
            S.op("act", lambda e: e.activation(out=xs.t[:], in_=xt_ap, func=AF.Copy, scale=rs.t[:, 0:1]), [xr, rs.r], [xs.r])
            yield
            for _p in range(pad):
                yield

            def tr(e):
                for j in range(8):
                    ins = e.transpose(pT.t[:, j, :], xs.t[:, j * 128:(j + 1) * 128], ident.t[:])
                return ins
            S.op("pe", tr, [xs.r, ident.r], [pT.r])
            yield
            for j in range(8):
                S.op("dve", lambda e, j=j: e.tensor_scalar(out=hT.t[:, j, col0:col0 + 128], in0=pT.t[:, j, :],
                                                          scalar1=Gt(j), scalar2=shift_ap_fn(j), op0=ALU.mult, op1=ALU.add),
                     [pT.r, gsel, MF.r], [hT.r])
                yield

        def norm_mod_T(*a_, **k_):
            for _ in norm_mod_gen(*a_, **k_):
                pass

        for l in range(L):
            last = (l == L - 1)
            xsrc = xin if l == 0 else xres
            with contextlib.ExitStack() as es:
                wA = mk(es, "wA", [128, 8, NA], BF16)
                wqb = mk(es, "wqb", [128, 2, 768], BF16)
                wkvb = mk(es, "wkvb", [128, 1024], BF16)
                ACH = ((0, 416), (416, 512), (928, 512), (1440, 512))
                wA_r = [S.newres("wA%d" % c) for c in range(4)]

                def load_weights_A():
                    for c, (c0, cw) in enumerate(ACH):
                        S.dma("sp", [(wA.t[:, j, c0:c0 + cw], wAb[l, j * 128:(j + 1) * 128, c0:c0 + cw]) for j in range(8)], wA_r[c], True)
                    S.dma("sp", [(wqb.t[:, j, :], wqbb[l, j * 128:(j + 1) * 128, :]) for j in range(2)], wqb.r, True)
                    S.dma("sp", [(wkvb.t[:], wkvbb[l])], wkvb.r, True)
                xt = [mk(es, "xt%d" % i, [128, D], F32) for i in range(2)]
                rp = [mk(es, "rp%d" % i, [128, 2, 96], F32) for i in range(3)]
                Pc = [mk(es, "Pc%d" % i, [128, NA], F32) for i in range(2)]
                junk = mk(es, "junk", [128, D], F32)
                ss = mk(es, "ss", [128, 1], F32)
                sd = mk(es, "sd", [128, 1], F32)
                rs = mk(es, "rs", [128, 1], F32)
                xs = mk(es, "xs", [128, D], BF16)
                hT = [mk(es, "hT%d" % i, [128, 8, 128], BF16) for i in range(2)]
                ss12b = [mk(es, "ss12%d" % i, [128, 12], F32) for i in range(2)]
                sd12b = [mk(es, "sd12%d" % i, [128, 12], F32) for i in range(2)]
                rs12b = [mk(es, "rs12%d" % i, [128, 12], F32) for i in range(2)]
                sqtb = [mk(es, "sqt%d" % i, [128, 640], F32) for i in range(2)]
                tbb = [mk(es, "tb%d" % i, [128, 4, 64], F32) for i in range(2)]
                u = mk(es, "u", [128, 768], F32)
                t1a = mk(es, "t1a", [128, 256], F32)
                t2a = mk(es, "t2a", [128, 256], F32)
                t2k = mk(es, "t2k", [128, 32], F32)
                mt = [(mk(es, "ubm%d" % i, [128, 512], F32), mk(es, "t1m%d" % i, [128, 512], F32), mk(es, "t2m%d" % i, [128, 512], F32)) for i in range(2)]
                qn = mk(es, "qn", [128, 384], BF16)
                qnT = mk(es, "qnT", [128, 3, 128], BF16)
                kr = mk(es, "kr", [128, 32], F32)
                QAt = mk(es, "QAt", [128, 8, 96], BF16)
                KAt = mk(es, "KAt", [128, 8, 96], BF16)
                QBt = mk(es, "QBt", [128, 512], BF16)
                KBt = mk(es, "KBt", [128, 128], BF16)
                QCt = mk(es, "QCt", [128, 512], BF16)
                KCt = mk(es, "KCt", [128, 128], BF16)
                stg = []
                for i in range(2):
                    stg.append(dict(
                        QTA=mk(es, "sQTA%d" % i, [128, 8, 512], BF16), KTA=mk(es, "sKTA%d" % i, [128, 8, 512], BF16),
                        VA=mk(es, "sVA%d" % i, [128, 4, 512], BF16),
                        QTB=mk(es, "sQTB%d" % i, [128, 4, 512], BF16), KTB=mk(es, "sKTB%d" % i, [128, 512], BF16),
                        VB=mk(es, "sVB%d" % i, [128, 4, 128], BF16),
                        QTC=mk(es, "sQTC%d" % i, [128, 4, 512], BF16), KTC=mk(es, "sKTC%d" % i, [128, 512], BF16),
                        VC=mk(es, "sVC%d" % i, [128, 4, 128], BF16)))
                pTA = mkp(es, "pTA", [128, 8, 128], BF16)
                pTs = [mkp(es, "pTs%d" % i, [128, 8, 128], BF16) for i in range(3)]
                P = [mkp(es, "P%d" % i, [128, 512], F32) for i in range(2)]
                Q = [mkp(es, "Q%d" % i, [128, 512], F32) for i in range(2)]
                tmp = (junk, ss, sd, rs, xs)
                ptc = [0]
                pcc = [0]

                def nextpT():
                    ptc[0] += 1
                    return pTs[ptc[0] % 3]

                blocks = [(0, 2)] + [(2 + 4 * b, 4) for b in range(NLT // 4)]
                tiles = []
                for bi, (tb0, ntile) in enumerate(blocks):
                    for s in range(ntile):
                        tiles.append((tb0 + s, bi, s, s == ntile - 1, tb0, ntile))

                def ld_tile(t, k):
                    S.dma("sp", [(xt[k % 2].t[:], xsrc[t * 128:(t + 1) * 128, :])], xt[k % 2].r, True)
                    S.dma("sp", [(rp[k % 3].t[:, :, 0:64], ropeH[t * 128:(t + 1) * 128]),
                                 (rp[k % 3].t[:, :, 64:96], ropeA[t * 128:(t + 1) * 128])], rp[k % 3].r, True)

                def stage1(k):
                    t = tiles[k][0]
                    r = 1 if t < 2 else 0
                    xb = xt[k % 2]
                    hTb = hT[k % 2]
                    pc = Pc[k % 2]
                    if k + 1 < len(tiles):
                        ld_tile(tiles[k + 1][0], k + 1)
                    yield from norm_mod_gen(xb.t[:], xb.r, tmp, hTb, 0, lambda j: G1.t[:, l, r, j:j + 1],
                                            lambda j: MF.t[:, l, r, j:j + 1], G1.r, pTA)
                    for ci_, (c0, cw) in enumerate(ACH):
                        pb_ = P[pcc[0] % 2]
                        pcc[0] += 1

                        def mm(e, c0=c0, cw=cw, pb_=pb_):
                            for j in range(8):
                                ins = e.matmul(pb_.t[:, 0:cw], lhsT=hTb.t[:, j, :], rhs=wA.t[:, j, c0:c0 + cw],
                                               start=(j == 0), stop=(j == 7))
                            return ins
                        S.op("pe", mm, [hTb.r, wA_r[ci_]], [pb_.r])
                        yield
                        S.op("act", lambda e, c0=c0, cw=cw, pb_=pb_: e.activation(out=pc.t[:, c0:c0 + cw], in_=pb_.t[:, 0:cw], func=AF.Copy),
                             [pb_.r], [pc.r])
                        yield

                def prefix(k):
                    pc = Pc[k % 2]
                    rpb = rp[k % 3]
                    X = pc.t
                    ss12, sd12, rs12, sqt, tb = ss12b[k % 2], sd12b[k % 2], rs12b[k % 2], sqtb[k % 2], tbb[k % 2]
                    for a in range(4):
                        S.op("pool", lambda e, a=a: e.tensor_tensor(out=tb.t[:, a, :], in0=rpb.t[:, a % 2, 0:64],
                                                                    in1=gqkb.t[:, l, a, :], op=ALU.mult), [rpb.r, gqkb.r], [tb.r])
                        yield
                    S.op("act", lambda e: e.activation(out=sqt.t[:, 0:256], in_=X[:, 0:256], func=AF.Square, accum_out=ss12.t[:, 0:1]),
                         [pc.r], [sqt.r, ss12.r])
                    yield
                    S.op("act", lambda e: e.activation(out=sqt.t[:, 0:128], in_=X[:, 256:384], func=AF.Square, accum_out=ss12.t[:, 1:2]),
                         [pc.r], [sqt.r, ss12.r])
                    yield
                    S.op("act", lambda e: e.activation(out=sqt.t[:, 0:640], in_=X[:, 416:1056], func=AF.Square), [pc.r], [sqt.r])
                    yield
                    S.op("dve", lambda e: e.tensor_reduce(out=ss12.t[:, 2:12], in_=sqt.t[:, 0:640].rearrange("p (h d) -> p h d", d=64),
                                                         axis=AX.X, op=ALU.add), [sqt.r], [ss12.r])
                    yield
                    S.op("dve", lambda e: e.tensor_tensor(out=ss12.t[:], in0=ss12.t[:], in1=invn.t[:], op=ALU.mult), [ss12.r, invn.r], [ss12.r])
                    yield
                    S.op("act", lambda e: e.activation(out=sd12.t[:], in_=ss12.t[:], func=AF.Sqrt, bias=epsb.t[:, 0:1]), [ss12.r, epsb.r], [sd12.r])
                    yield
                    S.op("dve", lambda e: e.reciprocal(out=rs12.t[:], in_=sd12.t[:]), [sd12.r], [rs12.r])
                    yield

                def stage2(k, extra=()):
                    t, bi, s, _, _, _ = tiles[k]
                    sg = stg[bi % 2]
                    pc = Pc[k % 2]
                    rpb = rp[k % 3]
                    X = pc.t
                    rs12, tb = rs12b[k % 2], tbb[k % 2]

                    def chainA():
                        S.op("dve", lambda e: e.tensor_scalar(out=qn.t[:, 0:256], in0=X[:, 0:256], scalar1=rs12.t[:, 0:1], scalar2=None,
                                                             op0=ALU.mult), [pc.r, rs12.r], [qn.r])
                        yield
                        S.op("dve", lambda e: e.tensor_scalar(out=qn.t[:, 256:384], in0=X[:, 256:384], scalar1=rs12.t[:, 1:2], scalar2=None,
                                                             op0=ALU.mult), [pc.r, rs12.r], [qn.r])
                        yield
                        pq = nextpT()

                        def trq(e, pq=pq):
                            for j in range(3):
                                ins = e.transpose(pq.t[:, j, :], qn.t[:, j * 128:(j + 1) * 128], ident.t[:])
                            return ins
                        S.op("pe", trq, [qn.r, ident.r], [pq.r])
                        yield
                        for j in range(3):
                            gsc = gqa.t[:, l, j:j + 1] if j < 2 else gkva.t[:, l, 0:1]
                            S.op("dve", lambda e, j=j, gsc=gsc, pq=pq: e.tensor_scalar(out=qnT.t[:, j, :], in0=pq.t[:, j, :], scalar1=gsc,
                                                                                      scalar2=None, op0=ALU.mult), [pq.r, gqa.r, gkva.r], [qnT.r])
                            yield
                        for (c0, cw, qb_) in ((0, 512, Q[0]), (512, 256, Q[1])):
                            def mmq(e, c0=c0, cw=cw, qb_=qb_):
                                for j in range(2):
                                    ins = e.matmul(qb_.t[:, 0:cw], lhsT=qnT.t[:, j, :], rhs=wqb.t[:, j, c0:c0 + cw], start=(j == 0), stop=(j == 1))
                                return ins
                            S.op("pe", mmq, [qnT.r, wqb.r], [qb_.r])
                            yield
                        sa4 = rpb.t[:, 1, 64:96].rearrange("p (a q f) -> p a q f", a=2, q=2)
                        S.op("act", lambda e: e.activation(out=u.t[:, 0:512], in_=Q[0].t[:, :], func=AF.Copy), [Q[0].r], [u.r])
                        yield
                        S.op("act", lambda e: e.activation(out=u.t[:, 512:768], in_=Q[1].t[:, 0:256], func=AF.Copy), [Q[1].r], [u.r])
                        yield
                        for hh, qb_ in ((0, Q[0]), (1, Q[1])):
                            S.op("pe", lambda e, hh=hh, qb_=qb_: e.matmul(qb_.t[:, :], lhsT=qnT.t[:, 2, :], rhs=wkvb.t[:, hh * 512:(hh + 1) * 512],
                                                                          start=True, stop=True), [qnT.r, wkvb.r], [qb_.r])
                            yield
                        u3 = u.t[:].rearrange("p (h d) -> p h d", d=96)
                        S.op("act", lambda e: e.activation(out=QAt.t[:, :, 0:64], in_=u3[:, :, 0:64], func=AF.Copy), [u.r], [QAt.r])
                        yield
                        ca_b = rpb.t[:, 0, 64:96].unsqueeze(1).broadcast_to([128, 8, 32])
                        t13 = t1a.t[:, 0:256].rearrange("p (h d) -> p h d", d=32)
                        S.op("dve", lambda e: e.tensor_tensor(out=t13, in0=u3[:, :, 64:96], in1=ca_b, op=ALU.mult), [u.r, rpb.r], [t1a.r])
                        yield
                        u5 = u3[:, :, 64:96].rearrange("p h (a q f) -> p h a q f", a=2, q=2)
                        t25 = t2a.t[:, 0:256].rearrange("p (h a q f) -> p h a q f", a=2, q=2, f=8)
                        for a in range(2):
                            for q_ in range(2):
                                S.op("pool", lambda e, a=a, q_=q_: e.tensor_tensor(
                                    out=t25[:, :, a, q_, :], in0=u5[:, :, a, 1 - q_, :],
                                    in1=sa4[:, a, q_, :].unsqueeze(1).broadcast_to([128, 8, 8]),
                                    op=ALU.mult), [u.r, rpb.r], [t2a.r])
                                yield
                        for hh, qb_ in ((0, Q[0]), (1, Q[1])):
                            kv3 = qb_.t[:, :].rearrange("p (h d) -> p h d", d=128)
                            S.op("act", lambda e, hh=hh, kv3=kv3: e.activation(out=KAt.t[:, hh * 4:(hh + 1) * 4, 0:64], in_=kv3[:, :, 0:64], func=AF.Copy),
                                 [qb_.r], [KAt.r])
                            yield
                            S.op("dve", lambda e, hh=hh, kv3=kv3: e.tensor_copy(
                                out=sg["VA"].t[:, s, hh * 256:(hh + 1) * 256].rearrange("p (h d) -> p h d", d=64), in_=kv3[:, :, 64:128]),
                                [qb_.r], [sg["VA"].r])
                            yield
                        S.op("dve", lambda e: e.tensor_tensor(out=QAt.t[:, :, 64:96], in0=t13, in1=t2a.t[:, 0:256].rearrange("p (h d) -> p h d", d=32),
                                                             op=ALU.add), [t1a.r, t2a.r], [QAt.r])
                        yield
                        for (src, dst) in ((KAt, sg["KTA"]), (QAt, sg["QTA"])):
                            pt_ = nextpT()

                            def trh(e, src=src, pt_=pt_):
                                for h in range(8):
                                    ins = e.transpose(pt_.t[0:96, h, :], src.t[:, h, :], ident.t[:])
                                return ins
                            S.op("pe", trh, [src.r, ident.r], [pt_.r])
                            yield
                            S.op("act", lambda e, dst=dst, pt_=pt_: e.activation(out=dst.t[0:96, :, s * 128:(s + 1) * 128], in_=pt_.t[0:96, :, :],
                                                                                 func=AF.Copy), [pt_.r], [dst.r])
                            yield

                    def chainKr():
                        S.op("dve", lambda e: e.tensor_tensor(out=kr.t[:], in0=X[:, 384:416], in1=rpb.t[:, 0, 64:96], op=ALU.mult),
                             [pc.r, rpb.r], [kr.r])
                        yield
                        akr4 = X[:, 384:416].rearrange("p (a q f) -> p a q f", a=2, q=2)
                        sa4 = rpb.t[:, 1, 64:96].rearrange("p (a q f) -> p a q f", a=2, q=2)
                        t24 = t2k.t[:, 0:32].rearrange("p (a q f) -> p a q f", a=2, q=2)
                        for q_ in range(2):
                            S.op("pool", lambda e, q_=q_: e.tensor_tensor(out=t24[:, :, q_, :], in0=akr4[:, :, 1 - q_, :], in1=sa4[:, :, q_, :],
                                                                         op=ALU.mult), [pc.r, rpb.r], [t2k.r])
                            yield
                        S.op("dve", lambda e: e.tensor_tensor(out=kr.t[:], in0=kr.t[:], in1=t2k.t[:, 0:32], op=ALU.add), [kr.r, t2k.r], [kr.r])
                        yield
                        S.op("dve", lambda e: e.tensor_copy(out=KAt.t[:, :, 64:96], in_=kr.t[:].unsqueeze(1).broadcast_to([128, 8, 32])),
                             [kr.r], [KAt.r])
                        yield

                    def chainM(mix):
                        qo, ko, vo = (416, 928, 1056) if mix == 0 else (1184, 1696, 1824)
                        Qt_, Kt_ = (QBt, KBt) if mix == 0 else (QCt, KCt)
                        ub_, t1_, t2_ = mt[mix]
                        for (nh, so_, dstT, ci, si, rcol) in ((8, qo, Qt_, 0, 1, 2), (2, ko, Kt_, 2, 3, 10)):
                            w_ = nh * 64
                            src_ap = X[:, so_:so_ + w_]
                            if mix == 0:
                                S.op("dve", lambda e, src_ap=src_ap, nh=nh, rcol=rcol, w_=w_: e.tensor_tensor(
                                    out=ub_.t[:, 0:w_].rearrange("p (h d) -> p h d", d=64), in0=src_ap.rearrange("p (h d) -> p h d", d=64),
                                    in1=rs12.t[:, rcol:rcol + nh].unsqueeze(2).broadcast_to([128, nh, 64]), op=ALU.mult),
                                    [pc.r, rs12.r], [ub_.r])
                                yield
                                uflat = ub_.t[:, 0:w_]
                                ur = ub_.r
                                Ct = tb.t[:, ci, :]
                                St = tb.t[:, si, :]
                                tr_ = tb.r
                            else:
                                uflat = src_ap
                                ur = pc.r
                                Ct = rpb.t[:, 0, 0:64]
                                St = rpb.t[:, 1, 0:64]
                                tr_ = rpb.r
                            uu = uflat.rearrange("p (h d) -> p h d", d=64)
                            S.op("dve", lambda e, uu=uu, Ct=Ct, nh=nh, w_=w_: e.tensor_tensor(
                                out=t1_.t[:, 0:w_].rearrange("p (h d) -> p h d", d=64), in0=uu,
                                in1=Ct.unsqueeze(1).broadcast_to([128, nh, 64]), op=ALU.mult), [ur, tr_], [t1_.r])
                            yield
                            u5 = uflat.rearrange("p (h a q f) -> p h a q f", a=2, q=2, f=16)
                            t25 = t2_.t[:, 0:w_].rearrange("p (h a q f) -> p h a q f", a=2, q=2, f=16)
                            S4 = St.rearrange("p (a q f) -> p a q f", a=2, q=2)
                            for a in range(2):
                                for q_ in range(2):
                                    S.op("pool", lambda e, a=a, q_=q_, u5=u5, t25=t25, S4=S4, nh=nh: e.tensor_tensor(
                                        out=t25[:, :, a, q_, :], in0=u5[:, :, a, 1 - q_, :],
                                        in1=S4[:, a, q_, :].unsqueeze(1).broadcast_to([128, nh, 16]), op=ALU.mult), [ur, tr_], [t2_.r])
                                    yield
                            S.op("dve", lambda e, dstT=dstT, w_=w_: e.tensor_tensor(out=dstT.t[:, 0:w_], in0=t1_.t[:, 0:w_], in1=t2_.t[:, 0:w_], op=ALU.add),
                                 [t1_.r, t2_.r], [dstT.r])
                            yield
                        vdst = sg["VB"] if mix == 0 else sg["VC"]
                        S.op("pool", lambda e, vdst=vdst, vo=vo: e.tensor_copy(out=vdst.t[:, s, :], in_=X[:, vo:vo + 128]), [pc.r], [vdst.r])
                        yield
                        pt_ = nextpT()
                        qdst = sg["QTB"] if mix == 0 else sg["QTC"]
                        kdst = sg["KTB"] if mix == 0 else sg["KTC"]

                        def trb(e, Qt_=Qt_, Kt_=Kt_, pt_=pt_):
                            for j in range(4):
                                e.transpose(pt_.t[:, j, :], Qt_.t[:, j * 128:(j + 1) * 128], ident.t[:])
                            return e.transpose(pt_.t[:, 4, :], Kt_.t[:, :], ident.t[:])
                        S.op("pe", trb, [Qt_.r, Kt_.r, ident.r], [pt_.r])
                        yield
                        S.op("act", lambda e, qdst=qdst, pt_=pt_: e.activation(out=qdst.t[:, :, s * 128:(s + 1) * 128], in_=pt_.t[:, 0:4, :], func=AF.Copy),
                             [pt_.r], [qdst.r])
                        yield
                        S.op("act", lambda e, kdst=kdst, pt_=pt_: e.activation(out=kdst.t[:, s * 128:(s + 1) * 128], in_=pt_.t[:, 4, :], func=AF.Copy),
                             [pt_.r], [kdst.r])
                        yield

                    ga = chainA()
                    gens = [ga] + list(extra) + [chainKr(), ga, chainM(1), chainM(0)]
                    while gens:
                        for g_ in list(gens):
                            if g_ not in gens:
                                continue
                            try:
                                next(g_)
                            except StopIteration:
                                while g_ in gens:
                                    gens.remove(g_)

                def stores(k):
                    t, bi, s, _, tb0, ntile = tiles[k]
                    sg = stg[bi % 2]
                    t0 = tb0 * 128
                    n = ntile * 128
                    S.dma("sp", [(QTA.rearrange("h d t -> d h t")[:, :, t0:t0 + n], sg["QTA"].t[0:96, :, 0:n])], sg["QTA"].r, False)
                    S.dma("sp", [(KTA.rearrange("h d t -> d h t")[:, :, t0:t0 + n], sg["KTA"].t[0:96, :, 0:n])], sg["KTA"].r, False)
                    S.dma("sp", [(VA[t0:t0 + n, :].rearrange("(s p) c -> p s c", p=128), sg["VA"].t[:, 0:ntile, :])], sg["VA"].r, False)
                    for nm, dq, dk, dv in (("B", QTB, KTB, VB), ("C", QTC, KTC, VC)):
                        S.dma("sp", [(dq.rearrange("j d t -> d j t")[:, :, t0:t0 + n], sg["QT" + nm].t[:, :, 0:n])], sg["QT" + nm].r, False)
                        S.dma("sp", [(dk[:, t0:t0 + n], sg["KT" + nm].t[:, 0:n])], sg["KT" + nm].r, False)
                        S.dma("sp", [(dv[t0:t0 + n, :].rearrange("(s p) c -> p s c", p=128), sg["V" + nm].t[:, 0:ntile, :])], sg["V" + nm].r, False)

                ld_tile(tiles[0][0], 0)
                load_weights_A()
                def s1p(k):
                    yield from stage1(k)
                    yield from prefix(k)
                for _ in s1p(0):
                    pass
                for k in range(len(tiles)):
                    stage2(k, [s1p(k + 1)] if k + 1 < len(tiles) else [])
                    if tiles[k][3]:
                        stores(k)
                S.barrier()

            with contextlib.ExitStack() as es:
                KT = [mk(es, "KT%d" % i, [128, T], BF16) for i in range(2)]
                QT = [mk(es, "QT%d" % i, [128, 4, T], BF16) for i in range(2)]
                V = [mk(es, "V%d" % i, [128, NT, 65], BF16) for i in range(2)]
                Osb = [mk(es, "Os%d" % i, [128, NT, 512], BF16) for i in range(2)]
                Os = Osb[0]
                PT = [mk(es, "PT%d" % i, [128, 1024], BF16) for i in range(4)]
                rl = [mk(es, "rl%d" % i, [128, 4], F32) for i in range(2)]
                Sb = [mkp(es, "Sb%d" % i, [128, 1024], F32) for i in range(3)]
                Ob = [mkp(es, "Ob%d" % i, [128, 512], F32) for i in range(2)]
                for i in range(2):
                    S.op("dve", lambda e, i=i: e.memset(V[i].t[:, :, 64:65], 1.0), [], [V[i].r])
                cnt = {"s": 0, "p": 0, "o": 0}
                osr = [Osb[0].r]

                free_ob = [1, 0]

                def attn_gen(kt_ap_fn, q_ap, nq, v_ap_fn, kts, scale, reads, out_ap, nsub, sink_ap=None, mask_fn=None, LOOK=1):
                    obi = free_ob.pop()
                    ob = Ob[obi]
                    rlb = rl[obi]
                    O3 = ob.t[:, 0:260].rearrange("p (s c) -> p s c", c=65)
                    nk = len(kts)
                    pairs = [kts[i:i + 2] for i in range(0, nk, 2)]
                    npair = len(pairs)

                    def qk(pi):
                        sb_ = Sb[cnt["s"] % 3]
                        cnt["s"] += 1
                        pt_ = PT[cnt["p"] % 4]
                        cnt["p"] += 1
                        pk = pairs[pi]

                        def f(e):
                            for x_, kt in enumerate(pk):
                                o2 = sb_.t[:, x_ * 512:x_ * 512 + nq]
                                so_ = o2 if len(q_ap.shape) == 2 else o2.rearrange("p (j q) -> p j q", q=128)
                                ins = e.matmul(so_, lhsT=kt_ap_fn(kt), rhs=q_ap, start=True, stop=True)
                            return ins
                        S.op("pe", f, reads, [sb_.r])
                        if nq == 512 or len(pk) == 1:
                            w_ = nq if len(pk) == 1 else 1024
                            S.op("act", lambda e: e.activation(out=pt_.t[:, 0:w_], in_=sb_.t[:, 0:w_], func=AF.Exp, scale=scale), [sb_.r], [pt_.r])
                        else:
                            S.op("act", lambda e: e.activation(out=pt_.t[:, :].rearrange("p (x c) -> p x c", x=2)[:, :, 0:nq],
                                                               in_=sb_.t[:, :].rearrange("p (x c) -> p x c", x=2)[:, :, 0:nq], func=AF.Exp, scale=scale),
                                 [sb_.r], [pt_.r])
                        for x_, kt in enumerate(pk):
                            m = mask_fn(kt) if mask_fn is not None else None
                            if m is not None:
                                pv_ = pt_.t[:, x_ * 512:x_ * 512 + nq].rearrange("p (s q) -> p s q", q=128)
                                S.op("dve", lambda e, pv_=pv_, m=m: e.tensor_tensor(out=pv_, in0=pv_, in1=m.unsqueeze(1).broadcast_to([128, nq // 128, 128]),
                                                                                   op=ALU.mult), [pt_.r, masks.r], [pt_.r])
                        return pt_

                    def pv(pi, pt_):
                        pk = pairs[pi]

                        def f(e):
                            for x_, kt in enumerate(pk):
                                for s_ in range(nsub):
                                    first = (pi == 0 and x_ == 0 and s_ == 0)
                                    lastm = (pi == npair - 1 and x_ == len(pk) - 1 and s_ == nsub - 1)
                                    ins = e.matmul(O3[:, s_, :], lhsT=pt_.t[:, x_ * 512 + s_ * 128:x_ * 512 + (s_ + 1) * 128], rhs=v_ap_fn(kt),
                                                   start=first, stop=lastm, skip_group_check=True)
                            return ins
                        S.op("pe", f, [pt_.r] + reads, [ob.r])
                    pts = {}
                    for i in range(min(LOOK, npair)):
                        pts[i] = qk(i)
                        yield
                    for i in range(npair):
                        if i + LOOK < npair:
                            pts[i + LOOK] = qk(i + LOOK)
                            yield
                        pv(i, pts.pop(i))
                        yield
                    if sink_ap is not None:
                        S.op("dve", lambda e: e.tensor_tensor(out=rlb.t[:, 0:nsub], in0=O3[:, 0:nsub, 64], in1=sink_ap, op=ALU.add),
                             [ob.r, esink.r], [rlb.r])
                        S.op("dve", lambda e: e.reciprocal(out=rlb.t[:, 0:nsub], in_=rlb.t[:, 0:nsub]), [rlb.r], [rlb.r])
                    else:
                        S.op("dve", lambda e: e.reciprocal(out=rlb.t[:, 0:nsub], in_=O3[:, 0:nsub, 64]), [ob.r], [rlb.r])
                    S.op("dve", lambda e: e.tensor_tensor(out=out_ap, in0=O3[:, 0:nsub, 0:64],
                                                         in1=rlb.t[:, 0:nsub].unsqueeze(2).broadcast_to([128, nsub, 64]), op=ALU.mult),
                         [ob.r, rlb.r], [osr[0]])
                    free_ob.append(obi)

                def run_units(units, width=2):
                    it = iter(units)
                    active = []

                    def refill():
                        while len(active) < width:
                            try:
                                a_, k_ = next(it)
                            except StopIteration:
                                return
                            active.append(attn_gen(*a_, **k_))
                    refill()
                    while active:
                        for g_ in list(active):
                            try:
                                next(g_)
                            except StopIteration:
                                active.remove(g_)
                                refill()

                all_k = list(range(NT))
                ctx_k = [0, 1]
                def loadA(h, i):
                    S.dma("sp", [(KT[i].t[0:96, :], KTA[h])], KT[i].r, True)
                    S.dma("sp", [(QT[i].t[0:96, 0, :], QTA[h])], QT[i].r, True)
                    S.dma("sp", [(V[i].t[:, :, 0:64], VA[:, h * 64:(h + 1) * 64].rearrange("(k p) d -> p k d", p=128))], V[i].r, True)
                loadA(0, 0)
                for h in range(8):
                    i = h % 2
                    if h + 1 < 8:
                        loadA(h + 1, (h + 1) % 2)
                    rd = [KT[i].r, QT[i].r, V[i].r]
                    ktf = (lambda i: (lambda kt: KT[i].t[0:96, kt * 128:(kt + 1) * 128]))(i)
                    vf = (lambda i: (lambda kt: V[i].t[:, kt, :]))(i)
                    units = []
                    if not last:
                        units.append(((ktf, QT[i].t[0:96, 0, 0:256], 256, vf, ctx_k, SC_A, rd, Os.t[:, 0:2, h * 64:(h + 1) * 64], 2), {}))
                    for c in range(NLT // 4):
                        q0 = 256 + c * 512
                        units.append(((ktf, QT[i].t[0:96, 0, q0:q0 + 512], 512, vf, all_k, SC_A, rd,
                                       Os.t[:, 2 + c * 4:6 + c * 4, h * 64:(h + 1) * 64], 4), {}))
                    run_units(units)
                    if h == 0:
                        precast_C(l, [Os.r])
                        if l + 1 < L:
                            precast_A(l + 1, [Os.r])
                lo = 0 if not last else 2
                S.dma("act", [(OA[lo * 128:T, :].rearrange("(k p) c -> p k c", p=128), Os.t[:, lo:NT, :])], Os.r, False)
                for mix in range(2):
                    dq, dk, dv, do = (QTB, KTB, VB, OB) if mix == 0 else (QTC, KTC, VC, OC)
                    Os = Osb[(mix + 1) % 2]
                    osr[0] = Os.r
                    KTm = KT[mix]
                    S.dma("sp", [(KTm.t[:, :], dk[:, :])], KTm.r, True)
                    if mix == 0:
                        S.op("dve", lambda e: e.memset(QT[0].t[64:128, :, :], 0.0), [], [QT[0].r])
                        S.op("pool", lambda e: e.memset(QT[1].t[0:64, :, :], 0.0), [], [QT[1].r])
                    for g in range(2):
                        S.dma("sp", [(QT[g].t[g * 64:(g + 1) * 64, :, :], dq.rearrange("j d t -> d j t")[g * 64:(g + 1) * 64])], QT[g].r, True)
                        S.dma("sp", [(V[g].t[:, :, 0:64], dv[:, g * 64:(g + 1) * 64].rearrange("(k p) d -> p k d", p=128))], V[g].r, True)
                    for g in range(2):
                        rd = [KTm.r, QT[g].r, V[g].r]
                        units = []
                        ktf = (lambda KTm: (lambda kt: KTm.t[:, kt * 128:(kt + 1) * 128]))(KTm)
                        vf = (lambda g: (lambda kt: V[g].t[:, kt, :]))(g)
                        for t in range(lo, NT):
                            if t < 2:
                                kts = ctx_k
                            elif mix == 0:
                                kts = all_k
                            else:
                                kts = ctx_k + [k for k in (t - 1, t, t + 1) if 2 <= k < NT]
                            mf = None
                            if mix == 1 and t >= 2:
                                def mf(kt, t=t):
                                    if kt < 2:
                                        return None
                                    if kt == t - 1:
                                        return masks.t[:, 0, :]
                                    if kt == t + 1:
                                        return masks.t[:, 1, :]
                                    return None
                            units.append(((ktf, QT[g].t[:, :, t * 128:(t + 1) * 128], 512, vf, kts, SC_H, rd,
                                           Os.t[:, t, g * 256:(g + 1) * 256].rearrange("p (s c) -> p s c", c=64), 4),
                                          dict(sink_ap=(esink.t[:, l, g * 4:(g + 1) * 4] if mix == 1 else None), mask_fn=mf)))
                        run_units(units)
                    S.dma("act", [(do[lo * 128:T, :].rearrange("(k p) c -> p k c", p=128), Os.t[:, lo:NT, :])], Os.r, False)
                S.barrier()

            tlo = 2 if last else 0
            with contextlib.ExitStack() as es:
                wG = mk(es, "wG", [128, 8, 3 * D], BF16)
                wbr = mk(es, "wbr", [128, 3, 4, D], BF16)
                wo = mk(es, "wo", [128, 8, D], BF16)
                wG_r = [S.newres("wG%d" % q) for q in range(4)]
                wbr_r = [S.newres("wbr%d" % q) for q in range(4)]

                def load_weights_C1():
                  for q in range(4):
                    S.dma("sp", [(wG.t[:, j, i * D + q * 256:i * D + (q + 1) * 256], wGb[l, j * 128:(j + 1) * 128, i * D + q * 256:i * D + (q + 1) * 256])
                                 for j in range(8) for i in range(3)], wG_r[q], True)
                    S.dma("sp", [(wbr.t[:, i, k, q * 256:(q + 1) * 256], wbrb[l, i, k * 128:(k + 1) * 128, q * 256:(q + 1) * 256])
                                 for i in range(3) for k in range(4)], wbr_r[q], True)
                  S.dma("sp", [(wo.t[:, j, :], wob[l, j * 128:(j + 1) * 128, :]) for j in range(8)], wo.r, True)
                m2b = mk(es, "m2b", [128, 2, D], F32)
                S.dma("sp", [(m2b.t[:, r, :], mscr[l, r, 2048:3072].unsqueeze(0).partition_broadcast(128)) for r in range(2)], m2b.r, True)
                xt = [mk(es, "cxt%d" % i, [128, 2, D], F32) for i in range(3)]
                ot = [mk(es, "cot%d" % i, [128, 3, 2, 512], BF16) for i in range(2)]
                junk = mk(es, "cjunk", [128, D], F32)
                ss = mk(es, "css", [128, 1], F32)
                sd = mk(es, "csd", [128, 1], F32)
                rs = mk(es, "crs", [128, 1], F32)
                xs = mk(es, "cxs", [128, D], BF16)
                tmp = (junk, ss, sd, rs, xs)
                hT = [mk(es, "chT%d" % i, [128, 8, 256], BF16) for i in range(2)]
                h2T = mk(es, "ch2T", [128, 8, 256], BF16)
                oT = [mk(es, "coT%d" % i, [128, 3, 4, 256], BF16) for i in range(2)]
                gt = [mk(es, "cgt%d" % i, [128, 3, 256], F32) for i in range(2)]
                ya = mk(es, "cya", [128, 256], F32)
                yb = mk(es, "cyb", [128, 256], F32)
                rj = mk(es, "crj", [128, 512], F32)
                yT = [mk(es, "cyT%d" % i, [128, 8, 256], BF16) for i in range(2)]
                pT = [mkp(es, "cpT%d" % i, [128, 8, 128], BF16) for i in range(2)]
                PG = [mkp(es, "cPG%d" % i, [128, 512], F32) for i in range(2)]
                PB = [mkp(es, "cPB%d" % i, [128, 512], F32) for i in range(2)]
                PO = [mkp(es, "cPO%d" % i, [128, 512], F32) for i in range(2)]
                ptc = [0]

                def nextpT():
                    ptc[0] += 1
                    return pT[ptc[0] % 2]
                nblk = (NT - tlo) // 2

                def blk_t0(b):
                    return (tlo + 2 * b) * 128

                def loadblk(b):
                    t0 = blk_t0(b)
                    S.dma("sp", [(xt[b % 3].t[:], (xsrc[t0:t0 + 256, :]).rearrange("(s p) c -> p s c", p=128))], xt[b % 3].r, True)
                    S.dma("sp", [(ot[b % 2].t[:, i], do[t0:t0 + 256, :].rearrange("(s p) c -> p s c", p=128)) for i, do in enumerate((OA, OB, OC))],
                          ot[b % 2].r, True)

                def front(b):
                    t0 = blk_t0(b)
                    r = 1 if t0 < 256 else 0
                    xb = xt[b % 3]
                    ob_ = ot[b % 2]
                    for s in range(2):
                        yield from norm_mod_gen(xb.t[:, s, :], xb.r, tmp, hT[b % 2], s * 128, lambda j: G1.t[:, l, r, j:j + 1],
                                                lambda j: MF.t[:, l, r, j:j + 1], G1.r, nextpT(), pad=2)
                    for i in range(3):
                        for s in range(2):
                            pt_ = nextpT()

                            def tro(e, i=i, s=s, pt_=pt_):
                                for k in range(4):
                                    ins = e.transpose(pt_.t[:, k, :], ob_.t[:, i, s, k * 128:(k + 1) * 128], ident.t[:])
                                return ins
                            S.op("pe", tro, [ob_.r, ident.r], [pt_.r])
                            yield
                            S.op("act", lambda e, i=i, s=s, pt_=pt_: e.activation(out=oT[b % 2].t[:, i, :, s * 128:(s + 1) * 128], in_=pt_.t[:, 0:4, :],
                                                                                 func=AF.Copy), [pt_.r], [oT[b % 2].r])
                            yield

                def mainc(b):
                    hTb, oTb, yTb = hT[b % 2], oT[b % 2], yT[b % 2]
                    c_ = 0
                    for ft in range(8):
                        gtb = gt[ft % 2]
                        banks = []
                        for i in range(3):
                            pg_, pb_ = PG[c_ % 2], PB[c_ % 2]
                            c_ += 1
                            banks.append(pb_)

                            def mg(e, i=i, ft=ft, pg_=pg_):
                                for j in range(8):
                                    ins = e.matmul(pg_.t[:, 0:256], lhsT=wG.t[:, j, i * D + ft * 128:i * D + (ft + 1) * 128], rhs=hTb.t[:, j, :],
                                                   start=(j == 0), stop=(j == 7))
                                return ins
                            S.op("pe", mg, [wG_r[ft // 2], hTb.r], [pg_.r])
                            yield
                            S.op("act", lambda e, i=i, ft=ft, gtb=gtb, pg_=pg_: e.activation(out=gtb.t[:, i, :], in_=pg_.t[:, 0:256], func=AF.Sigmoid,
                                                                                            bias=bgT.t[:, l, i * 8 + ft:i * 8 + ft + 1]), [pg_.r, bgT.r], [gtb.r])
                            yield

                            def mb(e, i=i, ft=ft, pb_=pb_):
                                for k in range(4):
                                    ins = e.matmul(pb_.t[:, 0:256], lhsT=wbr.t[:, i, k, ft * 128:(ft + 1) * 128], rhs=oTb.t[:, i, k, :],
                                                   start=(k == 0), stop=(k == 3))
                                return ins
                            S.op("pe", mb, [wbr_r[ft // 2], oTb.r], [pb_.r])
                            yield
                            if i == 0:
                                S.op("dve", lambda e, gtb=gtb, pb_=pb_: e.tensor_tensor(out=ya.t[:], in0=pb_.t[:, 0:256], in1=gtb.t[:, 0, :], op=ALU.mult),
                                     [pb_.r, gtb.r], [ya.r])
                                yield
                            elif i == 1:
                                S.op("dve", lambda e, gtb=gtb, pb_=pb_: e.tensor_tensor(out=yb.t[:], in0=pb_.t[:, 0:256], in1=gtb.t[:, 1, :], op=ALU.mult),
                                     [pb_.r, gtb.r], [yb.r])
                                yield
                                S.op("pool", lambda e: e.tensor_tensor(out=ya.t[:], in0=ya.t[:], in1=yb.t[:], op=ALU.add), [ya.r, yb.r], [ya.r])
                                yield
                            else:
                                S.op("dve", lambda e, gtb=gtb, pb_=pb_: e.tensor_tensor(out=yb.t[:], in0=pb_.t[:, 0:256], in1=gtb.t[:, 2, :], op=ALU.mult),
                                     [pb_.r, gtb.r], [yb.r])
                                yield
                                S.op("pool", lambda e, ft=ft: e.tensor_tensor(out=yTb.t[:, ft, :], in0=ya.t[:], in1=yb.t[:], op=ALU.add), [ya.r, yb.r], [yTb.r])
                                yield

                def tail(b):
                    t0 = blk_t0(b)
                    r = 1 if t0 < 256 else 0
                    xb = xt[b % 3]
                    yTb = yT[b % 2]
                    for s in range(2):
                        for c in range(2):
                            pb_ = PO[(s * 2 + c) % 2]

                            def mo(e, s=s, c=c, pb_=pb_):
                                for j in range(8):
                                    ins = e.matmul(pb_.t[:, :], lhsT=yTb.t[:, j, s * 128:(s + 1) * 128], rhs=wo.t[:, j, c * 512:(c + 1) * 512],
                                                   start=(j == 0), stop=(j == 7))
                                return ins
                            S.op("pe", mo, [yTb.r, wo.r], [pb_.r])
                            yield
                            yield
                            S.op("dve", lambda e, c=c, pb_=pb_: e.tensor_tensor(out=rj.t[:, 0:512], in0=pb_.t[:, :], in1=m2b.t[:, r, c * 512:(c + 1) * 512],
                                                                                op=ALU.mult), [pb_.r, m2b.r], [rj.r])
                            yield
                            S.op("pool", lambda e, s=s, c=c: e.tensor_tensor(out=xb.t[:, s, c * 512:(c + 1) * 512], in0=xb.t[:, s, c * 512:(c + 1) * 512],
                                                                             in1=rj.t[:, 0:512], op=ALU.add), [xb.r, rj.r], [xb.r])
                            yield
                    S.dma("sp", [(x1s[t0:t0 + 256, :].rearrange("(s p) c -> p s c", p=128), xb.t[:])], xb.r, False)
                    for s in range(2):
                        yield from norm_mod_gen(xb.t[:, s, :], xb.r, tmp, h2T, s * 128, lambda j: G2.t[:, l, r, j:j + 1],
                                                lambda j: MF.t[:, l, r, 24 + j:25 + j], G2.r, nextpT(), pad=2)
                    S.dma("sp", [(H2T.rearrange("j p t -> p j t")[:, :, t0:t0 + 256], h2T.t[:])], h2T.r, False)
                    yield

                def chain(*gs):
                    for g_ in gs:
                        yield from g_

                def rr(gens):
                    gens = list(gens)
                    while gens:
                        for g_ in list(gens):
                            try:
                                next(g_)
                            except StopIteration:
                                gens.remove(g_)

                loadblk(0)
                load_weights_C1()
                rr([front(0)])
                if nblk > 1:
                    loadblk(1)
                for b in range(nblk):
                    side = []
                    if b >= 1:
                        side.append(tail(b - 1))
                    if b + 1 < nblk:
                        side.append(front(b + 1))
                    rr([mainc(b), chain(*side)])
                    if b + 2 < nblk:
                        loadblk(b + 2)
                rr([tail(nblk - 1)])
                S.barrier()

            with contextlib.ExitStack() as es:
                wu = mk(es, "wu", [128, 8, 2 * DFF], BF16)
                wd = mk(es, "wd", [128, 22, D], BF16)
                wu_r = [S.newres("wu%d" % q) for q in range(4)]
                wd_r = [S.newres("wd%d" % q) for q in range(2)]

                def load_weights_C2():
                  for q in range(4):
                    f0, f1 = q * 6 * 128, min(22, (q + 1) * 6) * 128
                    S.dma("sp", [(wu.t[:, j, h_ * DFF + f0:h_ * DFF + f1], wub[l, j * 128:(j + 1) * 128, h_ * DFF + f0:h_ * DFF + f1])
                                 for j in range(8) for h_ in range(2)], wu_r[q], True)
                  for q in range(2):
                    S.dma("sp", [(wd.t[:, j, :], wdb[l, j * 128:(j + 1) * 128, :]) for j in range(q * 11, (q + 1) * 11)], wd_r[q], True)
                m5b = mk(es, "m5b", [128, 2, D], F32)
                S.dma("sp", [(m5b.t[:, r, :], mscr[l, r, 5120:6144].unsqueeze(0).partition_broadcast(128)) for r in range(2)], m5b.r, True)
                hb = [mk(es, "fh%d" % i, [128, 8, 256], BF16) for i in range(2)]
                cab = [mk(es, "fca%d" % i, [128, 256], F32) for i in range(2)]
                cgb = [mk(es, "fcg%d" % i, [128, 256], F32) for i in range(2)]
                sab = [mk(es, "fsa%d" % i, [128, 256], F32) for i in range(2)]
                aT = mk(es, "faT", [128, 22, 256], BF16)
                x1 = [mk(es, "fx1%d" % i, [128, D], F32) for i in range(2)]
                xo = [mk(es, "fxo%d" % i, [128, D], F32) for i in range(2)]
                fj = mk(es, "fj", [128, D], F32)
                fss = mk(es, "fss", [128, 1], F32)
                fsd = mk(es, "fsd", [128, 1], F32)
                frs = mk(es, "frs", [128, 1], F32)
                PU = [mkp(es, "fPU%d" % i, [128, 512], F32) for i in range(6)]
                PD = [mkp(es, "fPD%d" % i, [128, 512], F32) for i in range(2)]
                seqs = ([] if last else [(0, CTX, 1)]) + [(CTX, SEQ, 0)]
                blks = []
                for (s0, sl, r) in seqs:
                    o = 0
                    while o < sl:
                        n = min(254, sl - o)
                        blks.append((s0, sl, r, o, n))
                        o += n

                def loadh(bi):
                    s0, sl, r, o, n = blks[bi]
                    hbb = hb[bi % 2]
                    lo_ = max(o - 1, 0)
                    hi_ = min(o + n + 1, sl)
                    c0 = lo_ - (o - 1)
                    if o == 0:
                        S.op("dve", lambda e: e.memset(hbb.t[:, :, 0:1], 0.0), [], [hbb.r])
                    if o + n == sl:
                        S.op("dve", lambda e: e.memset(hbb.t[:, :, n + 1:n + 2], 0.0), [], [hbb.r])
                    S.dma("sp", [(hbb.t[:, :, c0:c0 + hi_ - lo_], H2T.rearrange("j p t -> p j t")[:, :, s0 + lo_:s0 + hi_])], hbb.r, True)
                loadh(0)
                load_weights_C2()
                xc = 0
                for bi, (s0, sl, r, o, n) in enumerate(blks):
                    hbb = hb[bi % 2]
                    if bi + 1 < len(blks):
                        loadh(bi + 1)
                    N = n + 2
                    for ft in range(22):
                        ca, cg, sa = cab[ft % 2], cgb[ft % 2], sab[ft % 2]
                        pa = PU[(2 * ft) % 6]
                        pg = PU[(2 * ft + 1) % 6]
                        for (pp, col) in ((pa, ft * 128), (pg, DFF + ft * 128)):
                            def mu(e, pp=pp, col=col):
                                for j in range(8):
                                    ins = e.matmul(pp.t[:, 0:N], lhsT=wu.t[:, j, col:col + 128], rhs=hbb.t[:, j, 0:N], start=(j == 0), stop=(j == 7))
                                return ins
                            S.op("pe", mu, [wu_r[ft // 6], hbb.r], [pp.r])
                        for (pp, dst, fi) in ((pa, ca, ft), (pg, cg, 22 + ft)):
                            S.op("act", lambda e, pp=pp, dst=dst, fi=fi: e.activation(out=dst.t[:, 0:n], in_=pp.t[:, 1:n + 1], func=AF.Identity,
                                                                                     scale=cvw.t[:, l, 1, fi:fi + 1], bias=cvw.t[:, l, 3, fi:fi + 1]),
                                 [pp.r, cvw.r], [dst.r])
                            S.op("dve", lambda e, pp=pp, dst=dst, fi=fi: e.scalar_tensor_tensor(out=dst.t[:, 0:n], in0=pp.t[:, 0:n], scalar=cvw.t[:, l, 0, fi:fi + 1],
                                                                                                in1=dst.t[:, 0:n], op0=ALU.mult, op1=ALU.add),
                                 [pp.r, cvw.r, dst.r], [dst.r])
                            S.op("dve", lambda e, pp=pp, dst=dst, fi=fi: e.scalar_tensor_tensor(out=dst.t[:, 0:n], in0=pp.t[:, 2:n + 2], scalar=cvw.t[:, l, 2, fi:fi + 1],
                                                                                                in1=dst.t[:, 0:n], op0=ALU.mult, op1=ALU.add),
                                 [pp.r, cvw.r, dst.r], [dst.r])
                        S.op("act", lambda e: e.activation(out=sa.t[:, 0:n], in_=ca.t[:, 0:n], func=AF.Silu), [ca.r], [sa.r])
                        S.op("pool", lambda e, ft=ft, sa=sa, cg=cg: e.tensor_tensor(out=aT.t[:, ft, 0:n], in0=sa.t[:, 0:n], in1=cg.t[:, 0:n], op=ALU.mult), [sa.r, cg.r], [aT.r])
                    q = 0
                    while q < n:
                        m = min(128, n - q)
                        tok0 = s0 + o + q
                        x1b = x1[xc % 2]
                        xob = xo[xc % 2]
                        S.dma("sp", [(x1b.t[0:m, :], x1s[tok0:tok0 + m, :])], x1b.r, True)
                        for c in range(2):
                            pb_ = PD[c]

                            def md(e, c=c, pb_=pb_, q=q, m=m):
                                for j in range(22):
                                    ins = e.matmul(pb_.t[0:m, :], lhsT=aT.t[:, j, q:q + m], rhs=wd.t[:, j, c * 512:(c + 1) * 512], start=(j == 0), stop=(j == 21))
                                return ins
                            S.op("pe", md, [aT.r, wd_r[0], wd_r[1]], [pb_.r])
                            S.op("dve", lambda e, c=c, pb_=pb_, m=m: e.tensor_tensor(out=fj.t[0:m, 0:512], in0=pb_.t[0:m, :], in1=m5b.t[0:m, r, c * 512:(c + 1) * 512],
                                                                                     op=ALU.mult), [pb_.r, m5b.r], [fj.r])
                            S.op("dve", lambda e, c=c, m=m, x1b=x1b, xob=xob: e.tensor_tensor(out=xob.t[0:m, c * 512:(c + 1) * 512], in0=x1b.t[0:m, c * 512:(c + 1) * 512],
                                                                                              in1=fj.t[0:m, 0:512], op=ALU.add), [x1b.r, fj.r], [xob.r])
                        if not last:
                            S.dma("pool", [(xres[tok0:tok0 + m, :], xob.t[0:m, :])], xob.r, False)
                        else:
                            S.op("act", lambda e, m=m, xob=xob: e.activation(out=fj.t[0:m, :], in_=xob.t[0:m, :], func=AF.Square, accum_out=fss.t[0:m, :]),
                                 [xob.r], [fj.r, fss.r])
                            S.op("dve", lambda e, m=m: e.tensor_scalar(out=fss.t[0:m, :], in0=fss.t[0:m, :], scalar1=1.0 / D, scalar2=EPS, op0=ALU.mult, op1=ALU.add),
                                 [fss.r], [fss.r])
                            S.op("act", lambda e, m=m: e.activation(out=fsd.t[0:m, :], in_=fss.t[0:m, :], func=AF.Sqrt), [fss.r], [fsd.r])
                            S.op("dve", lambda e, m=m: e.reciprocal(out=frs.t[0:m, :], in_=fsd.t[0:m, :]), [fsd.r], [frs.r])
                            S.op("dve", lambda e, m=m, xob=xob: e.scalar_tensor_tensor(out=xob.t[0:m, :], in0=xob.t[0:m, :], scalar=frs.t[0:m, 0:1], in1=gfb.t[0:m, :],
                                                                                       op0=ALU.mult, op1=ALU.mult), [xob.r, frs.r, gfb.r], [xob.r])
                            S.dma("pool", [(out_d[tok0 - CTX:tok0 - CTX + m, :], xob.t[0:m, :])], xob.r, False)
                        xc += 1
                        q += m
                S.barrier()
    return nc


def _host_consts(SEQ):
    T = CTX + SEQ
    rows = SEQ // 64

    def table(hd):
        nf = hd // 4
        row = np.repeat(np.arange(rows, dtype=np.float32), 64)
        col = np.tile(np.arange(64, dtype=np.float32), rows)
        inv = (np.float32(10000.0) ** (-np.arange(nf, dtype=np.float32) / np.float32(nf))).astype(np.float32)
        ang = np.stack([row[:, None] * inv, col[:, None] * inv], axis=1).astype(np.float32)
        cos = np.cos(ang).astype(np.float32)
        sin = np.sin(ang).astype(np.float32)
        cf = np.ones((T, 2, 2, nf), np.float32)
        sf = np.zeros((T, 2, 2, nf), np.float32)
        cf[CTX:, :, 0, :] = cos
        cf[CTX:, :, 1, :] = cos
        sf[CTX:, :, 0, :] = -sin
        sf[CTX:, :, 1, :] = sin
        return np.stack([cf.reshape(T, hd), sf.reshape(T, hd)], axis=1)
    p = np.arange(128)[:, None]
    f = np.arange(128)[None, :]
    masks = np.stack([(f <= p), (p <= f)]).astype(np.float32).astype(ml_dtypes.bfloat16)
    return dict(ropeH=np.ascontiguousarray(table(64)), ropeA=np.ascontiguousarray(table(32)),
                ident=np.eye(128, dtype=np.float32).astype(ml_dtypes.bfloat16), masks=masks)


_NC_CACHE = {}


def _prep_inputs(inp, SEQ, nb):
    f = lambda a: np.ascontiguousarray(np.asarray(a, dtype=np.float32))
    w_in = f(inp["w_in"])
    perm = np.array([(g * 4 + j) * 64 + d for j in range(4) for g in range(2) for d in range(64)])
    offs = np.cumsum([0, 256, 128, 32, 512, 128, 128, 512, 128, 128])
    colsA = np.concatenate([np.arange(0, 416), offs[3] + perm, np.arange(offs[4], offs[6]), offs[6] + perm, np.arange(offs[7], offs[9])])
    assert colsA.size == NA
    w_inA = np.ascontiguousarray(w_in[:, :, colsA])
    w_inG = np.ascontiguousarray(w_in[:, :, offs[9]:])
    gq = f(inp["g_qn"])
    gk = f(inp["g_kn"])

    def sw(g):
        return g.reshape(L, 2, 2, 16)[:, :, ::-1, :].reshape(L, 64)
    g_qk = np.ascontiguousarray(np.stack([gq, sw(gq), gk, sw(gk)], axis=1))
    shared = dict(
        w_mod=f(inp["w_mod"]), b_mod=f(inp["b_mod"]), g_norm1=f(inp["g_norm1"]), g_norm2=f(inp["g_norm2"]),
        w_inA=w_inA, w_inG=w_inG, b_gate=f(inp["b_gate"]), g_q_a=f(inp["g_q_a"]), w_q_b=f(inp["w_q_b"]),
        g_kv_a=f(inp["g_kv_a"]), w_kv_b=f(inp["w_kv_b"]), g_qk=g_qk, sink=f(inp["sink"]),
        w_branch=f(inp["w_branch"]), w_out=f(inp["w_out"]), w_up=f(inp["w_up"]), w_conv=f(inp["w_conv"]),
        b_conv=f(inp["b_conv"]), w_down=f(inp["w_down"]), g_final=f(inp["g_final"]).reshape(1, D))
    shared.update(_host_consts(SEQ))
    x = f(inp["x"])
    ctx = f(inp["ctx"])
    c = f(inp["c"])
    c_ctx = f(inp["c_ctx"])
    maps = []
    for b in range(nb):
        m = dict(shared)
        m["xin"] = np.ascontiguousarray(np.concatenate([ctx[b], x[b]], axis=0))
        m["cvec"] = np.ascontiguousarray(np.stack([c[b], c_ctx], axis=0))
        maps.append(m)
    return maps


def kernel(**inputs):
    x = np.asarray(inputs["x"])
    nb, SEQ = x.shape[0], x.shape[1]
    if SEQ not in _NC_CACHE:
        _NC_CACHE[SEQ] = build_nc(SEQ)
    nc = _NC_CACHE[SEQ]
    maps = _prep_inputs(inputs, SEQ, nb)
    res = run_bass_kernel_spmd(nc, maps, core_ids=list(range(nb)))
    return np.stack([np.asarray(r["out"], dtype=np.float32) for r in res.results], axis=0)
```

```python
import contextlib
import math
import numpy as np
import ml_dtypes
import concourse.bass as bass
import concourse.mybir as mybir
from concourse.bass_utils import run_bass_kernel_spmd
from concourse.alu_op_type import AluOpType as ALU

F32 = mybir.dt.float32
BF16 = mybir.dt.bfloat16
AF = mybir.ActivationFunctionType
AX = mybir.AxisListType

D = 1024
CTX = 256
L = 2
EPS = 1e-6
DFF = 2816
NA = 1952
SC_A = 1.0 / math.sqrt(96.0)
SC_H = 1.0 / 8.0


class Res:
    __slots__ = ("name", "w", "r", "dsem")

    def __init__(self, name):
        self.name = name
        self.w = None
        self.r = {}
        self.dsem = None


class SemObj:
    __slots__ = ("h", "cnt")

    def __init__(self, h):
        self.h = h
        self.cnt = 0


class TT:
    __slots__ = ("t", "r")

    def __init__(self, t, r):
        self.t = t
        self.r = r


class Sched:
    def __init__(self, nc, es, nsem=90):
        self.nc = nc
        self.E = {"pe": nc.tensor, "act": nc.scalar, "dve": nc.vector, "pool": nc.gpsimd, "sp": nc.sync}
        self.pool = [SemObj(es.enter_context(nc.semaphore("sm%d" % i))) for i in range(nsem)]
        self.free = list(self.pool)
        self.esem = {k: self.free.pop() for k in self.E}
        self.waited = {k: {} for k in self.E}
        self.res = []
        self.dma_sems = []
        self.held = []
        self.dma_free = {"sp": [], "pool": [], "act": []}

    def newres(self, name):
        r = Res(name)
        self.res.append(r)
        return r

    def _wait(self, eng, stamps):
        need = {}
        for st in stamps:
            if st is None:
                continue
            so, c, ek = st
            k = id(so)
            if k not in need or need[k][1] < c:
                need[k] = st
        w = self.waited[eng]
        for k, (so, c, ek) in need.items():
            if w.get(k, 0) < c:
                self.E[eng].wait_ge(so.h, c)
                w[k] = c

    def _hazards(self, eng, reads, writes):
        st = []
        for R in reads:
            if R.w is not None:
                if not (R.w[2] == eng and eng == "pe"):
                    st.append(R.w)
        for R in writes:
            if R.w is not None and R.w[2] != eng:
                st.append(R.w)
            for s in R.r.values():
                if s[2] != eng:
                    st.append(s)
        return st

    def op(self, eng, fn, reads=(), writes=()):
        self._wait(eng, self._hazards(eng, reads, writes))
        inst = fn(self.E[eng])
        so = self.esem[eng]
        so.cnt += 1
        inst.then_inc(so.h, 1)
        stamp = (so, so.cnt, eng)
        for R in reads:
            R.r[id(so)] = stamp
        for R in writes:
            R.w = stamp
            R.r = {}
        return inst

    def dma(self, eng, pairs, sb, load, extra_reads=(), kw=None):
        if load:
            hz = self._hazards("dma", list(extra_reads), [sb])
        else:
            hz = self._hazards("dma", [sb] + list(extra_reads), [])
        self._wait(eng, hz)
        if sb.dsem is None:
            sb.dsem = {}
        if eng not in sb.dsem:
            if self.dma_free[eng]:
                sb.dsem[eng] = self.dma_free[eng].pop()
            else:
                sb.dsem[eng] = self.free.pop()
                self.dma_sems.append(sb.dsem[eng])
            self.held.append((eng, sb.dsem[eng]))
        so = sb.dsem[eng]
        for (o, i) in pairs:
            self.E[eng].dma_start(out=o, in_=i, **(kw or {})).then_inc(so.h, 16)
            so.cnt += 16
        stamp = (so, so.cnt, "dma")
        if load:
            sb.w = stamp
            sb.r = {}
        else:
            sb.r[id(so)] = stamp

    def barrier(self):
        stamps = []
        for k, so in self.esem.items():
            if so.cnt > 0:
                stamps.append((so, so.cnt, k))
        for so in self.dma_sems:
            if so.cnt > 0:
                stamps.append((so, so.cnt, "dma"))
        for eng in self.E:
            self._wait(eng, [s for s in stamps if s[2] != eng or eng in ("act", "dve", "pool")])
        for R in self.res:
            R.w = None
            R.r = {}
            R.dsem = None
        for (eng_, so_) in self.held:
            self.dma_free[eng_].append(so_)
        self.held = []
        self.res = []
        for k in ("pe", "act", "dve"):
            if self.esem[k].cnt > 14000:
                self.esem[k] = self.free.pop()


def build_nc(SEQ, dbg=False):
    T = CTX + SEQ
    NT = T // 128
    NLT = SEQ // 128
    nc = bass.Bass("TRN2", target_bir_lowering=False)

    def din(name, shape, dt=F32):
        return nc.dram_tensor(name, list(shape), dt, kind="ExternalInput").ap()

    def dscr(name, shape, dt=BF16):
        return nc.dram_tensor(name, list(shape), dt, kind=("ExternalOutput" if dbg else "Internal")).ap()

    xin = din("xin", [T, D])
    cvec = din("cvec", [2, D])
    w_mod = din("w_mod", [L, D, 6 * D])
    b_mod = din("b_mod", [L, 6 * D])
    g_norm1 = din("g_norm1", [L, D])
    g_norm2 = din("g_norm2", [L, D])
    w_inA = din("w_inA", [L, D, NA])
    w_inG = din("w_inG", [L, D, 3 * D])
    b_gate = din("b_gate", [L, 3 * D])
    g_q_a = din("g_q_a", [L, 256])
    w_q_b = din("w_q_b", [L, 256, 768])
    g_kv_a = din("g_kv_a", [L, 128])
    w_kv_b = din("w_kv_b", [L, 128, 1024])
    g_qk = din("g_qk", [L, 4, 64])
    sink = din("sink", [L, 8])
    w_branch = din("w_branch", [L, 3, 512, D])
    w_out = din("w_out", [L, D, D])
    w_up = din("w_up", [L, D, 2 * DFF])
    w_conv = din("w_conv", [L, 3, 2 * DFF])
    b_conv = din("b_conv", [L, 2 * DFF])
    w_down = din("w_down", [L, DFF, D])
    g_final = din("g_final", [1, D])
    ident_d = din("ident", [128, 128], BF16)
    masks_d = din("masks", [2, 128, 128], BF16)
    ropeH = din("ropeH", [T, 2, 64])
    ropeA = din("ropeA", [T, 2, 32])
    out_d = nc.dram_tensor("out", [SEQ, D], F32, kind="ExternalOutput").ap()

    mscr = dscr("mscr", [L, 2, 6 * D], F32)
    xres = dscr("xres", [T, D], F32)
    x1s = dscr("x1s", [T, D], F32)
    QTA = dscr("QTA", [8, 96, T])
    KTA = dscr("KTA", [8, 96, T])
    VA = dscr("VA", [T, 512])
    QTB = dscr("QTB", [4, 128, T])
    KTB = dscr("KTB", [128, T])
    VB = dscr("VB", [T, 128])
    QTC = dscr("QTC", [4, 128, T])
    KTC = dscr("KTC", [128, T])
    VC = dscr("VC", [T, 128])
    OA = dscr("OA", [T, 512])
    OB = dscr("OB", [T, 512])
    OC = dscr("OC", [T, 512])
    H2T = dscr("H2T", [8, 128, T])
    wAb = dscr("wAb", [L, D, NA])
    wqbb = dscr("wqbb", [L, 256, 768])
    wkvbb = dscr("wkvbb", [L, 128, 1024])
    wGb = dscr("wGb", [L, D, 3 * D])
    wbrb = dscr("wbrb", [L, 3, 512, D])
    wob = dscr("wob", [L, D, D])
    wub = dscr("wub", [L, D, 2 * DFF])
    wdb = dscr("wdb", [L, DFF, D])

    with contextlib.ExitStack() as es0:
        S = Sched(nc, es0)

        uid = [0]

        def mk(es, name, shape, dt):
            uid[0] += 1
            t = es.enter_context(nc.sbuf_tensor("sb_%s_%d" % (name, uid[0]), list(shape), dt))
            return TT(t, S.newres(name))

        def mkp(es, name, shape, dt):
            uid[0] += 1
            t = es.enter_context(nc.psum_tensor("ps_%s_%d" % (name, uid[0]), list(shape), dt))
            return TT(t, S.newres(name))

        def precast(pairs, name, after=()):
            S.dma("pool", pairs, S.newres(name), True, extra_reads=after)

        def rows(ap2, n):
            return [ap2[j * 128:(j + 1) * 128] for j in range(n)]

        def precast_A(l_, after=()):
            prs = []
            for (c0, cw) in ((0, 976), (976, 976)):
                prs += [(o[:, c0:c0 + cw], i[:, c0:c0 + cw]) for o, i in zip(rows(wAb[l_], 8), rows(w_inA[l_], 8))]
            prs += list(zip(rows(wqbb[l_], 2), rows(w_q_b[l_], 2)))
            prs += [(wkvbb[l_], w_kv_b[l_])]
            precast(prs, "pcA%d" % l_, after)

        def precast_C(l_, after=()):
            prs = []
            for c in range(2):
                prs += [(o[:, c * 1536:(c + 1) * 1536], i[:, c * 1536:(c + 1) * 1536]) for o, i in zip(rows(wGb[l_], 8), rows(w_inG[l_], 8))]
            for i_ in range(3):
                prs += list(zip(rows(wbrb[l_, i_], 4), rows(w_branch[l_, i_], 4)))
            prs += list(zip(rows(wob[l_], 8), rows(w_out[l_], 8)))
            precast(prs, "pcC1%d" % l_, after)
            prs = []
            for c in range(4):
                prs += [(o[:, c * 1408:(c + 1) * 1408], i[:, c * 1408:(c + 1) * 1408]) for o, i in zip(rows(wub[l_], 8), rows(w_up[l_], 8))]
            prs += list(zip(rows(wdb[l_], 22), rows(w_down[l_], 22)))
            precast(prs, "pcC2%d" % l_, after)

        precast_A(0)

        ident = mk(es0, "ident", [128, 128], BF16)
        masks = mk(es0, "masks", [128, 2, 128], BF16)
        MF = mk(es0, "MF", [128, L, 2, 48], F32)
        G1 = mk(es0, "G1", [128, L, 2, 8], F32)
        G2 = mk(es0, "G2", [128, L, 2, 8], F32)
        gn = mk(es0, "gn", [128, 2, L, 8], F32)
        bgT = mk(es0, "bgT", [128, L, 24], F32)
        cvw = mk(es0, "cvw", [128, L, 4, 44], F32)
        gqa = mk(es0, "gqa", [128, L, 2], F32)
        gkva = mk(es0, "gkva", [128, L, 1], F32)
        gqkb = mk(es0, "gqkb", [128, L, 4, 64], F32)
        esink = mk(es0, "esink", [128, L, 8], F32)
        gfb = mk(es0, "gfb", [128, D], F32)
        invn = mk(es0, "invn", [128, 12], F32)
        epsb = mk(es0, "epsb", [128, 1], F32)

        S.dma("sp", [(ident.t[:], ident_d[:, :])], ident.r, True)
        S.dma("sp", [(masks.t[:], masks_d.rearrange("m p f -> p m f"))], masks.r, True)
        es0.enter_context(nc.allow_non_contiguous_dma(reason="strided parameter / staging DMAs"))
        fm = lambda v: v.rearrange("(j p) -> p j", p=128)
        S.dma("sp", [(gn.t[:, 0, l_, :], fm(g_norm1[l_])) for l_ in range(L)] + [(gn.t[:, 1, l_, :], fm(g_norm2[l_])) for l_ in range(L)], gn.r, True)
        S.dma("sp", [(bgT.t[:, l_, :], fm(b_gate[l_])) for l_ in range(L)], bgT.r, True)
        S.dma("sp", [(cvw.t[:, l_, c_, :], fm(w_conv[l_, c_])) for l_ in range(L) for c_ in range(3)]
              + [(cvw.t[:, l_, 3, :], fm(b_conv[l_])) for l_ in range(L)], cvw.r, True)
        S.dma("sp", [(gqa.t[:, l_, :], fm(g_q_a[l_])) for l_ in range(L)], gqa.r, True)
        S.dma("sp", [(gkva.t[:, l_, :], fm(g_kv_a[l_])) for l_ in range(L)], gkva.r, True)
        S.dma("sp", [(gqkb.t[:].rearrange("p l a d -> p (l a d)"),
                      g_qk.rearrange("l a d -> (l a d)").unsqueeze(0).partition_broadcast(128))], gqkb.r, True)
        S.dma("sp", [(esink.t[:].rearrange("p l h -> p (l h)"),
                      sink.rearrange("l h -> (l h)").unsqueeze(0).partition_broadcast(128))], esink.r, True)
        S.dma("sp", [(gfb.t[:], g_final.partition_broadcast(128))], gfb.r, True)
        S.op("act", lambda e: e.activation(out=esink.t[:], in_=esink.t[:], func=AF.Exp), [esink.r], [esink.r])
        S.op("dve", lambda e: e.memset(invn.t[:, 0:1], 1.0 / 256), [], [invn.r])
        S.op("dve", lambda e: e.memset(invn.t[:, 1:2], 1.0 / 128), [], [invn.r])
        S.op("dve", lambda e: e.memset(invn.t[:, 2:12], 1.0 / 64), [], [invn.r])
        S.op("dve", lambda e: e.memset(epsb.t[:], EPS), [], [epsb.r])

        with contextlib.ExitStack() as es:
            cT = mk(es, "cT", [128, 8, 2], F32)
            LT = mk(es, "LT", [128, 8, 33], F32)
            wm = [mk(es, "wm%d" % i, [128, 8, 512], F32) for i in range(2)]
            bm = mk(es, "bm", [33, L, 6 * D], F32)
            mrow = mk(es, "mrow", [33, L, 6 * D], F32)
            pm = [mkp(es, "pm%d" % i, [128, 512], F32) for i in range(2)]
            S.dma("sp", [(cT.t[:, :, r_], fm(cvec[r_])) for r_ in range(2)], cT.r, True)
            S.dma("sp", [(bm.t[p_:p_ + 1, l_, :], b_mod[l_].unsqueeze(0)) for p_ in (0, 32) for l_ in range(L)], bm.r, True)
            S.op("dve", lambda e: e.memset(LT.t[:], 0.0), [], [LT.r])
            S.op("act", lambda e: e.activation(out=LT.t[:, :, 0:1], in_=cT.t[:, :, 0:1], func=AF.Silu), [cT.r, LT.r], [LT.r])
            S.op("act", lambda e: e.activation(out=LT.t[:, :, 32:33], in_=cT.t[:, :, 1:2], func=AF.Silu), [cT.r, LT.r], [LT.r])
            i = 0
            for l in range(L):
                for cc in range(12):
                    wb_ = wm[i % 2]
                    pb_ = pm[i % 2]
                    S.dma("sp", [(wb_.t[:], w_mod[l].rearrange("(j p) c -> p j c", p=128)[:, :, cc * 512:(cc + 1) * 512])], wb_.r, True)

                    def mm(e, wb_=wb_, pb_=pb_):
                        for j in range(8):
                            ins = e.matmul(pb_.t[0:33, :], lhsT=LT.t[:, j, :], rhs=wb_.t[:, j, :], start=(j == 0), stop=(j == 7))
                        return ins
                    S.op("pe", mm, [LT.r, wb_.r], [pb_.r])
                    for r0 in (0, 32):
                        S.op("dve", lambda e, r0=r0, pb_=pb_, l=l, cc=cc: e.tensor_tensor(
                            out=mrow.t[r0:r0 + 1, l, cc * 512:(cc + 1) * 512], in0=pb_.t[r0:r0 + 1, :],
                            in1=bm.t[r0:r0 + 1, l, cc * 512:(cc + 1) * 512], op=ALU.add), [pb_.r, bm.r], [mrow.r])
                    i += 1
            S.dma("sp", [(mscr[l_, r_, :].unsqueeze(0), mrow.t[32 * r_:32 * r_ + 1, l_, :]) for l_ in range(L) for r_ in range(2)], mrow.r, False)
            S.barrier()
        S.dma("sp", [(MF.t[:, l, r, :], mscr[l, r].rearrange("(j p) -> p j", p=128)) for l in range(L) for r in range(2)], MF.r, True)
        for l in range(L):
            for r in range(2):
                S.op("dve", lambda e, l=l, r=r: e.scalar_tensor_tensor(
                    out=G1.t[:, l, r, :], in0=MF.t[:, l, r, 8:16], scalar=1.0, in1=gn.t[:, 0, l, :], op0=ALU.add, op1=ALU.mult),
                    [MF.r, gn.r], [G1.r])
                S.op("dve", lambda e, l=l, r=r: e.scalar_tensor_tensor(
                    out=G2.t[:, l, r, :], in0=MF.t[:, l, r, 32:40], scalar=1.0, in1=gn.t[:, 1, l, :], op0=ALU.add, op1=ALU.mult),
                    [MF.r, gn.r], [G2.r])

        def norm_mod_gen(xt_ap, xr, tmp, hT, col0, Gt, shift_ap_fn, gsel, pT, pad=0):
            junk, ss, sd, rs, xs = tmp
            S.op("act", lambda e: e.activation(out=junk.t[:], in_=xt_ap, func=AF.Square, accum_out=ss.t[:]), [xr], [junk.r, ss.r])
            yield
            for _p in range(pad):
                yield
            S.op("dve", lambda e: e.tensor_scalar(out=ss.t[:], in0=ss.t[:], scalar1=1.0 / D, scalar2=EPS, op0=ALU.mult, op1=ALU.add),
                 [ss.r], [ss.r])
            yield
            for _p in range(pad):
                yield
            S.op("act", lambda e: e.activation(out=sd.t[:], in_=ss.t[:], func=AF.Sqrt), [ss.r], [sd.r])
            yield
            for _p in range(pad):
                yield
            S.op("dve", lambda e: e.reciprocal(out=rs.t[:], in_=sd.t[:]), [sd.r], [rs.r])
            yield
            for _p in range(pad):
                yield
            S.op("act", lambda e: e.activation(out=xs.t[:], in_=xt_ap, func=AF.Copy, scale=rs.t[:, 0:1]), [xr, rs.r], [xs.r])
            yield
            for _p in range(pad):
                yield

            def tr(e):
                for j in range(8):
                    ins = e.transpose(pT.t[:, j, :], xs.t[:, j * 128:(j + 1) * 128], ident.t[:])
                return ins
            S.op("pe", tr, [xs.r, ident.r], [pT.r])
            yield
            for j in range(8):
                S.op("dve", lambda e, j=j: e.tensor_scalar(out=hT.t[:, j, col0:col0 + 128], in0=pT.t[:, j, :],
                                                          scalar1=Gt(j), scalar2=shift_ap_fn(j), op0=ALU.mult, op1=ALU.add),
                     [pT.r, gsel, MF.r], [hT.r])
                yield

        def norm_mod_T(*a_, **k_):
            for _ in norm_mod_gen(*a_, **k_):
                pass

        for l in range(L):
            last = (l == L - 1)
            xsrc = xin if l == 0 else xres
            with contextlib.ExitStack() as es:
                wA = mk(es, "wA", [128, 8, NA], BF16)
                wqb = mk(es, "wqb", [128, 2, 768], BF16)
                wkvb = mk(es, "wkvb", [128, 1024], BF16)
                ACH = ((0, 416), (416, 512), (928, 512), (1440, 512))
                wA_r = [S.newres("wA%d" % c) for c in range(4)]

                def load_weights_A():
                    for c, (c0, cw) in enumerate(ACH):
                        S.dma("sp", [(wA.t[:, :, c0:c0 + cw], wAb[l].rearrange("(j p) c -> p j c", p=128)[:, :, c0:c0 + cw])], wA_r[c], True)
                    S.dma("sp", [(wqb.t[:, :, :], wqbb[l].rearrange("(j p) c -> p j c", p=128))], wqb.r, True)
                    S.dma("sp", [(wkvb.t[:], wkvbb[l])], wkvb.r, True)
                xt = [mk(es, "xt%d" % i, [128, D], F32) for i in range(2)]
                rp = [mk(es, "rp%d" % i, [128, 2, 96], F32) for i in range(3)]
                Pc = [mk(es, "Pc%d" % i, [128, NA], F32) for i in range(2)]
                junk = mk(es, "junk", [128, D], F32)
                ss = mk(es, "ss", [128, 1], F32)
                sd = mk(es, "sd", [128, 1], F32)
                rs = mk(es, "rs", [128, 1], F32)
                xs = mk(es, "xs", [128, D], BF16)
                hT = [mk(es, "hT%d" % i, [128, 8, 128], BF16) for i in range(2)]
                ss12b = [mk(es, "ss12%d" % i, [128, 12], F32) for i in range(2)]
                sd12b = [mk(es, "sd12%d" % i, [128, 12], F32) for i in range(2)]
                rs12b = [mk(es, "rs12%d" % i, [128, 12], F32) for i in range(2)]
                sqtb = [mk(es, "sqt%d" % i, [128, 640], F32) for i in range(2)]
                tbb = [mk(es, "tb%d" % i, [128, 4, 64], F32) for i in range(2)]
                u = mk(es, "u", [128, 768], F32)
                t1a = mk(es, "t1a", [128, 256], F32)
                t2a = mk(es, "t2a", [128, 256], F32)
                t2k = mk(es, "t2k", [128, 32], F32)
                mt = [(mk(es, "ubm%d" % i, [128, 512], F32), mk(es, "t1m%d" % i, [128, 512], F32), mk(es, "t2m%d" % i, [128, 512], F32)) for i in range(2)]
                qn = mk(es, "qn", [128, 384], BF16)
                qnT = mk(es, "qnT", [128, 3, 128], BF16)
                kr = mk(es, "kr", [128, 32], F32)
                QAt = mk(es, "QAt", [128, 8, 96], BF16)
                KAt = mk(es, "KAt", [128, 8, 96], BF16)
                QBt = mk(es, "QBt", [128, 512], BF16)
                KBt = mk(es, "KBt", [128, 128], BF16)
                QCt = mk(es, "QCt", [128, 512], BF16)
                KCt = mk(es, "KCt", [128, 128], BF16)
                stg = []
                for i in range(2):
                    stg.append(dict(
                        QTA=mk(es, "sQTA%d" % i, [128, 8, 512], BF16), KTA=mk(es, "sKTA%d" % i, [128, 8, 512], BF16),
                        VA=mk(es, "sVA%d" % i, [128, 4, 512], BF16),
                        QTB=mk(es, "sQTB%d" % i, [128, 4, 512], BF16), KTB=mk(es, "sKTB%d" % i, [128, 512], BF16),
                        VB=mk(es, "sVB%d" % i, [128, 4, 128], BF16),
                        QTC=mk(es, "sQTC%d" % i, [128, 4, 512], BF16), KTC=mk(es, "sKTC%d" % i, [128, 512], BF16),
                        VC=mk(es, "sVC%d" % i, [128, 4, 128], BF16)))
                pTA = mkp(es, "pTA", [128, 8, 128], BF16)
                pTs = [mkp(es, "pTs%d" % i, [128, 8, 128], BF16) for i in range(3)]
                P = [mkp(es, "P%d" % i, [128, 512], F32) for i in range(2)]
                Q = [mkp(es, "Q%d" % i, [128, 512], F32) for i in range(2)]
                tmp = (junk, ss, sd, rs, xs)
                ptc = [0]
                pcc = [0]

                def nextpT():
                    ptc[0] += 1
                    return pTs[ptc[0] % 3]

                blocks = [(0, 2)] + [(2 + 4 * b, 4) for b in range(NLT // 4)]
                tiles = []
                for bi, (tb0, ntile) in enumerate(blocks):
                    for s in range(ntile):
                        tiles.append((tb0 + s, bi, s, s == ntile - 1, tb0, ntile))

                def ld_tile(t, k):
                    S.dma("sp", [(xt[k % 2].t[:], xsrc[t * 128:(t + 1) * 128, :])], xt[k % 2].r, True)
                    S.dma("sp", [(rp[k % 3].t[:, :, 0:64], ropeH[t * 128:(t + 1) * 128]),
                                 (rp[k % 3].t[:, :, 64:96], ropeA[t * 128:(t + 1) * 128])], rp[k % 3].r, True)

                def stage1(k):
                    t = tiles[k][0]
                    r = 1 if t < 2 else 0
                    xb = xt[k % 2]
                    hTb = hT[k % 2]
                    pc = Pc[k % 2]
                    if k + 1 < len(tiles):
                        ld_tile(tiles[k + 1][0], k + 1)
                    yield from norm_mod_gen(xb.t[:], xb.r, tmp, hTb, 0, lambda j: G1.t[:, l, r, j:j + 1],
                                            lambda j: MF.t[:, l, r, j:j + 1], G1.r, pTA)
                    for ci_, (c0, cw) in enumerate(ACH):
                        pb_ = P[pcc[0] % 2]
                        pcc[0] += 1

                        def mm(e, c0=c0, cw=cw, pb_=pb_):
                            for j in range(8):
                                ins = e.matmul(pb_.t[:, 0:cw], lhsT=hTb.t[:, j, :], rhs=wA.t[:, j, c0:c0 + cw],
                                               start=(j == 0), stop=(j == 7))
                            return ins
                        S.op("pe", mm, [hTb.r, wA_r[ci_]], [pb_.r])
                        yield
                        S.op("act", lambda e, c0=c0, cw=cw, pb_=pb_: e.activation(out=pc.t[:, c0:c0 + cw], in_=pb_.t[:, 0:cw], func=AF.Copy),
                             [pb_.r], [pc.r])
                        yield

                def prefix(k):
                    pc = Pc[k % 2]
                    rpb = rp[k % 3]
                    X = pc.t
                    ss12, sd12, rs12, sqt, tb = ss12b[k % 2], sd12b[k % 2], rs12b[k % 2], sqtb[k % 2], tbb[k % 2]
                    for a in range(4):
                        S.op("pool", lambda e, a=a: e.tensor_tensor(out=tb.t[:, a, :], in0=rpb.t[:, a % 2, 0:64],
                                                                    in1=gqkb.t[:, l, a, :], op=ALU.mult), [rpb.r, gqkb.r], [tb.r])
                        yield
                    S.op("act", lambda e: e.activation(out=sqt.t[:, 0:256], in_=X[:, 0:256], func=AF.Square, accum_out=ss12.t[:, 0:1]),
                         [pc.r], [sqt.r, ss12.r])
                    yield
                    S.op("act", lambda e: e.activation(out=sqt.t[:, 0:128], in_=X[:, 256:384], func=AF.Square, accum_out=ss12.t[:, 1:2]),
                         [pc.r], [sqt.r, ss12.r])
                    yield
                    S.op("act", lambda e: e.activation(out=sqt.t[:, 0:640], in_=X[:, 416:1056], func=AF.Square), [pc.r], [sqt.r])
                    yield
                    S.op("dve", lambda e: e.tensor_reduce(out=ss12.t[:, 2:12], in_=sqt.t[:, 0:640].rearrange("p (h d) -> p h d", d=64),
                                                         axis=AX.X, op=ALU.add), [sqt.r], [ss12.r])
                    yield
                    S.op("dve", lambda e: e.tensor_tensor(out=ss12.t[:], in0=ss12.t[:], in1=invn.t[:], op=ALU.mult), [ss12.r, invn.r], [ss12.r])
                    yield
                    S.op("act", lambda e: e.activation(out=sd12.t[:], in_=ss12.t[:], func=AF.Sqrt, bias=epsb.t[:, 0:1]), [ss12.r, epsb.r], [sd12.r])
                    yield
                    S.op("dve", lambda e: e.reciprocal(out=rs12.t[:], in_=sd12.t[:]), [sd12.r], [rs12.r])
                    yield

                def stage2(k, extra=()):
                    t, bi, s, _, _, _ = tiles[k]
                    sg = stg[bi % 2]
                    pc = Pc[k % 2]
                    rpb = rp[k % 3]
                    X = pc.t
                    rs12, tb = rs12b[k % 2], tbb[k % 2]

                    def chainA():
                        S.op("dve", lambda e: e.tensor_scalar(out=qn.t[:, 0:256], in0=X[:, 0:256], scalar1=rs12.t[:, 0:1], scalar2=None,
                                                             op0=ALU.mult), [pc.r, rs12.r], [qn.r])
                        yield
                        S.op("dve", lambda e: e.tensor_scalar(out=qn.t[:, 256:384], in0=X[:, 256:384], scalar1=rs12.t[:, 1:2], scalar2=None,
                                                             op0=ALU.mult), [pc.r, rs12.r], [qn.r])
                        yield
                        pq = nextpT()

                        def trq(e, pq=pq):
                            for j in range(3):
                                ins = e.transpose(pq.t[:, j, :], qn.t[:, j * 128:(j + 1) * 128], ident.t[:])
                            return ins
                        S.op("pe", trq, [qn.r, ident.r], [pq.r])
                        yield
                        for j in range(3):
                            gsc = gqa.t[:, l, j:j + 1] if j < 2 else gkva.t[:, l, 0:1]
                            S.op("dve", lambda e, j=j, gsc=gsc, pq=pq: e.tensor_scalar(out=qnT.t[:, j, :], in0=pq.t[:, j, :], scalar1=gsc,
                                                                                      scalar2=None, op0=ALU.mult), [pq.r, gqa.r, gkva.r], [qnT.r])
                            yield
                        for (c0, cw, qb_) in ((0, 512, Q[0]), (512, 256, Q[1])):
                            def mmq(e, c0=c0, cw=cw, qb_=qb_):
                                for j in range(2):
                                    ins = e.matmul(qb_.t[:, 0:cw], lhsT=qnT.t[:, j, :], rhs=wqb.t[:, j, c0:c0 + cw], start=(j == 0), stop=(j == 1))
                                return ins
                            S.op("pe", mmq, [qnT.r, wqb.r], [qb_.r])
                            yield
                        sa4 = rpb.t[:, 1, 64:96].rearrange("p (a q f) -> p a q f", a=2, q=2)
                        S.op("act", lambda e: e.activation(out=u.t[:, 0:512], in_=Q[0].t[:, :], func=AF.Copy), [Q[0].r], [u.r])
                        yield
                        S.op("act", lambda e: e.activation(out=u.t[:, 512:768], in_=Q[1].t[:, 0:256], func=AF.Copy), [Q[1].r], [u.r])
                        yield
                        for hh, qb_ in ((0, Q[0]), (1, Q[1])):
                            S.op("pe", lambda e, hh=hh, qb_=qb_: e.matmul(qb_.t[:, :], lhsT=qnT.t[:, 2, :], rhs=wkvb.t[:, hh * 512:(hh + 1) * 512],
                                                                          start=True, stop=True), [qnT.r, wkvb.r], [qb_.r])
                            yield
                        u3 = u.t[:].rearrange("p (h d) -> p h d", d=96)
                        S.op("act", lambda e: e.activation(out=QAt.t[:, :, 0:64], in_=u3[:, :, 0:64], func=AF.Copy), [u.r], [QAt.r])
                        yield
                        ca_b = rpb.t[:, 0, 64:96].unsqueeze(1).broadcast_to([128, 8, 32])
                        t13 = t1a.t[:, 0:256].rearrange("p (h d) -> p h d", d=32)
                        S.op("dve", lambda e: e.tensor_tensor(out=t13, in0=u3[:, :, 64:96], in1=ca_b, op=ALU.mult), [u.r, rpb.r], [t1a.r])
                        yield
                        u5 = u3[:, :, 64:96].rearrange("p h (a q f) -> p h a q f", a=2, q=2)
                        t25 = t2a.t[:, 0:256].rearrange("p (h a q f) -> p h a q f", a=2, q=2, f=8)
                        for q_ in range(2):
                            S.op("pool", lambda e, q_=q_: e.tensor_tensor(
                                out=t25[:, :, :, q_, :], in0=u5[:, :, :, 1 - q_, :],
                                in1=sa4[:, :, q_, :].unsqueeze(1).broadcast_to([128, 8, 2, 8]),
                                op=ALU.mult), [u.r, rpb.r], [t2a.r])
                            yield
                        for hh, qb_ in ((0, Q[0]), (1, Q[1])):
                            kv3 = qb_.t[:, :].rearrange("p (h d) -> p h d", d=128)
                            S.op("act", lambda e, hh=hh, kv3=kv3: e.activation(out=KAt.t[:, hh * 4:(hh + 1) * 4, 0:64], in_=kv3[:, :, 0:64], func=AF.Copy),
                                 [qb_.r], [KAt.r])
                            yield
                            S.op("dve", lambda e, hh=hh, kv3=kv3: e.tensor_copy(
                                out=sg["VA"].t[:, s, hh * 256:(hh + 1) * 256].rearrange("p (h d) -> p h d", d=64), in_=kv3[:, :, 64:128]),
                                [qb_.r], [sg["VA"].r])
                            yield
                        S.op("dve", lambda e: e.tensor_tensor(out=QAt.t[:, :, 64:96], in0=t13, in1=t2a.t[:, 0:256].rearrange("p (h d) -> p h d", d=32),
                                                             op=ALU.add), [t1a.r, t2a.r], [QAt.r])
                        yield
                        for (src, dst) in ((KAt, sg["KTA"]), (QAt, sg["QTA"])):
                            pt_ = nextpT()

                            def trh(e, src=src, pt_=pt_):
                                for h in range(8):
                                    ins = e.transpose(pt_.t[0:96, h, :], src.t[:, h, :], ident.t[:])
                                return ins
                            S.op("pe", trh, [src.r, ident.r], [pt_.r])
                            yield
                            S.op("act", lambda e, dst=dst, pt_=pt_: e.activation(out=dst.t[0:96, :, s * 128:(s + 1) * 128], in_=pt_.t[0:96, :, :],
                                                                                 func=AF.Copy), [pt_.r], [dst.r])
                            yield

                    def chainKr():
                        S.op("dve", lambda e: e.tensor_tensor(out=kr.t[:], in0=X[:, 384:416], in1=rpb.t[:, 0, 64:96], op=ALU.mult),
                             [pc.r, rpb.r], [kr.r])
                        yield
                        akr4 = X[:, 384:416].rearrange("p (a q f) -> p a q f", a=2, q=2)
                        sa4 = rpb.t[:, 1, 64:96].rearrange("p (a q f) -> p a q f", a=2, q=2)
                        t24 = t2k.t[:, 0:32].rearrange("p (a q f) -> p a q f", a=2, q=2)
                        for q_ in range(2):
                            S.op("pool", lambda e, q_=q_: e.tensor_tensor(out=t24[:, :, q_, :], in0=akr4[:, :, 1 - q_, :], in1=sa4[:, :, q_, :],
                                                                         op=ALU.mult), [pc.r, rpb.r], [t2k.r])
                            yield
                        S.op("dve", lambda e: e.tensor_tensor(out=kr.t[:], in0=kr.t[:], in1=t2k.t[:, 0:32], op=ALU.add), [kr.r, t2k.r], [kr.r])
                        yield
                        S.op("dve", lambda e: e.tensor_copy(out=KAt.t[:, :, 64:96], in_=kr.t[:].unsqueeze(1).broadcast_to([128, 8, 32])),
                             [kr.r], [KAt.r])
                        yield

                    def chainM(mix):
                        qo, ko, vo = (416, 928, 1056) if mix == 0 else (1184, 1696, 1824)
                        Qt_, Kt_ = (QBt, KBt) if mix == 0 else (QCt, KCt)
                        ub_, t1_, t2_ = mt[mix]
                        for (nh, so_, dstT, ci, si, rcol) in ((8, qo, Qt_, 0, 1, 2), (2, ko, Kt_, 2, 3, 10)):
                            w_ = nh * 64
                            src_ap = X[:, so_:so_ + w_]
                            if mix == 0:
                                S.op("dve", lambda e, src_ap=src_ap, nh=nh, rcol=rcol, w_=w_: e.tensor_tensor(
                                    out=ub_.t[:, 0:w_].rearrange("p (h d) -> p h d", d=64), in0=src_ap.rearrange("p (h d) -> p h d", d=64),
                                    in1=rs12.t[:, rcol:rcol + nh].unsqueeze(2).broadcast_to([128, nh, 64]), op=ALU.mult),
                                    [pc.r, rs12.r], [ub_.r])
                                yield
                                uflat = ub_.t[:, 0:w_]
                                ur = ub_.r
                                Ct = tb.t[:, ci, :]
                                St = tb.t[:, si, :]
                                tr_ = tb.r
                            else:
                                uflat = src_ap
                                ur = pc.r
                                Ct = rpb.t[:, 0, 0:64]
                                St = rpb.t[:, 1, 0:64]
                                tr_ = rpb.r
                            uu = uflat.rearrange("p (h d) -> p h d", d=64)
                            S.op("dve", lambda e, uu=uu, Ct=Ct, nh=nh, w_=w_: e.tensor_tensor(
                                out=t1_.t[:, 0:w_].rearrange("p (h d) -> p h d", d=64), in0=uu,
                                in1=Ct.unsqueeze(1).broadcast_to([128, nh, 64]), op=ALU.mult), [ur, tr_], [t1_.r])
                            yield
                            u5 = uflat.rearrange("p (h a q f) -> p h a q f", a=2, q=2, f=16)
                            t25 = t2_.t[:, 0:w_].rearrange("p (h a q f) -> p h a q f", a=2, q=2, f=16)
                            S4 = St.rearrange("p (a q f) -> p a q f", a=2, q=2)
                            for q_ in range(2):
                                S.op("pool", lambda e, q_=q_, u5=u5, t25=t25, S4=S4, nh=nh: e.tensor_tensor(
                                    out=t25[:, :, :, q_, :], in0=u5[:, :, :, 1 - q_, :],
                                    in1=S4[:, :, q_, :].unsqueeze(1).broadcast_to([128, nh, 2, 16]), op=ALU.mult), [ur, tr_], [t2_.r])
                                yield
                            S.op("dve", lambda e, dstT=dstT, w_=w_: e.tensor_tensor(out=dstT.t[:, 0:w_], in0=t1_.t[:, 0:w_], in1=t2_.t[:, 0:w_], op=ALU.add),
                                 [t1_.r, t2_.r], [dstT.r])
                            yield
                        vdst = sg["VB"] if mix == 0 else sg["VC"]
                        S.op("pool", lambda e, vdst=vdst, vo=vo: e.tensor_copy(out=vdst.t[:, s, :], in_=X[:, vo:vo + 128]), [pc.r], [vdst.r])
                        yield
                        pt_ = nextpT()
                        qdst = sg["QTB"] if mix == 0 else sg["QTC"]
                        kdst = sg["KTB"] if mix == 0 else sg["KTC"]

                        def trb(e, Qt_=Qt_, Kt_=Kt_, pt_=pt_):
                            for j in range(4):
                                e.transpose(pt_.t[:, j, :], Qt_.t[:, j * 128:(j + 1) * 128], ident.t[:])
                            return e.transpose(pt_.t[:, 4, :], Kt_.t[:, :], ident.t[:])
                        S.op("pe", trb, [Qt_.r, Kt_.r, ident.r], [pt_.r])
                        yield
                        S.op("act", lambda e, qdst=qdst, pt_=pt_: e.activation(out=qdst.t[:, :, s * 128:(s + 1) * 128], in_=pt_.t[:, 0:4, :], func=AF.Copy),
                             [pt_.r], [qdst.r])
                        yield
                        S.op("act", lambda e, kdst=kdst, pt_=pt_: e.activation(out=kdst.t[:, s * 128:(s + 1) * 128], in_=pt_.t[:, 4, :], func=AF.Copy),
                             [pt_.r], [kdst.r])
                        yield

                    ga = chainA()
                    gens = [ga] + list(extra) + [chainKr(), ga, chainM(1), chainM(0)]
                    while gens:
                        for g_ in list(gens):
                            if g_ not in gens:
                                continue
                            try:
                                next(g_)
                            except StopIteration:
                                while g_ in gens:
                                    gens.remove(g_)

                def stores(k):
                    t, bi, s, _, tb0, ntile = tiles[k]
                    sg = stg[bi % 2]
                    t0 = tb0 * 128
                    n = ntile * 128
                    S.dma("sp", [(QTA.rearrange("h d t -> d h t")[:, :, t0:t0 + n], sg["QTA"].t[0:96, :, 0:n])], sg["QTA"].r, False)
                    S.dma("sp", [(KTA.rearrange("h d t -> d h t")[:, :, t0:t0 + n], sg["KTA"].t[0:96, :, 0:n])], sg["KTA"].r, False)
                    S.dma("sp", [(VA[t0:t0 + n, :].rearrange("(s p) c -> p s c", p=128), sg["VA"].t[:, 0:ntile, :])], sg["VA"].r, False)
                    for nm, dq, dk, dv in (("B", QTB, KTB, VB), ("C", QTC, KTC, VC)):
                        S.dma("sp", [(dq.rearrange("j d t -> d j t")[:, :, t0:t0 + n], sg["QT" + nm].t[:, :, 0:n])], sg["QT" + nm].r, False)
                        S.dma("sp", [(dk[:, t0:t0 + n], sg["KT" + nm].t[:, 0:n])], sg["KT" + nm].r, False)
                        S.dma("sp", [(dv[t0:t0 + n, :].rearrange("(s p) c -> p s c", p=128), sg["V" + nm].t[:, 0:ntile, :])], sg["V" + nm].r, False)

                ld_tile(tiles[0][0], 0)
                load_weights_A()
                def s1p(k):
                    yield from stage1(k)
                    yield from prefix(k)
                for _ in s1p(0):
                    pass
                for k in range(len(tiles)):
                    stage2(k, [s1p(k + 1)] if k + 1 < len(tiles) else [])
                    if tiles[k][3]:
                        stores(k)
                S.barrier()

            with contextlib.ExitStack() as es:
                KT = [mk(es, "KT%d" % i, [128, T], BF16) for i in range(2)]
                QT = [mk(es, "QT%d" % i, [128, 4, T], BF16) for i in range(2)]
                V = [mk(es, "V%d" % i, [128, NT, 65], BF16) for i in range(2)]
                Osb = [mk(es, "Os%d" % i, [128, NT, 512], BF16) for i in range(2)]
                Os = Osb[0]
                PT = [mk(es, "PT%d" % i, [128, 1024], BF16) for i in range(4)]
                rl = [mk(es, "rl%d" % i, [128, 4], F32) for i in range(2)]
                Sb = [mkp(es, "Sb%d" % i, [128, 1024], F32) for i in range(3)]
                Ob = [mkp(es, "Ob%d" % i, [128, 512], F32) for i in range(2)]
                for i in range(2):
                    S.op("dve", lambda e, i=i: e.memset(V[i].t[:, :, 64:65], 1.0), [], [V[i].r])
                cnt = {"s": 0, "p": 0, "o": 0}
                osr = [Osb[0].r]

                free_ob = [1, 0]

                def attn_gen(kt_ap_fn, q_ap, nq, v_ap_fn, kts, scale, reads, out_ap, nsub, sink_ap=None, mask_fn=None, LOOK=1):
                    obi = free_ob.pop()
                    ob = Ob[obi]
                    rlb = rl[obi]
                    O3 = ob.t[:, 0:260].rearrange("p (s c) -> p s c", c=65)
                    nk = len(kts)
                    pairs = [kts[i:i + 2] for i in range(0, nk, 2)]
                    npair = len(pairs)

                    def qk(pi):
                        sb_ = Sb[cnt["s"] % 3]
                        cnt["s"] += 1
                        pt_ = PT[cnt["p"] % 4]
                        cnt["p"] += 1
                        pk = pairs[pi]

                        def f(e):
                            for x_, kt in enumerate(pk):
                                o2 = sb_.t[:, x_ * 512:x_ * 512 + nq]
                                so_ = o2 if len(q_ap.shape) == 2 else o2.rearrange("p (j q) -> p j q", q=128)
                                ins = e.matmul(so_, lhsT=kt_ap_fn(kt), rhs=q_ap, start=True, stop=True)
                            return ins
                        S.op("pe", f, reads, [sb_.r])
                        if nq == 512 or len(pk) == 1:
                            w_ = nq if len(pk) == 1 else 1024
                            S.op("act", lambda e: e.activation(out=pt_.t[:, 0:w_], in_=sb_.t[:, 0:w_], func=AF.Exp, scale=scale), [sb_.r], [pt_.r])
                        else:
                            S.op("act", lambda e: e.activation(out=pt_.t[:, :].rearrange("p (x c) -> p x c", x=2)[:, :, 0:nq],
                                                               in_=sb_.t[:, :].rearrange("p (x c) -> p x c", x=2)[:, :, 0:nq], func=AF.Exp, scale=scale),
                                 [sb_.r], [pt_.r])
                        for x_, kt in enumerate(pk):
                            m = mask_fn(kt) if mask_fn is not None else None
                            if m is not None:
                                pv_ = pt_.t[:, x_ * 512:x_ * 512 + nq].rearrange("p (s q) -> p s q", q=128)
                                S.op("dve", lambda e, pv_=pv_, m=m: e.tensor_tensor(out=pv_, in0=pv_, in1=m.unsqueeze(1).broadcast_to([128, nq // 128, 128]),
                                                                                   op=ALU.mult), [pt_.r, masks.r], [pt_.r])
                        return pt_

                    def pv(pi, pt_):
                        pk = pairs[pi]

                        def f(e):
                            for x_, kt in enumerate(pk):
                                for s_ in range(nsub):
                                    first = (pi == 0 and x_ == 0 and s_ == 0)
                                    lastm = (pi == npair - 1 and x_ == len(pk) - 1 and s_ == nsub - 1)
                                    ins = e.matmul(O3[:, s_, :], lhsT=pt_.t[:, x_ * 512 + s_ * 128:x_ * 512 + (s_ + 1) * 128], rhs=v_ap_fn(kt),
                                                   start=first, stop=lastm, skip_group_check=True)
                            return ins
                        S.op("pe", f, [pt_.r] + reads, [ob.r])
                    pts = {}
                    for i in range(min(LOOK, npair)):
                        pts[i] = qk(i)
                        yield
                    for i in range(npair):
                        if i + LOOK < npair:
                            pts[i + LOOK] = qk(i + LOOK)
                            yield
                        pv(i, pts.pop(i))
                        yield
                    if sink_ap is not None:
                        S.op("dve", lambda e: e.tensor_tensor(out=rlb.t[:, 0:nsub], in0=O3[:, 0:nsub, 64], in1=sink_ap, op=ALU.add),
                             [ob.r, esink.r], [rlb.r])
                        S.op("dve", lambda e: e.reciprocal(out=rlb.t[:, 0:nsub], in_=rlb.t[:, 0:nsub]), [rlb.r], [rlb.r])
                    else:
                        S.op("dve", lambda e: e.reciprocal(out=rlb.t[:, 0:nsub], in_=O3[:, 0:nsub, 64]), [ob.r], [rlb.r])
                    S.op("dve", lambda e: e.tensor_tensor(out=out_ap, in0=O3[:, 0:nsub, 0:64],
                                                         in1=rlb.t[:, 0:nsub].unsqueeze(2).broadcast_to([128, nsub, 64]), op=ALU.mult),
                         [ob.r, rlb.r], [osr[0]])
                    free_ob.append(obi)

                def run_units(units, width=2):
                    it = iter(units)
                    active = []

                    def refill():
                        while len(active) < width:
                            try:
                                a_, k_ = next(it)
                            except StopIteration:
                                return
                            active.append(attn_gen(*a_, **k_))
                    refill()
                    while active:
                        for g_ in list(active):
                            try:
                                next(g_)
                            except StopIteration:
                                active.remove(g_)
                                refill()

                all_k = list(range(NT))
                ctx_k = [0, 1]
                def loadA(h, i):
                    S.dma("sp", [(KT[i].t[0:96, :], KTA[h])], KT[i].r, True)
                    S.dma("sp", [(QT[i].t[0:96, 0, :], QTA[h])], QT[i].r, True)
                    S.dma("sp", [(V[i].t[:, :, 0:64], VA[:, h * 64:(h + 1) * 64].rearrange("(k p) d -> p k d", p=128))], V[i].r, True)
                loadA(0, 0)
                for h in range(8):
                    i = h % 2
                    if h + 1 < 8:
                        loadA(h + 1, (h + 1) % 2)
                    rd = [KT[i].r, QT[i].r, V[i].r]
                    ktf = (lambda i: (lambda kt: KT[i].t[0:96, kt * 128:(kt + 1) * 128]))(i)
                    vf = (lambda i: (lambda kt: V[i].t[:, kt, :]))(i)
                    units = []
                    if not last:
                        units.append(((ktf, QT[i].t[0:96, 0, 0:256], 256, vf, ctx_k, SC_A, rd, Os.t[:, 0:2, h * 64:(h + 1) * 64], 2), {}))
                    for c in range(NLT // 4):
                        q0 = 256 + c * 512
                        units.append(((ktf, QT[i].t[0:96, 0, q0:q0 + 512], 512, vf, all_k, SC_A, rd,
                                       Os.t[:, 2 + c * 4:6 + c * 4, h * 64:(h + 1) * 64], 4), {}))
                    run_units(units)
                    if h == 0:
                        precast_C(l, [Os.r])
                        if l + 1 < L:
                            precast_A(l + 1, [Os.r])
                lo = 0 if not last else 2
                S.dma("act", [(OA[lo * 128:T, :].rearrange("(k p) c -> p k c", p=128), Os.t[:, lo:NT, :])], Os.r, False)
                for mix in range(2):
                    dq, dk, dv, do = (QTB, KTB, VB, OB) if mix == 0 else (QTC, KTC, VC, OC)
                    Os = Osb[(mix + 1) % 2]
                    osr[0] = Os.r
                    KTm = KT[mix]
                    S.dma("sp", [(KTm.t[:, :], dk[:, :])], KTm.r, True)
                    if mix == 0:
                        S.op("dve", lambda e: e.memset(QT[0].t[64:128, :, :], 0.0), [], [QT[0].r])
                        S.op("pool", lambda e: e.memset(QT[1].t[0:64, :, :], 0.0), [], [QT[1].r])
                    for g in range(2):
                        S.dma("sp", [(QT[g].t[g * 64:(g + 1) * 64, :, :], dq.rearrange("j d t -> d j t")[g * 64:(g + 1) * 64])], QT[g].r, True)
                        S.dma("sp", [(V[g].t[:, :, 0:64], dv[:, g * 64:(g + 1) * 64].rearrange("(k p) d -> p k d", p=128))], V[g].r, True)
                    for g in range(2):
                        rd = [KTm.r, QT[g].r, V[g].r]
                        units = []
                        ktf = (lambda KTm: (lambda kt: KTm.t[:, kt * 128:(kt + 1) * 128]))(KTm)
                        vf = (lambda g: (lambda kt: V[g].t[:, kt, :]))(g)
                        for t in range(lo, NT):
                            if t < 2:
                                kts = ctx_k
                            elif mix == 0:
                                kts = all_k
                            else:
                                kts = ctx_k + [k for k in (t - 1, t, t + 1) if 2 <= k < NT]
                            mf = None
                            if mix == 1 and t >= 2:
                                def mf(kt, t=t):
                                    if kt < 2:
                                        return None
                                    if kt == t - 1:
                                        return masks.t[:, 0, :]
                                    if kt == t + 1:
                                        return masks.t[:, 1, :]
                                    return None
                            units.append(((ktf, QT[g].t[:, :, t * 128:(t + 1) * 128], 512, vf, kts, SC_H, rd,
                                           Os.t[:, t, g * 256:(g + 1) * 256].rearrange("p (s c) -> p s c", c=64), 4),
                                          dict(sink_ap=(esink.t[:, l, g * 4:(g + 1) * 4] if mix == 1 else None), mask_fn=mf)))
                        run_units(units)
                    S.dma("act", [(do[lo * 128:T, :].rearrange("(k p) c -> p k c", p=128), Os.t[:, lo:NT, :])], Os.r, False)
                S.barrier()

            tlo = 2 if last else 0
            with contextlib.ExitStack() as es:
                wG = mk(es, "wG", [128, 8, 3 * D], BF16)
                wbr = mk(es, "wbr", [128, 3, 4, D], BF16)
                wo = mk(es, "wo", [128, 8, D], BF16)
                wG_r = [S.newres("wG%d" % q) for q in range(4)]
                wbr_r = [S.newres("wbr%d" % q) for q in range(4)]

                def load_weights_C1():
                  for q in range(4):
                    S.dma("sp", [(wG.t[:, :, i * D + q * 256:i * D + (q + 1) * 256],
                                  wGb[l].rearrange("(j p) c -> p j c", p=128)[:, :, i * D + q * 256:i * D + (q + 1) * 256]) for i in range(3)], wG_r[q], True)
                    S.dma("sp", [(wbr.t[:, i, :, q * 256:(q + 1) * 256],
                                  wbrb[l, i].rearrange("(k p) c -> p k c", p=128)[:, :, q * 256:(q + 1) * 256]) for i in range(3)], wbr_r[q], True)
                  S.dma("sp", [(wo.t[:, :, :], wob[l].rearrange("(j p) c -> p j c", p=128))], wo.r, True)
                m2b = mk(es, "m2b", [128, 2, D], F32)
                S.dma("sp", [(m2b.t[:, r, :], mscr[l, r, 2048:3072].unsqueeze(0).partition_broadcast(128)) for r in range(2)], m2b.r, True)
                xt = [mk(es, "cxt%d" % i, [128, 2, D], F32) for i in range(3)]
                ot = [mk(es, "cot%d" % i, [128, 3, 2, 512], BF16) for i in range(2)]
                junk = mk(es, "cjunk", [128, D], F32)
                ss = mk(es, "css", [128, 1], F32)
                sd = mk(es, "csd", [128, 1], F32)
                rs = mk(es, "crs", [128, 1], F32)
                xs = mk(es, "cxs", [128, D], BF16)
                tmp = (junk, ss, sd, rs, xs)
                hT = [mk(es, "chT%d" % i, [128, 8, 256], BF16) for i in range(2)]
                h2T = mk(es, "ch2T", [128, 8, 256], BF16)
                oT = [mk(es, "coT%d" % i, [128, 3, 4, 256], BF16) for i in range(2)]
                gt = [mk(es, "cgt%d" % i, [128, 3, 256], F32) for i in range(2)]
                ya = mk(es, "cya", [128, 256], F32)
                yb = mk(es, "cyb", [128, 256], F32)
                rj = mk(es, "crj", [128, 512], F32)
                yT = [mk(es, "cyT%d" % i, [128, 8, 256], BF16) for i in range(2)]
                pT = [mkp(es, "cpT%d" % i, [128, 8, 128], BF16) for i in range(2)]
                PG = [mkp(es, "cPG%d" % i, [128, 512], F32) for i in range(2)]
                PB = [mkp(es, "cPB%d" % i, [128, 512], F32) for i in range(2)]
                PO = [mkp(es, "cPO%d" % i, [128, 512], F32) for i in range(2)]
                ptc = [0]

                def nextpT():
                    ptc[0] += 1
                    return pT[ptc[0] % 2]
                nblk = (NT - tlo) // 2

                def blk_t0(b):
                    return (tlo + 2 * b) * 128

                def loadblk(b):
                    t0 = blk_t0(b)
                    S.dma("sp", [(xt[b % 3].t[:], (xsrc[t0:t0 + 256, :]).rearrange("(s p) c -> p s c", p=128))], xt[b % 3].r, True)
                    S.dma("sp", [(ot[b % 2].t[:, i], do[t0:t0 + 256, :].rearrange("(s p) c -> p s c", p=128)) for i, do in enumerate((OA, OB, OC))],
                          ot[b % 2].r, True)

                def front(b):
                    t0 = blk_t0(b)
                    r = 1 if t0 < 256 else 0
                    xb = xt[b % 3]
                    ob_ = ot[b % 2]
                    for s in range(2):
                        yield from norm_mod_gen(xb.t[:, s, :], xb.r, tmp, hT[b % 2], s * 128, lambda j: G1.t[:, l, r, j:j + 1],
                                                lambda j: MF.t[:, l, r, j:j + 1], G1.r, nextpT(), pad=2)
                    for i in range(3):
                        for s in range(2):
                            pt_ = nextpT()

                            def tro(e, i=i, s=s, pt_=pt_):
                                for k in range(4):
                                    ins = e.transpose(pt_.t[:, k, :], ob_.t[:, i, s, k * 128:(k + 1) * 128], ident.t[:])
                                return ins
                            S.op("pe", tro, [ob_.r, ident.r], [pt_.r])
                            yield
                            S.op("act", lambda e, i=i, s=s, pt_=pt_: e.activation(out=oT[b % 2].t[:, i, :, s * 128:(s + 1) * 128], in_=pt_.t[:, 0:4, :],
                                                                                 func=AF.Copy), [pt_.r], [oT[b % 2].r])
                            yield

                def mainc(b):
                    hTb, oTb, yTb = hT[b % 2], oT[b % 2], yT[b % 2]
                    c_ = 0
                    for ft in range(8):
                        gtb = gt[ft % 2]
                        banks = []
                        for i in range(3):
                            pg_, pb_ = PG[c_ % 2], PB[c_ % 2]
                            c_ += 1
                            banks.append(pb_)

                            def mg(e, i=i, ft=ft, pg_=pg_):
                                for j in range(8):
                                    ins = e.matmul(pg_.t[:, 0:256], lhsT=wG.t[:, j, i * D + ft * 128:i * D + (ft + 1) * 128], rhs=hTb.t[:, j, :],
                                                   start=(j == 0), stop=(j == 7))
                                return ins
                            S.op("pe", mg, [wG_r[ft // 2], hTb.r], [pg_.r])
                            yield
                            S.op("act", lambda e, i=i, ft=ft, gtb=gtb, pg_=pg_: e.activation(out=gtb.t[:, i, :], in_=pg_.t[:, 0:256], func=AF.Sigmoid,
                                                                                            bias=bgT.t[:, l, i * 8 + ft:i * 8 + ft + 1]), [pg_.r, bgT.r], [gtb.r])
                            yield

                            def mb(e, i=i, ft=ft, pb_=pb_):
                                for k in range(4):
                                    ins = e.matmul(pb_.t[:, 0:256], lhsT=wbr.t[:, i, k, ft * 128:(ft + 1) * 128], rhs=oTb.t[:, i, k, :],
                                                   start=(k == 0), stop=(k == 3))
                                return ins
                            S.op("pe", mb, [wbr_r[ft // 2], oTb.r], [pb_.r])
                            yield
                            if i == 0:
                                S.op("dve", lambda e, gtb=gtb, pb_=pb_: e.tensor_tensor(out=ya.t[:], in0=pb_.t[:, 0:256], in1=gtb.t[:, 0, :], op=ALU.mult),
                                     [pb_.r, gtb.r], [ya.r])
                                yield
                            elif i == 1:
                                S.op("dve", lambda e, gtb=gtb, pb_=pb_: e.tensor_tensor(out=yb.t[:], in0=pb_.t[:, 0:256], in1=gtb.t[:, 1, :], op=ALU.mult),
                                     [pb_.r, gtb.r], [yb.r])
                                yield
                                S.op("pool", lambda e: e.tensor_tensor(out=ya.t[:], in0=ya.t[:], in1=yb.t[:], op=ALU.add), [ya.r, yb.r], [ya.r])
                                yield
                            else:
                                S.op("dve", lambda e, gtb=gtb, pb_=pb_: e.tensor_tensor(out=yb.t[:], in0=pb_.t[:, 0:256], in1=gtb.t[:, 2, :], op=ALU.mult),
                                     [pb_.r, gtb.r], [yb.r])
                                yield
                                S.op("pool", lambda e, ft=ft: e.tensor_tensor(out=yTb.t[:, ft, :], in0=ya.t[:], in1=yb.t[:], op=ALU.add), [ya.r, yb.r], [yTb.r])
                                yield

                def tail(b):
                    t0 = blk_t0(b)
                    r = 1 if t0 < 256 else 0
                    xb = xt[b % 3]
                    yTb = yT[b % 2]
                    for s in range(2):
                        for c in range(2):
                            pb_ = PO[(s * 2 + c) % 2]

                            def mo(e, s=s, c=c, pb_=pb_):
                                for j in range(8):
                                    ins = e.matmul(pb_.t[:, :], lhsT=yTb.t[:, j, s * 128:(s + 1) * 128], rhs=wo.t[:, j, c * 512:(c + 1) * 512],
                                                   start=(j == 0), stop=(j == 7))
                                return ins
                            S.op("pe", mo, [yTb.r, wo.r], [pb_.r])
                            yield
                            yield
                            S.op("dve", lambda e, c=c, pb_=pb_: e.tensor_tensor(out=rj.t[:, 0:512], in0=pb_.t[:, :], in1=m2b.t[:, r, c * 512:(c + 1) * 512],
                                                                                op=ALU.mult), [pb_.r, m2b.r], [rj.r])
                            yield
                            S.op("pool", lambda e, s=s, c=c: e.tensor_tensor(out=xb.t[:, s, c * 512:(c + 1) * 512], in0=xb.t[:, s, c * 512:(c + 1) * 512],
                                                                             in1=rj.t[:, 0:512], op=ALU.add), [xb.r, rj.r], [xb.r])
                            yield
                    S.dma("sp", [(x1s[t0:t0 + 256, :].rearrange("(s p) c -> p s c", p=128), xb.t[:])], xb.r, False)
                    for s in range(2):
                        yield from norm_mod_gen(xb.t[:, s, :], xb.r, tmp, h2T, s * 128, lambda j: G2.t[:, l, r, j:j + 1],
                                                lambda j: MF.t[:, l, r, 24 + j:25 + j], G2.r, nextpT(), pad=2)
                    S.dma("sp", [(H2T.rearrange("j p t -> p j t")[:, :, t0:t0 + 256], h2T.t[:])], h2T.r, False)
                    yield

                def chain(*gs):
                    for g_ in gs:
                        yield from g_

                def rr(gens):
                    gens = list(gens)
                    while gens:
                        for g_ in list(gens):
                            try:
                                next(g_)
                            except StopIteration:
                                gens.remove(g_)

                loadblk(0)
                load_weights_C1()
                rr([front(0)])
                if nblk > 1:
                    loadblk(1)
                for b in range(nblk):
                    side = []
                    if b >= 1:
                        side.append(tail(b - 1))
                    if b + 1 < nblk:
                        side.append(front(b + 1))
                    rr([mainc(b), chain(*side)])
                    if b + 2 < nblk:
                        loadblk(b + 2)
                rr([tail(nblk - 1)])
                S.barrier()

            with contextlib.ExitStack() as es:
                wu = mk(es, "wu", [128, 8, 2 * DFF], BF16)
                wd = mk(es, "wd", [128, 22, D], BF16)
                wu_r = [S.newres("wu%d" % q) for q in range(4)]
                wd_r = [S.newres("wd%d" % q) for q in range(2)]

                def load_weights_C2():
                  for q in range(4):
                    f0, f1 = q * 6 * 128, min(22, (q + 1) * 6) * 128
                    S.dma("sp", [(wu.t[:, :, h_ * DFF + f0:h_ * DFF + f1], wub[l].rearrange("(j p) c -> p j c", p=128)[:, :, h_ * DFF + f0:h_ * DFF + f1])
                                 for h_ in range(2)], wu_r[q], True)
                  for q in range(2):
                    S.dma("sp", [(wd.t[:, q * 11:(q + 1) * 11, :], wdb[l, q * 11 * 128:(q + 1) * 11 * 128, :].rearrange("(j p) c -> p j c", p=128))], wd_r[q], True)
                m5b = mk(es, "m5b", [128, 2, D], F32)
                S.dma("sp", [(m5b.t[:, r, :], mscr[l, r, 5120:6144].unsqueeze(0).partition_broadcast(128)) for r in range(2)], m5b.r, True)
                hb = [mk(es, "fh%d" % i, [128, 8, 256], BF16) for i in range(2)]
                cab = [mk(es, "fca%d" % i, [128, 256], F32) for i in range(2)]
                cgb = [mk(es, "fcg%d" % i, [128, 256], F32) for i in range(2)]
                sab = [mk(es, "fsa%d" % i, [128, 256], F32) for i in range(2)]
                aT = mk(es, "faT", [128, 22, 256], BF16)
                x1 = [mk(es, "fx1%d" % i, [128, D], F32) for i in range(2)]
                xo = [mk(es, "fxo%d" % i, [128, D], F32) for i in range(2)]
                fj = mk(es, "fj", [128, D], F32)
                fss = mk(es, "fss", [128, 1], F32)
                fsd = mk(es, "fsd", [128, 1], F32)
                frs = mk(es, "frs", [128, 1], F32)
                PU = [mkp(es, "fPU%d" % i, [128, 512], F32) for i in range(6)]
                PD = [mkp(es, "fPD%d" % i, [128, 512], F32) for i in range(2)]
                seqs = ([] if last else [(0, CTX, 1)]) + [(CTX, SEQ, 0)]
                blks = []
                for (s0, sl, r) in seqs:
                    o = 0
                    while o < sl:
                        n = min(254, sl - o)
                        blks.append((s0, sl, r, o, n))
                        o += n

                def loadh(bi):
                    s0, sl, r, o, n = blks[bi]
                    hbb = hb[bi % 2]
                    lo_ = max(o - 1, 0)
                    hi_ = min(o + n + 1, sl)
                    c0 = lo_ - (o - 1)
                    if o == 0:
                        S.op("dve", lambda e: e.memset(hbb.t[:, :, 0:1], 0.0), [], [hbb.r])
                    if o + n == sl:
                        S.op("dve", lambda e: e.memset(hbb.t[:, :, n + 1:n + 2], 0.0), [], [hbb.r])
                    S.dma("sp", [(hbb.t[:, :, c0:c0 + hi_ - lo_], H2T.rearrange("j p t -> p j t")[:, :, s0 + lo_:s0 + hi_])], hbb.r, True)
                loadh(0)
                load_weights_C2()
                xc = 0
                for bi, (s0, sl, r, o, n) in enumerate(blks):
                    hbb = hb[bi % 2]
                    if bi + 1 < len(blks):
                        loadh(bi + 1)
                    N = n + 2
                    for ft in range(22):
                        ca, cg, sa = cab[ft % 2], cgb[ft % 2], sab[ft % 2]
                        pa = PU[(2 * ft) % 6]
                        pg = PU[(2 * ft + 1) % 6]
                        for (pp, col) in ((pa, ft * 128), (pg, DFF + ft * 128)):
                            def mu(e, pp=pp, col=col):
                                for j in range(8):
                                    ins = e.matmul(pp.t[:, 0:N], lhsT=wu.t[:, j, col:col + 128], rhs=hbb.t[:, j, 0:N], start=(j == 0), stop=(j == 7))
                                return ins
                            S.op("pe", mu, [wu_r[ft // 6], hbb.r], [pp.r])
                        for (pp, dst, fi) in ((pa, ca, ft), (pg, cg, 22 + ft)):
                            S.op("act", lambda e, pp=pp, dst=dst, fi=fi: e.activation(out=dst.t[:, 0:n], in_=pp.t[:, 1:n + 1], func=AF.Identity,
                                                                                     scale=cvw.t[:, l, 1, fi:fi + 1], bias=cvw.t[:, l, 3, fi:fi + 1]),
                                 [pp.r, cvw.r], [dst.r])
                            S.op("dve", lambda e, pp=pp, dst=dst, fi=fi: e.scalar_tensor_tensor(out=dst.t[:, 0:n], in0=pp.t[:, 0:n], scalar=cvw.t[:, l, 0, fi:fi + 1],
                                                                                                in1=dst.t[:, 0:n], op0=ALU.mult, op1=ALU.add),
                                 [pp.r, cvw.r, dst.r], [dst.r])
                            S.op("dve", lambda e, pp=pp, dst=dst, fi=fi: e.scalar_tensor_tensor(out=dst.t[:, 0:n], in0=pp.t[:, 2:n + 2], scalar=cvw.t[:, l, 2, fi:fi + 1],
                                                                                                in1=dst.t[:, 0:n], op0=ALU.mult, op1=ALU.add),
                                 [pp.r, cvw.r, dst.r], [dst.r])
                        S.op("act", lambda e: e.activation(out=sa.t[:, 0:n], in_=ca.t[:, 0:n], func=AF.Silu), [ca.r], [sa.r])
                        S.op("pool", lambda e, ft=ft, sa=sa, cg=cg: e.tensor_tensor(out=aT.t[:, ft, 0:n], in0=sa.t[:, 0:n], in1=cg.t[:, 0:n], op=ALU.mult), [sa.r, cg.r], [aT.r])
                    q = 0
                    while q < n:
                        m = min(128, n - q)
                        tok0 = s0 + o + q
                        x1b = x1[xc % 2]
                        xob = xo[xc % 2]
                        S.dma("sp", [(x1b.t[0:m, :], x1s[tok0:tok0 + m, :])], x1b.r, True)
                        for c in range(2):
                            pb_ = PD[c]

                            def md(e, c=c, pb_=pb_, q=q, m=m):
                                for j in range(22):
                                    ins = e.matmul(pb_.t[0:m, :], lhsT=aT.t[:, j, q:q + m], rhs=wd.t[:, j, c * 512:(c + 1) * 512], start=(j == 0), stop=(j == 21))
                                return ins
                            S.op("pe", md, [aT.r, wd_r[0], wd_r[1]], [pb_.r])
                            S.op("dve", lambda e, c=c, pb_=pb_, m=m: e.tensor_tensor(out=fj.t[0:m, 0:512], in0=pb_.t[0:m, :], in1=m5b.t[0:m, r, c * 512:(c + 1) * 512],
                                                                                     op=ALU.mult), [pb_.r, m5b.r], [fj.r])
                            S.op("dve", lambda e, c=c, m=m, x1b=x1b, xob=xob: e.tensor_tensor(out=xob.t[0:m, c * 512:(c + 1) * 512], in0=x1b.t[0:m, c * 512:(c + 1) * 512],
                                                                                              in1=fj.t[0:m, 0:512], op=ALU.add), [x1b.r, fj.r], [xob.r])
                        if not last:
                            S.dma("pool", [(xres[tok0:tok0 + m, :], xob.t[0:m, :])], xob.r, False)
                        else:
                            S.op("act", lambda e, m=m, xob=xob: e.activation(out=fj.t[0:m, :], in_=xob.t[0:m, :], func=AF.Square, accum_out=fss.t[0:m, :]),
                                 [xob.r], [fj.r, fss.r])
                            S.op("dve", lambda e, m=m: e.tensor_scalar(out=fss.t[0:m, :], in0=fss.t[0:m, :], scalar1=1.0 / D, scalar2=EPS, op0=ALU.mult, op1=ALU.add),
                                 [fss.r], [fss.r])
                            S.op("act", lambda e, m=m: e.activation(out=fsd.t[0:m, :], in_=fss.t[0:m, :], func=AF.Sqrt), [fss.r], [fsd.r])
                            S.op("dve", lambda e, m=m: e.reciprocal(out=frs.t[0:m, :], in_=fsd.t[0:m, :]), [fsd.r], [frs.r])
                            S.op("dve", lambda e, m=m, xob=xob: e.scalar_tensor_tensor(out=xob.t[0:m, :], in0=xob.t[0:m, :], scalar=frs.t[0:m, 0:1], in1=gfb.t[0:m, :],
                                                                                       op0=ALU.mult, op1=ALU.mult), [xob.r, frs.r, gfb.r], [xob.r])
                            S.dma("pool", [(out_d[tok0 - CTX:tok0 - CTX + m, :], xob.t[0:m, :])], xob.r, False)
                        xc += 1
                        q += m
                S.barrier()
    return nc


def _host_consts(SEQ):
    T = CTX + SEQ
    rows = SEQ // 64

    def table(hd):
        nf = hd // 4
        row = np.repeat(np.arange(rows, dtype=np.float32), 64)
        col = np.tile(np.arange(64, dtype=np.float32), rows)
        inv = (np.float32(10000.0) ** (-np.arange(nf, dtype=np.float32) / np.float32(nf))).astype(np.float32)
        ang = np.stack([row[:, None] * inv, col[:, None] * inv], axis=1).astype(np.float32)
        cos = np.cos(ang).astype(np.float32)
        sin = np.sin(ang).astype(np.float32)
        cf = np.ones((T, 2, 2, nf), np.float32)
        sf = np.zeros((T, 2, 2, nf), np.float32)
        cf[CTX:, :, 0, :] = cos
        cf[CTX:, :, 1, :] = cos
        sf[CTX:, :, 0, :] = -sin
        sf[CTX:, :, 1, :] = sin
        return np.stack([cf.reshape(T, hd), sf.reshape(T, hd)], axis=1)
    p = np.arange(128)[:, None]
    f = np.arange(128)[None, :]
    masks = np.stack([(f <= p), (p <= f)]).astype(np.float32).astype(ml_dtypes.bfloat16)
    return dict(ropeH=np.ascontiguousarray(table(64)), ropeA=np.ascontiguousarray(table(32)),
                ident=np.eye(128, dtype=np.float32).astype(ml_dtypes.bfloat16), masks=masks)


_NC_CACHE = {}


def _prep_inputs(inp, SEQ, nb):
    f = lambda a: np.ascontiguousarray(np.asarray(a, dtype=np.float32))
    w_in = f(inp["w_in"])
    perm = np.array([(g * 4 + j) * 64 + d for j in range(4) for g in range(2) for d in range(64)])
    offs = np.cumsum([0, 256, 128, 32, 512, 128, 128, 512, 128, 128])
    colsA = np.concatenate([np.arange(0, 416), offs[3] + perm, np.arange(offs[4], offs[6]), offs[6] + perm, np.arange(offs[7], offs[9])])
    assert colsA.size == NA
    w_inA = np.ascontiguousarray(w_in[:, :, colsA])
    w_inG = np.ascontiguousarray(w_in[:, :, offs[9]:])
    gq = f(inp["g_qn"])
    gk = f(inp["g_kn"])

    def sw(g):
        return g.reshape(L, 2, 2, 16)[:, :, ::-1, :].reshape(L, 64)
    g_qk = np.ascontiguousarray(np.stack([gq, sw(gq), gk, sw(gk)], axis=1))
    shared = dict(
        w_mod=f(inp["w_mod"]), b_mod=f(inp["b_mod"]), g_norm1=f(inp["g_norm1"]), g_norm2=f(inp["g_norm2"]),
        w_inA=w_inA, w_inG=w_inG, b_gate=f(inp["b_gate"]), g_q_a=f(inp["g_q_a"]), w_q_b=f(inp["w_q_b"]),
        g_kv_a=f(inp["g_kv_a"]), w_kv_b=f(inp["w_kv_b"]), g_qk=g_qk, sink=f(inp["sink"]),
        w_branch=f(inp["w_branch"]), w_out=f(inp["w_out"]), w_up=f(inp["w_up"]), w_conv=f(inp["w_conv"]),
        b_conv=f(inp["b_conv"]), w_down=f(inp["w_down"]), g_final=f(inp["g_final"]).reshape(1, D))
    shared.update(_host_consts(SEQ))
    x = f(inp["x"])
    ctx = f(inp["ctx"])
    c = f(inp["c"])
    c_ctx = f(inp["c_ctx"])
    maps = []
    for b in range(nb):
        m = dict(shared)
        m["xin"] = np.ascontiguousarray(np.concatenate([ctx[b], x[b]], axis=0))
        m["cvec"] = np.ascontiguousarray(np.stack([c[b], c_ctx], axis=0))
        maps.append(m)
    return maps


def kernel(**inputs):
    x = np.asarray(inputs["x"])
    nb, SEQ = x.shape[0], x.shape[1]
    if SEQ not in _NC_CACHE:
        _NC_CACHE[SEQ] = build_nc(SEQ)
    nc = _NC_CACHE[SEQ]
    maps = _prep_inputs(inputs, SEQ, nb)
    res = run_bass_kernel_spmd(nc, maps, core_ids=list(range(nb)))
    return np.stack([np.asarray(r["out"], dtype=np.float32) for r in res.results], axis=0)
```

```python
import contextlib
import math
import numpy as np
import ml_dtypes
import concourse.bass as bass
import concourse.mybir as mybir
from concourse.bass_utils import run_bass_kernel_spmd
from concourse.alu_op_type import AluOpType as ALU

F32 = mybir.dt.float32
BF16 = mybir.dt.bfloat16
AF = mybir.ActivationFunctionType
AX = mybir.AxisListType

D = 1024
CTX = 256
L = 2
EPS = 1e-6
DFF = 2816
NA = 1952
SC_A = 1.0 / math.sqrt(96.0)
SC_H = 1.0 / 8.0


class Res:
    __slots__ = ("name", "w", "r", "dsem")

    def __init__(self, name):
        self.name = name
        self.w = None
        self.r = {}
        self.dsem = None


class SemObj:
    __slots__ = ("h", "cnt")

    def __init__(self, h):
        self.h = h
        self.cnt = 0


class TT:
    __slots__ = ("t", "r")

    def __init__(self, t, r):
        self.t = t
        self.r = r


class Sched:
    def __init__(self, nc, es, nsem=90):
        self.nc = nc
        self.E = {"pe": nc.tensor, "act": nc.scalar, "dve": nc.vector, "pool": nc.gpsimd, "sp": nc.sync}
        self.pool = [SemObj(es.enter_context(nc.semaphore("sm%d" % i))) for i in range(nsem)]
        self.free = list(self.pool)
        self.esem = {k: self.free.pop() for k in self.E}
        self.waited = {k: {} for k in self.E}
        self.res = []
        self.dma_sems = []
        self.held = []
        self.dma_free = {"sp": [], "pool": [], "act": []}

    def newres(self, name):
        r = Res(name)
        self.res.append(r)
        return r

    def _wait(self, eng, stamps):
        need = {}
        for st in stamps:
            if st is None:
                continue
            so, c, ek = st
            k = id(so)
            if k not in need or need[k][1] < c:
                need[k] = st
        w = self.waited[eng]
        for k, (so, c, ek) in need.items():
            if w.get(k, 0) < c:
                self.E[eng].wait_ge(so.h, c)
                w[k] = c

    def _hazards(self, eng, reads, writes):
        st = []
        for R in reads:
            if R.w is not None:
                if not (R.w[2] == eng and eng == "pe"):
                    st.append(R.w)
        for R in writes:
            if R.w is not None and R.w[2] != eng:
                st.append(R.w)
            for s in R.r.values():
                if s[2] != eng:
                    st.append(s)
        return st

    def op(self, eng, fn, reads=(), writes=()):
        self._wait(eng, self._hazards(eng, reads, writes))
        inst = fn(self.E[eng])
        so = self.esem[eng]
        so.cnt += 1
        inst.then_inc(so.h, 1)
        stamp = (so, so.cnt, eng)
        for R in reads:
            R.r[id(so)] = stamp
        for R in writes:
            R.w = stamp
            R.r = {}
        return inst

    def dma(self, eng, pairs, sb, load, extra_reads=(), kw=None):
        if load:
            hz = self._hazards("dma", list(extra_reads), [sb])
        else:
            hz = self._hazards("dma", [sb] + list(extra_reads), [])
        self._wait(eng, hz)
        if sb.dsem is None:
            sb.dsem = {}
        if eng not in sb.dsem:
            if self.dma_free[eng]:
                sb.dsem[eng] = self.dma_free[eng].pop()
            else:
                sb.dsem[eng] = self.free.pop()
                self.dma_sems.append(sb.dsem[eng])
            self.held.append((eng, sb.dsem[eng]))
        so = sb.dsem[eng]
        for (o, i) in pairs:
            self.E[eng].dma_start(out=o, in_=i, **(kw or {})).then_inc(so.h, 16)
            so.cnt += 16
        stamp = (so, so.cnt, "dma")
        if load:
            sb.w = stamp
            sb.r = {}
        else:
            sb.r[id(so)] = stamp

    def barrier(self):
        stamps = []
        for k, so in self.esem.items():
            if so.cnt > 0:
                stamps.append((so, so.cnt, k))
        for so in self.dma_sems:
            if so.cnt > 0:
                stamps.append((so, so.cnt, "dma"))
        for eng in self.E:
            self._wait(eng, [s for s in stamps if s[2] != eng or eng in ("act", "dve", "pool")])
        for R in self.res:
            R.w = None
            R.r = {}
            R.dsem = None
        for (eng_, so_) in self.held:
            self.dma_free[eng_].append(so_)
        self.held = []
        self.res = []
        for k in ("pe", "act", "dve"):
            if self.esem[k].cnt > 14000:
                self.esem[k] = self.free.pop()


def build_nc(SEQ, dbg=False):
    T = CTX + SEQ
    NT = T // 128
    NLT = SEQ // 128
    nc = bass.Bass("TRN2", target_bir_lowering=False)

    def din(name, shape, dt=F32):
        return nc.dram_tensor(name, list(shape), dt, kind="ExternalInput").ap()

    def dscr(name, shape, dt=BF16):
        return nc.dram_tensor(name, list(shape), dt, kind=("ExternalOutput" if dbg else "Internal")).ap()

    xin = din("xin", [T, D])
    cvec = din("cvec", [2, D])
    w_mod = din("w_mod", [L, D, 6 * D])
    b_mod = din("b_mod", [L, 6 * D])
    g_norm1 = din("g_norm1", [L, D])
    g_norm2 = din("g_norm2", [L, D])
    w_inA = din("w_inA", [L, D, NA])
    w_inG = din("w_inG", [L, D, 3 * D])
    b_gate = din("b_gate", [L, 3 * D])
    g_q_a = din("g_q_a", [L, 256])
    w_q_b = din("w_q_b", [L, 256, 768])
    g_kv_a = din("g_kv_a", [L, 128])
    w_kv_b = din("w_kv_b", [L, 128, 1024])
    g_qk = din("g_qk", [L, 4, 64])
    sink = din("sink", [L, 8])
    w_branch = din("w_branch", [L, 3, 512, D])
    w_out = din("w_out", [L, D, D])
    w_up = din("w_up", [L, D, 2 * DFF])
    w_conv = din("w_conv", [L, 3, 2 * DFF])
    b_conv = din("b_conv", [L, 2 * DFF])
    w_down = din("w_down", [L, DFF, D])
    g_final = din("g_final", [1, D])
    ident_d = din("ident", [128, 128], BF16)
    masks_d = din("masks", [2, 128, 128], BF16)
    ropeH = din("ropeH", [T, 2, 64])
    ropeA = din("ropeA", [T, 2, 32])
    out_d = nc.dram_tensor("out", [SEQ, D], F32, kind="ExternalOutput").ap()

    mscr = dscr("mscr", [L, 2, 6 * D], F32)
    xres = dscr("xres", [T, D], F32)
    x1s = dscr("x1s", [T, D], F32)
    QTA = dscr("QTA", [8, 96, T])
    KTA = dscr("KTA", [8, 96, T])
    VA = dscr("VA", [T, 512])
    QTB = dscr("QTB", [4, 128, T])
    KTB = dscr("KTB", [128, T])
    VB = dscr("VB", [T, 128])
    QTC = dscr("QTC", [4, 128, T])
    KTC = dscr("KTC", [128, T])
    VC = dscr("VC", [T, 128])
    OA = dscr("OA", [T, 512])
    OB = dscr("OB", [T, 512])
    OC = dscr("OC", [T, 512])
    H2T = dscr("H2T", [8, 128, T])
    wAb = dscr("wAb", [L, D, NA])
    wqbb = dscr("wqbb", [L, 256, 768])
    wkvbb = dscr("wkvbb", [L, 128, 1024])
    wGb = dscr("wGb", [L, D, 3 * D])
    wbrb = dscr("wbrb", [L, 3, 512, D])
    wob = dscr("wob", [L, D, D])
    wub = dscr("wub", [L, D, 2 * DFF])
    wdb = dscr("wdb", [L, DFF, D])

    with contextlib.ExitStack() as es0:
        S = Sched(nc, es0)

        uid = [0]

        def mk(es, name, shape, dt):
            uid[0] += 1
            t = es.enter_context(nc.sbuf_tensor("sb_%s_%d" % (name, uid[0]), list(shape), dt))
            return TT(t, S.newres(name))

        def mkp(es, name, shape, dt):
            uid[0] += 1
            t = es.enter_context(nc.psum_tensor("ps_%s_%d" % (name, uid[0]), list(shape), dt))
            return TT(t, S.newres(name))

        def precast(pairs, name, after=()):
            S.dma("pool", pairs, S.newres(name), True, extra_reads=after)

        def rows(ap2, n):
            return [ap2[j * 128:(j + 1) * 128] for j in range(n)]

        def precast_A(l_, after=()):
            prs = []
            for (c0, cw) in ((0, 976), (976, 976)):
                prs += [(o[:, c0:c0 + cw], i[:, c0:c0 + cw]) for o, i in zip(rows(wAb[l_], 8), rows(w_inA[l_], 8))]
            prs += list(zip(rows(wqbb[l_], 2), rows(w_q_b[l_], 2)))
            prs += [(wkvbb[l_], w_kv_b[l_])]
            precast(prs, "pcA%d" % l_, after)

        def precast_piece(l_, piece, after=()):
            prs = []
            if piece in (0, 1):
                c = piece
                prs += [(o[:, c * 1536:(c + 1) * 1536], i[:, c * 1536:(c + 1) * 1536]) for o, i in zip(rows(wGb[l_], 8), rows(w_inG[l_], 8))]
            elif piece == 2:
                for i_ in range(3):
                    prs += list(zip(rows(wbrb[l_, i_], 4), rows(w_branch[l_, i_], 4)))
                prs += list(zip(rows(wob[l_], 8), rows(w_out[l_], 8)))
            elif piece in (3, 4):
                for c in (2 * (piece - 3), 2 * (piece - 3) + 1):
                    prs += [(o[:, c * 1408:(c + 1) * 1408], i[:, c * 1408:(c + 1) * 1408]) for o, i in zip(rows(wub[l_], 8), rows(w_up[l_], 8))]
            elif piece == 5:
                prs += list(zip(rows(wdb[l_], 22), rows(w_down[l_], 22)))
            elif piece == 6:
                if l_ + 1 < L:
                    precast_A(l_ + 1, after)
                return
            precast(prs, "pc%d_%d" % (l_, piece), after)

        precast_A(0)

        ident = mk(es0, "ident", [128, 128], BF16)
        masks = mk(es0, "masks", [128, 2, 128], BF16)
        MF = mk(es0, "MF", [128, L, 2, 48], F32)
        G1 = mk(es0, "G1", [128, L, 2, 8], F32)
        G2 = mk(es0, "G2", [128, L, 2, 8], F32)
        gn = mk(es0, "gn", [128, 2, L, 8], F32)
        bgT = mk(es0, "bgT", [128, L, 24], F32)
        cvw = mk(es0, "cvw", [128, L, 4, 44], F32)
        gqa = mk(es0, "gqa", [128, L, 2], F32)
        gkva = mk(es0, "gkva", [128, L, 1], F32)
        gqkb = mk(es0, "gqkb", [128, L, 4, 64], F32)
        esink = mk(es0, "esink", [128, L, 8], F32)
        gfb = mk(es0, "gfb", [128, D], F32)
        invn = mk(es0, "invn", [128, 12], F32)
        epsb = mk(es0, "epsb", [128, 1], F32)

        S.dma("sp", [(ident.t[:], ident_d[:, :])], ident.r, True)
        S.dma("sp", [(masks.t[:], masks_d.rearrange("m p f -> p m f"))], masks.r, True)
        es0.enter_context(nc.allow_non_contiguous_dma(reason="strided parameter / staging DMAs"))
        fm = lambda v: v.rearrange("(j p) -> p j", p=128)
        S.dma("sp", [(gn.t[:, 0, l_, :], fm(g_norm1[l_])) for l_ in range(L)] + [(gn.t[:, 1, l_, :], fm(g_norm2[l_])) for l_ in range(L)], gn.r, True)
        S.dma("sp", [(bgT.t[:, l_, :], fm(b_gate[l_])) for l_ in range(L)], bgT.r, True)
        S.dma("sp", [(cvw.t[:, l_, c_, :], fm(w_conv[l_, c_])) for l_ in range(L) for c_ in range(3)]
              + [(cvw.t[:, l_, 3, :], fm(b_conv[l_])) for l_ in range(L)], cvw.r, True)
        S.dma("sp", [(gqa.t[:, l_, :], fm(g_q_a[l_])) for l_ in range(L)], gqa.r, True)
        S.dma("sp", [(gkva.t[:, l_, :], fm(g_kv_a[l_])) for l_ in range(L)], gkva.r, True)
        S.dma("sp", [(gqkb.t[:].rearrange("p l a d -> p (l a d)"),
                      g_qk.rearrange("l a d -> (l a d)").unsqueeze(0).partition_broadcast(128))], gqkb.r, True)
        S.dma("sp", [(esink.t[:].rearrange("p l h -> p (l h)"),
                      sink.rearrange("l h -> (l h)").unsqueeze(0).partition_broadcast(128))], esink.r, True)
        S.dma("sp", [(gfb.t[:], g_final.partition_broadcast(128))], gfb.r, True)
        S.op("act", lambda e: e.activation(out=esink.t[:], in_=esink.t[:], func=AF.Exp), [esink.r], [esink.r])
        S.op("dve", lambda e: e.memset(invn.t[:, 0:1], 1.0 / 256), [], [invn.r])
        S.op("dve", lambda e: e.memset(invn.t[:, 1:2], 1.0 / 128), [], [invn.r])
        S.op("dve", lambda e: e.memset(invn.t[:, 2:12], 1.0 / 64), [], [invn.r])
        S.op("dve", lambda e: e.memset(epsb.t[:], EPS), [], [epsb.r])

        with contextlib.ExitStack() as es:
            cT = mk(es, "cT", [128, 8, 2], F32)
            LT = mk(es, "LT", [128, 8, 33], F32)
            wm = [mk(es, "wm%d" % i, [128, 8, 512], F32) for i in range(2)]
            bm = mk(es, "bm", [33, L, 6 * D], F32)
            mrow = mk(es, "mrow", [33, L, 6 * D], F32)
            pm = [mkp(es, "pm%d" % i, [128, 512], F32) for i in range(2)]
            S.dma("sp", [(cT.t[:, :, r_], fm(cvec[r_])) for r_ in range(2)], cT.r, True)
            S.dma("sp", [(bm.t[p_:p_ + 1, l_, :], b_mod[l_].unsqueeze(0)) for p_ in (0, 32) for l_ in range(L)], bm.r, True)
            S.op("dve", lambda e: e.memset(LT.t[:], 0.0), [], [LT.r])
            S.op("act", lambda e: e.activation(out=LT.t[:, :, 0:1], in_=cT.t[:, :, 0:1], func=AF.Silu), [cT.r, LT.r], [LT.r])
            S.op("act", lambda e: e.activation(out=LT.t[:, :, 32:33], in_=cT.t[:, :, 1:2], func=AF.Silu), [cT.r, LT.r], [LT.r])
            i = 0
            for l in range(L):
                for cc in range(12):
                    wb_ = wm[i % 2]
                    pb_ = pm[i % 2]
                    S.dma("sp", [(wb_.t[:], w_mod[l].rearrange("(j p) c -> p j c", p=128)[:, :, cc * 512:(cc + 1) * 512])], wb_.r, True)

                    def mm(e, wb_=wb_, pb_=pb_):
                        for j in range(8):
                            ins = e.matmul(pb_.t[0:33, :], lhsT=LT.t[:, j, :], rhs=wb_.t[:, j, :], start=(j == 0), stop=(j == 7))
                        return ins
                    S.op("pe", mm, [LT.r, wb_.r], [pb_.r])
                    for r0 in (0, 32):
                        S.op("dve", lambda e, r0=r0, pb_=pb_, l=l, cc=cc: e.tensor_tensor(
                            out=mrow.t[r0:r0 + 1, l, cc * 512:(cc + 1) * 512], in0=pb_.t[r0:r0 + 1, :],
                            in1=bm.t[r0:r0 + 1, l, cc * 512:(cc + 1) * 512], op=ALU.add), [pb_.r, bm.r], [mrow.r])
                    i += 1
            S.dma("sp", [(mscr[l_, r_, :].unsqueeze(0), mrow.t[32 * r_:32 * r_ + 1, l_, :]) for l_ in range(L) for r_ in range(2)], mrow.r, False)
            S.barrier()
        S.dma("sp", [(MF.t[:, l, r, :], mscr[l, r].rearrange("(j p) -> p j", p=128)) for l in range(L) for r in range(2)], MF.r, True)
        for l in range(L):
            for r in range(2):
                S.op("dve", lambda e, l=l, r=r: e.scalar_tensor_tensor(
                    out=G1.t[:, l, r, :], in0=MF.t[:, l, r, 8:16], scalar=1.0, in1=gn.t[:, 0, l, :], op0=ALU.add, op1=ALU.mult),
                    [MF.r, gn.r], [G1.r])
                S.op("dve", lambda e, l=l, r=r: e.scalar_tensor_tensor(
                    out=G2.t[:, l, r, :], in0=MF.t[:, l, r, 32:40], scalar=1.0, in1=gn.t[:, 1, l, :], op0=ALU.add, op1=ALU.mult),
                    [MF.r, gn.r], [G2.r])

        def norm_mod_gen(xt_ap, xr, tmp, hT, col0, Gt, shift_ap_fn, gsel, pT, pad=0):
            junk, ss, sd, rs, xs = tmp
            S.op("act", lambda e: e.activation(out=junk.t[:], in_=xt_ap, func=AF.Square, accum_out=ss.t[:]), [xr], [junk.r, ss.r])
            yield
            for _p in range(pad):
                yield
            S.op("dve", lambda e: e.tensor_scalar(out=ss.t[:], in0=ss.t[:], scalar1=1.0 / D, scalar2=EPS, op0=ALU.mult, op1=ALU.add),
                 [ss.r], [ss.r])
            yield
            for _p in range(pad):
                yield
            S.op("act", lambda e: e.activation(out=sd.t[:], in_=ss.t[:], func=AF.Sqrt), [ss.r], [sd.r])
            yield
            for _p in range(pad):
                yield
            S.op("dve", lambda e: e.reciprocal(out=rs.t[:], in_=sd.t[:]), [sd.r], [rs.r])
            yield
            for _p in range(pad):
                yield
            S.op("act", lambda e: e.activation(out=xs.t[:], in_=xt_ap, func=AF.Copy, scale=rs.t[:, 0:1]), [xr, rs.r], [xs.r])
            yield
            for _p in range(pad):
                yield

            def tr(e):
                for j in range(8):
                    ins = e.transpose(pT.t[:, j, :], xs.t[:, j * 128:(j + 1) * 128], ident.t[:])
                return ins
            S.op("pe", tr, [xs.r, ident.r], [pT.r])
            yield
            for j in range(8):
                S.op("dve", lambda e, j=j: e.tensor_scalar(out=hT.t[:, j, col0:col0 + 128], in0=pT.t[:, j, :],
                                                          scalar1=Gt(j), scalar2=shift_ap_fn(j), op0=ALU.mult, op1=ALU.add),
                     [pT.r, gsel, MF.r], [hT.r])
                yield

        def norm_mod_T(*a_, **k_):
            for _ in norm_mod_gen(*a_, **k_):
                pass

        for l in range(L):
            last = (l == L - 1)
            xsrc = xin if l == 0 else xres
            with contextlib.ExitStack() as es:
                wA = mk(es, "wA", [128, 8, NA], BF16)
                wqb = mk(es, "wqb", [128, 2, 768], BF16)
                wkvb = mk(es, "wkvb", [128, 1024], BF16)
                ACH = ((0, 416), (416, 512), (928, 512), (1440, 512))
                wA_r = [S.newres("wA%d" % c) for c in range(4)]

                def load_weights_A():
                    for c, (c0, cw) in enumerate(ACH):
                        S.dma("sp", [(wA.t[:, :, c0:c0 + cw], wAb[l].rearrange("(j p) c -> p j c", p=128)[:, :, c0:c0 + cw])], wA_r[c], True)
                    S.dma("sp", [(wqb.t[:, :, :], wqbb[l].rearrange("(j p) c -> p j c", p=128))], wqb.r, True)
                    S.dma("sp", [(wkvb.t[:], wkvbb[l])], wkvb.r, True)
                xt = [mk(es, "xt%d" % i, [128, D], F32) for i in range(2)]
                rp = [mk(es, "rp%d" % i, [128, 2, 96], F32) for i in range(3)]
                Pc = [mk(es, "Pc%d" % i, [128, NA], F32) for i in range(2)]
                junk = mk(es, "junk", [128, D], F32)
                ss = mk(es, "ss", [128, 1], F32)
                sd = mk(es, "sd", [128, 1], F32)
                rs = mk(es, "rs", [128, 1], F32)
                xs = mk(es, "xs", [128, D], BF16)
                hT = [mk(es, "hT%d" % i, [128, 8, 128], BF16) for i in range(2)]
                ss12b = [mk(es, "ss12%d" % i, [128, 12], F32) for i in range(2)]
                sd12b = [mk(es, "sd12%d" % i, [128, 12], F32) for i in range(2)]
                rs12b = [mk(es, "rs12%d" % i, [128, 12], F32) for i in range(2)]
                sqtb = [mk(es, "sqt%d" % i, [128, 640], F32) for i in range(2)]
                tbb = [mk(es, "tb%d" % i, [128, 4, 64], F32) for i in range(2)]
                u = mk(es, "u", [128, 768], F32)
                t1a = mk(es, "t1a", [128, 256], F32)
                t2a = mk(es, "t2a", [128, 256], F32)
                t2k = mk(es, "t2k", [128, 32], F32)
                mt = [(mk(es, "ubm%d" % i, [128, 512], F32), mk(es, "t1m%d" % i, [128, 512], F32), mk(es, "t2m%d" % i, [128, 512], F32)) for i in range(2)]
                qn = mk(es, "qn", [128, 384], BF16)
                qnT = mk(es, "qnT", [128, 3, 128], BF16)
                kr = mk(es, "kr", [128, 32], F32)
                QAt = mk(es, "QAt", [128, 8, 96], BF16)
                KAt = mk(es, "KAt", [128, 8, 96], BF16)
                QBt = mk(es, "QBt", [128, 512], BF16)
                KBt = mk(es, "KBt", [128, 128], BF16)
                QCt = mk(es, "QCt", [128, 512], BF16)
                KCt = mk(es, "KCt", [128, 128], BF16)
                stg = []
                for i in range(2):
                    stg.append(dict(
                        QTA=mk(es, "sQTA%d" % i, [128, 8, 512], BF16), KTA=mk(es, "sKTA%d" % i, [128, 8, 512], BF16),
                        VA=mk(es, "sVA%d" % i, [128, 4, 512], BF16),
                        QTB=mk(es, "sQTB%d" % i, [128, 4, 512], BF16), KTB=mk(es, "sKTB%d" % i, [128, 512], BF16),
                        VB=mk(es, "sVB%d" % i, [128, 4, 128], BF16),
                        QTC=mk(es, "sQTC%d" % i, [128, 4, 512], BF16), KTC=mk(es, "sKTC%d" % i, [128, 512], BF16),
                        VC=mk(es, "sVC%d" % i, [128, 4, 128], BF16)))
                pTA = mkp(es, "pTA", [128, 8, 128], BF16)
                pTs = [mkp(es, "pTs%d" % i, [128, 8, 128], BF16) for i in range(3)]
                P = [mkp(es, "P%d" % i, [128, 512], F32) for i in range(2)]
                Q = [mkp(es, "Q%d" % i, [128, 512], F32) for i in range(2)]
                tmp = (junk, ss, sd, rs, xs)
                ptc = [0]
                pcc = [0]

                def nextpT():
                    ptc[0] += 1
                    return pTs[ptc[0] % 3]

                blocks = [(0, 2)] + [(2 + 4 * b, 4) for b in range(NLT // 4)]
                tiles = []
                for bi, (tb0, ntile) in enumerate(blocks):
                    for s in range(ntile):
                        tiles.append((tb0 + s, bi, s, s == ntile - 1, tb0, ntile))

                def ld_tile(t, k):
                    S.dma("sp", [(xt[k % 2].t[:], xsrc[t * 128:(t + 1) * 128, :])], xt[k % 2].r, True)
                    S.dma("sp", [(rp[k % 3].t[:, :, 0:64], ropeH[t * 128:(t + 1) * 128]),
                                 (rp[k % 3].t[:, :, 64:96], ropeA[t * 128:(t + 1) * 128])], rp[k % 3].r, True)

                def stage1(k):
                    t = tiles[k][0]
                    r = 1 if t < 2 else 0
                    xb = xt[k % 2]
                    hTb = hT[k % 2]
                    pc = Pc[k % 2]
                    if k + 1 < len(tiles):
                        ld_tile(tiles[k + 1][0], k + 1)
                    yield from norm_mod_gen(xb.t[:], xb.r, tmp, hTb, 0, lambda j: G1.t[:, l, r, j:j + 1],
                                            lambda j: MF.t[:, l, r, j:j + 1], G1.r, pTA)
                    for ci_, (c0, cw) in enumerate(ACH):
                        pb_ = P[pcc[0] % 2]
                        pcc[0] += 1

                        def mm(e, c0=c0, cw=cw, pb_=pb_):
                            for j in range(8):
                                ins = e.matmul(pb_.t[:, 0:cw], lhsT=hTb.t[:, j, :], rhs=wA.t[:, j, c0:c0 + cw],
                                               start=(j == 0), stop=(j == 7))
                            return ins
                        S.op("pe", mm, [hTb.r, wA_r[ci_]], [pb_.r])
                        yield
                        S.op("act", lambda e, c0=c0, cw=cw, pb_=pb_: e.activation(out=pc.t[:, c0:c0 + cw], in_=pb_.t[:, 0:cw], func=AF.Copy),
                             [pb_.r], [pc.r])
                        yield

                def prefix(k):
                    pc = Pc[k % 2]
                    rpb = rp[k % 3]
                    X = pc.t
                    ss12, sd12, rs12, sqt, tb = ss12b[k % 2], sd12b[k % 2], rs12b[k % 2], sqtb[k % 2], tbb[k % 2]
                    for a in range(4):
                        S.op("pool", lambda e, a=a: e.tensor_tensor(out=tb.t[:, a, :], in0=rpb.t[:, a % 2, 0:64],
                                                                    in1=gqkb.t[:, l, a, :], op=ALU.mult), [rpb.r, gqkb.r], [tb.r])
                        yield
                    S.op("act", lambda e: e.activation(out=sqt.t[:, 0:256], in_=X[:, 0:256], func=AF.Square, accum_out=ss12.t[:, 0:1]),
                         [pc.r], [sqt.r, ss12.r])
                    yield
                    S.op("act", lambda e: e.activation(out=sqt.t[:, 0:128], in_=X[:, 256:384], func=AF.Square, accum_out=ss12.t[:, 1:2]),
                         [pc.r], [sqt.r, ss12.r])
                    yield
                    S.op("act", lambda e: e.activation(out=sqt.t[:, 0:640], in_=X[:, 416:1056], func=AF.Square), [pc.r], [sqt.r])
                    yield
                    S.op("dve", lambda e: e.tensor_reduce(out=ss12.t[:, 2:12], in_=sqt.t[:, 0:640].rearrange("p (h d) -> p h d", d=64),
                                                         axis=AX.X, op=ALU.add), [sqt.r], [ss12.r])
                    yield
                    S.op("dve", lambda e: e.tensor_tensor(out=ss12.t[:], in0=ss12.t[:], in1=invn.t[:], op=ALU.mult), [ss12.r, invn.r], [ss12.r])
                    yield
                    S.op("act", lambda e: e.activation(out=sd12.t[:], in_=ss12.t[:], func=AF.Sqrt, bias=epsb.t[:, 0:1]), [ss12.r, epsb.r], [sd12.r])
                    yield
                    S.op("dve", lambda e: e.reciprocal(out=rs12.t[:], in_=sd12.t[:]), [sd12.r], [rs12.r])
                    yield

                def stage2(k, extra=()):
                    t, bi, s, _, _, _ = tiles[k]
                    sg = stg[bi % 2]
                    pc = Pc[k % 2]
                    rpb = rp[k % 3]
                    X = pc.t
                    rs12, tb = rs12b[k % 2], tbb[k % 2]

                    def chainA():
                        S.op("dve", lambda e: e.tensor_scalar(out=qn.t[:, 0:256], in0=X[:, 0:256], scalar1=rs12.t[:, 0:1], scalar2=None,
                                                             op0=ALU.mult), [pc.r, rs12.r], [qn.r])
                        yield
                        S.op("dve", lambda e: e.tensor_scalar(out=qn.t[:, 256:384], in0=X[:, 256:384], scalar1=rs12.t[:, 1:2], scalar2=None,
                                                             op0=ALU.mult), [pc.r, rs12.r], [qn.r])
                        yield
                        pq = nextpT()

                        def trq(e, pq=pq):
                            for j in range(3):
                                ins = e.transpose(pq.t[:, j, :], qn.t[:, j * 128:(j + 1) * 128], ident.t[:])
                            return ins
                        S.op("pe", trq, [qn.r, ident.r], [pq.r])
                        yield
                        for j in range(3):
                            gsc = gqa.t[:, l, j:j + 1] if j < 2 else gkva.t[:, l, 0:1]
                            S.op("dve", lambda e, j=j, gsc=gsc, pq=pq: e.tensor_scalar(out=qnT.t[:, j, :], in0=pq.t[:, j, :], scalar1=gsc,
                                                                                      scalar2=None, op0=ALU.mult), [pq.r, gqa.r, gkva.r], [qnT.r])
                            yield
                        for (c0, cw, qb_) in ((0, 512, Q[0]), (512, 256, Q[1])):
                            def mmq(e, c0=c0, cw=cw, qb_=qb_):
                                for j in range(2):
                                    ins = e.matmul(qb_.t[:, 0:cw], lhsT=qnT.t[:, j, :], rhs=wqb.t[:, j, c0:c0 + cw], start=(j == 0), stop=(j == 1))
                                return ins
                            S.op("pe", mmq, [qnT.r, wqb.r], [qb_.r])
                            yield
                        sa4 = rpb.t[:, 1, 64:96].rearrange("p (a q f) -> p a q f", a=2, q=2)
                        S.op("act", lambda e: e.activation(out=u.t[:, 0:512], in_=Q[0].t[:, :], func=AF.Copy), [Q[0].r], [u.r])
                        yield
                        S.op("act", lambda e: e.activation(out=u.t[:, 512:768], in_=Q[1].t[:, 0:256], func=AF.Copy), [Q[1].r], [u.r])
                        yield
                        for hh, qb_ in ((0, Q[0]), (1, Q[1])):
                            S.op("pe", lambda e, hh=hh, qb_=qb_: e.matmul(qb_.t[:, :], lhsT=qnT.t[:, 2, :], rhs=wkvb.t[:, hh * 512:(hh + 1) * 512],
                                                                          start=True, stop=True), [qnT.r, wkvb.r], [qb_.r])
                            yield
                        u3 = u.t[:].rearrange("p (h d) -> p h d", d=96)
                        S.op("act", lambda e: e.activation(out=QAt.t[:, :, 0:64], in_=u3[:, :, 0:64], func=AF.Copy), [u.r], [QAt.r])
                        yield
                        ca_b = rpb.t[:, 0, 64:96].unsqueeze(1).broadcast_to([128, 8, 32])
                        t13 = t1a.t[:, 0:256].rearrange("p (h d) -> p h d", d=32)
                        S.op("dve", lambda e: e.tensor_tensor(out=t13, in0=u3[:, :, 64:96], in1=ca_b, op=ALU.mult), [u.r, rpb.r], [t1a.r])
                        yield
                        u5 = u3[:, :, 64:96].rearrange("p h (a q f) -> p h a q f", a=2, q=2)
                        t25 = t2a.t[:, 0:256].rearrange("p (h a q f) -> p h a q f", a=2, q=2, f=8)
                        for q_ in range(2):
                            S.op("pool", lambda e, q_=q_: e.tensor_tensor(
                                out=t25[:, :, :, q_, :], in0=u5[:, :, :, 1 - q_, :],
                                in1=sa4[:, :, q_, :].unsqueeze(1).broadcast_to([128, 8, 2, 8]),
                                op=ALU.mult), [u.r, rpb.r], [t2a.r])
                            yield
                        for hh, qb_ in ((0, Q[0]), (1, Q[1])):
                            kv3 = qb_.t[:, :].rearrange("p (h d) -> p h d", d=128)
                            S.op("act", lambda e, hh=hh, kv3=kv3: e.activation(out=KAt.t[:, hh * 4:(hh + 1) * 4, 0:64], in_=kv3[:, :, 0:64], func=AF.Copy),
                                 [qb_.r], [KAt.r])
                            yield
                            S.op("dve", lambda e, hh=hh, kv3=kv3: e.tensor_copy(
                                out=sg["VA"].t[:, s, hh * 256:(hh + 1) * 256].rearrange("p (h d) -> p h d", d=64), in_=kv3[:, :, 64:128]),
                                [qb_.r], [sg["VA"].r])
                            yield
                        S.op("dve", lambda e: e.tensor_tensor(out=QAt.t[:, :, 64:96], in0=t13, in1=t2a.t[:, 0:256].rearrange("p (h d) -> p h d", d=32),
                                                             op=ALU.add), [t1a.r, t2a.r], [QAt.r])
                        yield
                        for (src, dst) in ((KAt, sg["KTA"]), (QAt, sg["QTA"])):
                            pt_ = nextpT()

                            def trh(e, src=src, pt_=pt_):
                                for h in range(8):
                                    ins = e.transpose(pt_.t[0:96, h, :], src.t[:, h, :], ident.t[:])
                                return ins
                            S.op("pe", trh, [src.r, ident.r], [pt_.r])
                            yield
                            S.op("act", lambda e, dst=dst, pt_=pt_: e.activation(out=dst.t[0:96, :, s * 128:(s + 1) * 128], in_=pt_.t[0:96, :, :],
                                                                                 func=AF.Copy), [pt_.r], [dst.r])
                            yield

                    def chainKr():
                        S.op("dve", lambda e: e.tensor_tensor(out=kr.t[:], in0=X[:, 384:416], in1=rpb.t[:, 0, 64:96], op=ALU.mult),
                             [pc.r, rpb.r], [kr.r])
                        yield
                        akr4 = X[:, 384:416].rearrange("p (a q f) -> p a q f", a=2, q=2)
                        sa4 = rpb.t[:, 1, 64:96].rearrange("p (a q f) -> p a q f", a=2, q=2)
                        t24 = t2k.t[:, 0:32].rearrange("p (a q f) -> p a q f", a=2, q=2)
                        for q_ in range(2):
                            S.op("pool", lambda e, q_=q_: e.tensor_tensor(out=t24[:, :, q_, :], in0=akr4[:, :, 1 - q_, :], in1=sa4[:, :, q_, :],
                                                                         op=ALU.mult), [pc.r, rpb.r], [t2k.r])
                            yield
                        S.op("dve", lambda e: e.tensor_tensor(out=kr.t[:], in0=kr.t[:], in1=t2k.t[:, 0:32], op=ALU.add), [kr.r, t2k.r], [kr.r])
                        yield
                        S.op("dve", lambda e: e.tensor_copy(out=KAt.t[:, :, 64:96], in_=kr.t[:].unsqueeze(1).broadcast_to([128, 8, 32])),
                             [kr.r], [KAt.r])
                        yield

                    def chainM(mix):
                        qo, ko, vo = (416, 928, 1056) if mix == 0 else (1184, 1696, 1824)
                        Qt_, Kt_ = (QBt, KBt) if mix == 0 else (QCt, KCt)
                        ub_, t1_, t2_ = mt[mix]
                        for (nh, so_, dstT, ci, si, rcol) in ((8, qo, Qt_, 0, 1, 2), (2, ko, Kt_, 2, 3, 10)):
                            w_ = nh * 64
                            src_ap = X[:, so_:so_ + w_]
                            if mix == 0:
                                S.op("dve", lambda e, src_ap=src_ap, nh=nh, rcol=rcol, w_=w_: e.tensor_tensor(
                                    out=ub_.t[:, 0:w_].rearrange("p (h d) -> p h d", d=64), in0=src_ap.rearrange("p (h d) -> p h d", d=64),
                                    in1=rs12.t[:, rcol:rcol + nh].unsqueeze(2).broadcast_to([128, nh, 64]), op=ALU.mult),
                                    [pc.r, rs12.r], [ub_.r])
                                yield
                                uflat = ub_.t[:, 0:w_]
                                ur = ub_.r
                                Ct = tb.t[:, ci, :]
                                St = tb.t[:, si, :]
                                tr_ = tb.r
                            else:
                                uflat = src_ap
                                ur = pc.r
                                Ct = rpb.t[:, 0, 0:64]
                                St = rpb.t[:, 1, 0:64]
                                tr_ = rpb.r
                            uu = uflat.rearrange("p (h d) -> p h d", d=64)
                            S.op("dve", lambda e, uu=uu, Ct=Ct, nh=nh, w_=w_: e.tensor_tensor(
                                out=t1_.t[:, 0:w_].rearrange("p (h d) -> p h d", d=64), in0=uu,
                                in1=Ct.unsqueeze(1).broadcast_to([128, nh, 64]), op=ALU.mult), [ur, tr_], [t1_.r])
                            yield
                            u5 = uflat.rearrange("p (h a q f) -> p h a q f", a=2, q=2, f=16)
                            t25 = t2_.t[:, 0:w_].rearrange("p (h a q f) -> p h a q f", a=2, q=2, f=16)
                            S4 = St.rearrange("p (a q f) -> p a q f", a=2, q=2)
                            for q_ in range(2):
                                S.op("pool", lambda e, q_=q_, u5=u5, t25=t25, S4=S4, nh=nh: e.tensor_tensor(
                                    out=t25[:, :, :, q_, :], in0=u5[:, :, :, 1 - q_, :],
                                    in1=S4[:, :, q_, :].unsqueeze(1).broadcast_to([128, nh, 2, 16]), op=ALU.mult), [ur, tr_], [t2_.r])
                                yield
                            S.op("dve", lambda e, dstT=dstT, w_=w_: e.tensor_tensor(out=dstT.t[:, 0:w_], in0=t1_.t[:, 0:w_], in1=t2_.t[:, 0:w_], op=ALU.add),
                                 [t1_.r, t2_.r], [dstT.r])
                            yield
                        vdst = sg["VB"] if mix == 0 else sg["VC"]
                        S.op("pool", lambda e, vdst=vdst, vo=vo: e.tensor_copy(out=vdst.t[:, s, :], in_=X[:, vo:vo + 128]), [pc.r], [vdst.r])
                        yield
                        pt_ = nextpT()
                        qdst = sg["QTB"] if mix == 0 else sg["QTC"]
                        kdst = sg["KTB"] if mix == 0 else sg["KTC"]

                        def trb(e, Qt_=Qt_, Kt_=Kt_, pt_=pt_):
                            for j in range(4):
                                e.transpose(pt_.t[:, j, :], Qt_.t[:, j * 128:(j + 1) * 128], ident.t[:])
                            return e.transpose(pt_.t[:, 4, :], Kt_.t[:, :], ident.t[:])
                        S.op("pe", trb, [Qt_.r, Kt_.r, ident.r], [pt_.r])
                        yield
                        S.op("act", lambda e, qdst=qdst, pt_=pt_: e.activation(out=qdst.t[:, :, s * 128:(s + 1) * 128], in_=pt_.t[:, 0:4, :], func=AF.Copy),
                             [pt_.r], [qdst.r])
                        yield
                        S.op("act", lambda e, kdst=kdst, pt_=pt_: e.activation(out=kdst.t[:, s * 128:(s + 1) * 128], in_=pt_.t[:, 4, :], func=AF.Copy),
                             [pt_.r], [kdst.r])
                        yield

                    ga = chainA()
                    gens = [ga] + list(extra) + [chainKr(), ga, chainM(1), chainM(0)]
                    while gens:
                        for g_ in list(gens):
                            if g_ not in gens:
                                continue
                            try:
                                next(g_)
                            except StopIteration:
                                while g_ in gens:
                                    gens.remove(g_)

                def stores(k):
                    t, bi, s, _, tb0, ntile = tiles[k]
                    sg = stg[bi % 2]
                    t0 = tb0 * 128
                    n = ntile * 128
                    S.dma("sp", [(QTA.rearrange("h d t -> d h t")[:, :, t0:t0 + n], sg["QTA"].t[0:96, :, 0:n])], sg["QTA"].r, False)
                    S.dma("sp", [(KTA.rearrange("h d t -> d h t")[:, :, t0:t0 + n], sg["KTA"].t[0:96, :, 0:n])], sg["KTA"].r, False)
                    S.dma("sp", [(VA[t0:t0 + n, :].rearrange("(s p) c -> p s c", p=128), sg["VA"].t[:, 0:ntile, :])], sg["VA"].r, False)
                    for nm, dq, dk, dv in (("B", QTB, KTB, VB), ("C", QTC, KTC, VC)):
                        S.dma("sp", [(dq.rearrange("j d t -> d j t")[:, :, t0:t0 + n], sg["QT" + nm].t[:, :, 0:n])], sg["QT" + nm].r, False)
                        S.dma("sp", [(dk[:, t0:t0 + n], sg["KT" + nm].t[:, 0:n])], sg["KT" + nm].r, False)
                        S.dma("sp", [(dv[t0:t0 + n, :].rearrange("(s p) c -> p s c", p=128), sg["V" + nm].t[:, 0:ntile, :])], sg["V" + nm].r, False)

                ld_tile(tiles[0][0], 0)
                load_weights_A()
                def s1p(k):
                    yield from stage1(k)
                    yield from prefix(k)
                for _ in s1p(0):
                    pass
                for k in range(len(tiles)):
                    stage2(k, [s1p(k + 1)] if k + 1 < len(tiles) else [])
                    if tiles[k][3]:
                        stores(k)
                S.barrier()

            with contextlib.ExitStack() as es:
                KT = [mk(es, "KT%d" % i, [128, T], BF16) for i in range(2)]
                QT = [mk(es, "QT%d" % i, [128, 4, T], BF16) for i in range(2)]
                V = [mk(es, "V%d" % i, [128, NT, 65], BF16) for i in range(2)]
                Osb = [mk(es, "Os%d" % i, [128, NT, 512], BF16) for i in range(2)]
                Os = Osb[0]
                PT = [mk(es, "PT%d" % i, [128, 1024], BF16) for i in range(4)]
                rl = [mk(es, "rl%d" % i, [128, 4], F32) for i in range(2)]
                Sb = [mkp(es, "Sb%d" % i, [128, 1024], F32) for i in range(3)]
                Ob = [mkp(es, "Ob%d" % i, [128, 512], F32) for i in range(2)]
                for i in range(2):
                    S.op("dve", lambda e, i=i: e.memset(V[i].t[:, :, 64:65], 1.0), [], [V[i].r])
                MB = mk(es, "MB", [128, 2, 512], BF16)
                for m_ in range(2):
                    S.op("dve", lambda e, m_=m_: e.tensor_scalar(out=MB.t[:, m_, :].rearrange("p (j q) -> p j q", q=128),
                                                                in0=masks.t[:, m_, :].unsqueeze(1).broadcast_to([128, 4, 128]),
                                                                scalar1=-1.0, scalar2=30000.0, op0=ALU.add, op1=ALU.mult), [masks.r], [MB.r])
                cnt = {"s": 0, "p": 0, "o": 0}
                osr = [Osb[0].r]

                free_ob = [1, 0]

                def attn_gen(kt_ap_fn, q_ap, nq, v_ap_fn, kts, scale, reads, out_ap, nsub, sink_ap=None, mask_fn=None, LOOK=1):
                    obi = free_ob.pop()
                    ob = Ob[obi]
                    rlb = rl[obi]
                    O3 = ob.t[:, 0:260].rearrange("p (s c) -> p s c", c=65)
                    nk = len(kts)
                    pairs = [kts[i:i + 2] for i in range(0, nk, 2)]
                    npair = len(pairs)

                    def qk(pi):
                        sb_ = Sb[cnt["s"] % 3]
                        cnt["s"] += 1
                        pt_ = PT[cnt["p"] % 4]
                        cnt["p"] += 1
                        pk = pairs[pi]

                        def f(e):
                            for x_, kt in enumerate(pk):
                                o2 = sb_.t[:, x_ * 512:x_ * 512 + nq]
                                so_ = o2 if len(q_ap.shape) == 2 else o2.rearrange("p (j q) -> p j q", q=128)
                                m = mask_fn(kt) if mask_fn is not None else None
                                ins = e.matmul(so_, lhsT=kt_ap_fn(kt), rhs=q_ap, start=True, stop=(m is None))
                                if m is not None:
                                    ins = e.matmul(o2, lhsT=ident.t[:], rhs=MB.t[:, m, 0:nq], start=False, stop=True)
                            return ins
                        S.op("pe", f, reads + [ident.r, MB.r], [sb_.r])
                        if nq == 512 or len(pk) == 1:
                            w_ = nq if len(pk) == 1 else 1024
                            S.op("act", lambda e: e.activation(out=pt_.t[:, 0:w_], in_=sb_.t[:, 0:w_], func=AF.Exp, scale=scale), [sb_.r], [pt_.r])
                        else:
                            S.op("act", lambda e: e.activation(out=pt_.t[:, :].rearrange("p (x c) -> p x c", x=2)[:, :, 0:nq],
                                                               in_=sb_.t[:, :].rearrange("p (x c) -> p x c", x=2)[:, :, 0:nq], func=AF.Exp, scale=scale),
                                 [sb_.r], [pt_.r])
                        return pt_

                    def pv(pi, pt_):
                        pk = pairs[pi]

                        def f(e):
                            for x_, kt in enumerate(pk):
                                for s_ in range(nsub):
                                    first = (pi == 0 and x_ == 0 and s_ == 0)
                                    lastm = (pi == npair - 1 and x_ == len(pk) - 1 and s_ == nsub - 1)
                                    ins = e.matmul(O3[:, s_, :], lhsT=pt_.t[:, x_ * 512 + s_ * 128:x_ * 512 + (s_ + 1) * 128], rhs=v_ap_fn(kt),
                                                   start=first, stop=lastm, skip_group_check=True)
                            return ins
                        S.op("pe", f, [pt_.r] + reads, [ob.r])
                    pts = {}
                    for i in range(min(LOOK, npair)):
                        pts[i] = qk(i)
                        yield
                    for i in range(npair):
                        if i + LOOK < npair:
                            pts[i + LOOK] = qk(i + LOOK)
                            yield
                        pv(i, pts.pop(i))
                        yield
                    if sink_ap is not None:
                        S.op("dve", lambda e: e.tensor_tensor(out=rlb.t[:, 0:nsub], in0=O3[:, 0:nsub, 64], in1=sink_ap, op=ALU.add),
                             [ob.r, esink.r], [rlb.r])
                        S.op("dve", lambda e: e.reciprocal(out=rlb.t[:, 0:nsub], in_=rlb.t[:, 0:nsub]), [rlb.r], [rlb.r])
                    else:
                        S.op("dve", lambda e: e.reciprocal(out=rlb.t[:, 0:nsub], in_=O3[:, 0:nsub, 64]), [ob.r], [rlb.r])
                    S.op("dve", lambda e: e.tensor_tensor(out=out_ap, in0=O3[:, 0:nsub, 0:64],
                                                         in1=rlb.t[:, 0:nsub].unsqueeze(2).broadcast_to([128, nsub, 64]), op=ALU.mult),
                         [ob.r, rlb.r], [osr[0]])
                    free_ob.append(obi)

                def run_units(units, width=2):
                    it = iter(units)
                    active = []

                    def refill():
                        while len(active) < width:
                            try:
                                a_, k_ = next(it)
                            except StopIteration:
                                return
                            active.append(attn_gen(*a_, **k_))
                    refill()
                    while active:
                        for g_ in list(active):
                            try:
                                next(g_)
                            except StopIteration:
                                active.remove(g_)
                                refill()

                all_k = list(range(NT))
                ctx_k = [0, 1]
                def loadA(h, i):
                    S.dma("sp", [(KT[i].t[0:96, :], KTA[h])], KT[i].r, True)
                    S.dma("sp", [(QT[i].t[0:96, 0, :], QTA[h])], QT[i].r, True)
                    S.dma("sp", [(V[i].t[:, :, 0:64], VA[:, h * 64:(h + 1) * 64].rearrange("(k p) d -> p k d", p=128))], V[i].r, True)
                loadA(0, 0)
                for h in range(8):
                    i = h % 2
                    if h + 1 < 8:
                        loadA(h + 1, (h + 1) % 2)
                    rd = [KT[i].r, QT[i].r, V[i].r]
                    ktf = (lambda i: (lambda kt: KT[i].t[0:96, kt * 128:(kt + 1) * 128]))(i)
                    vf = (lambda i: (lambda kt: V[i].t[:, kt, :]))(i)
                    units = []
                    if not last:
                        units.append(((ktf, QT[i].t[0:96, 0, 0:256], 256, vf, ctx_k, SC_A, rd, Os.t[:, 0:2, h * 64:(h + 1) * 64], 2), {}))
                    for c in range(NLT // 4):
                        q0 = 256 + c * 512
                        units.append(((ktf, QT[i].t[0:96, 0, q0:q0 + 512], 512, vf, all_k, SC_A, rd,
                                       Os.t[:, 2 + c * 4:6 + c * 4, h * 64:(h + 1) * 64], 4), {}))
                    run_units(units)
                    if h < 7:
                        precast_piece(l, h, [Os.r])
                lo = 0 if not last else 2
                S.dma("act", [(OA[lo * 128:T, :].rearrange("(k p) c -> p k c", p=128), Os.t[:, lo:NT, :])], Os.r, False)
                for mix in range(2):
                    dq, dk, dv, do = (QTB, KTB, VB, OB) if mix == 0 else (QTC, KTC, VC, OC)
                    Os = Osb[(mix + 1) % 2]
                    osr[0] = Os.r
                    KTm = KT[mix]
                    S.dma("sp", [(KTm.t[:, :], dk[:, :])], KTm.r, True)
                    if mix == 0:
                        S.op("dve", lambda e: e.memset(QT[0].t[64:128, :, :], 0.0), [], [QT[0].r])
                        S.op("pool", lambda e: e.memset(QT[1].t[0:64, :, :], 0.0), [], [QT[1].r])
                    for g in range(2):
                        S.dma("sp", [(QT[g].t[g * 64:(g + 1) * 64, :, :], dq.rearrange("j d t -> d j t")[g * 64:(g + 1) * 64])], QT[g].r, True)
                        S.dma("sp", [(V[g].t[:, :, 0:64], dv[:, g * 64:(g + 1) * 64].rearrange("(k p) d -> p k d", p=128))], V[g].r, True)
                    for g in range(2):
                        rd = [KTm.r, QT[g].r, V[g].r]
                        units = []
                        ktf = (lambda KTm: (lambda kt: KTm.t[:, kt * 128:(kt + 1) * 128]))(KTm)
                        vf = (lambda g: (lambda kt: V[g].t[:, kt, :]))(g)
                        for t in range(lo, NT):
                            if t < 2:
                                kts = ctx_k
                            elif mix == 0:
                                kts = all_k
                            else:
                                kts = ctx_k + [k for k in (t - 1, t, t + 1) if 2 <= k < NT]
                            mf = None
                            if mix == 1 and t >= 2:
                                def mf(kt, t=t):
                                    if kt < 2:
                                        return None
                                    if kt == t - 1:
                                        return 0
                                    if kt == t + 1:
                                        return 1
                                    return None
                            units.append(((ktf, QT[g].t[:, :, t * 128:(t + 1) * 128], 512, vf, kts, SC_H, rd,
                                           Os.t[:, t, g * 256:(g + 1) * 256].rearrange("p (s c) -> p s c", c=64), 4),
                                          dict(sink_ap=(esink.t[:, l, g * 4:(g + 1) * 4] if mix == 1 else None), mask_fn=mf)))
                        run_units(units)
                    S.dma("act", [(do[lo * 128:T, :].rearrange("(k p) c -> p k c", p=128), Os.t[:, lo:NT, :])], Os.r, False)
                S.barrier()

            tlo = 2 if last else 0
            with contextlib.ExitStack() as es:
                wG = mk(es, "wG", [128, 8, 3 * D], BF16)
                wbr = mk(es, "wbr", [128, 3, 4, D], BF16)
                wo = mk(es, "wo", [128, 8, D], BF16)
                wG_r = [S.newres("wG%d" % q) for q in range(4)]
                wbr_r = [S.newres("wbr%d" % q) for q in range(4)]

                def load_weights_C1():
                  for q in range(4):
                    S.dma("sp", [(wG.t[:, :, i * D + q * 256:i * D + (q + 1) * 256],
                                  wGb[l].rearrange("(j p) c -> p j c", p=128)[:, :, i * D + q * 256:i * D + (q + 1) * 256]) for i in range(3)], wG_r[q], True)
                    S.dma("sp", [(wbr.t[:, i, :, q * 256:(q + 1) * 256],
                                  wbrb[l, i].rearrange("(k p) c -> p k c", p=128)[:, :, q * 256:(q + 1) * 256]) for i in range(3)], wbr_r[q], True)
                  S.dma("sp", [(wo.t[:, :, :], wob[l].rearrange("(j p) c -> p j c", p=128))], wo.r, True)
                m2b = mk(es, "m2b", [128, 2, D], F32)
                S.dma("sp", [(m2b.t[:, r, :], mscr[l, r, 2048:3072].unsqueeze(0).partition_broadcast(128)) for r in range(2)], m2b.r, True)
                xt = [mk(es, "cxt%d" % i, [128, 2, D], F32) for i in range(3)]
                ot = [mk(es, "cot%d" % i, [128, 3, 2, 512], BF16) for i in range(2)]
                junk = mk(es, "cjunk", [128, D], F32)
                ss = mk(es, "css", [128, 1], F32)
                sd = mk(es, "csd", [128, 1], F32)
                rs = mk(es, "crs", [128, 1], F32)
                xs = mk(es, "cxs", [128, D], BF16)
                tmp = (junk, ss, sd, rs, xs)
                hT = [mk(es, "chT%d" % i, [128, 8, 256], BF16) for i in range(2)]
                h2T = mk(es, "ch2T", [128, 8, 256], BF16)
                oT = [mk(es, "coT%d" % i, [128, 3, 4, 256], BF16) for i in range(2)]
                gt = [mk(es, "cgt%d" % i, [128, 3, 256], F32) for i in range(2)]
                ya = mk(es, "cya", [128, 256], F32)
                yb = mk(es, "cyb", [128, 256], F32)
                rj = mk(es, "crj", [128, 512], F32)
                yT = [mk(es, "cyT%d" % i, [128, 8, 256], BF16) for i in range(2)]
                pT = [mkp(es, "cpT%d" % i, [128, 8, 128], BF16) for i in range(2)]
                PG = [mkp(es, "cPG%d" % i, [128, 512], F32) for i in range(2)]
                PB = [mkp(es, "cPB%d" % i, [128, 512], F32) for i in range(2)]
                PO = [mkp(es, "cPO%d" % i, [128, 512], F32) for i in range(2)]
                ptc = [0]

                def nextpT():
                    ptc[0] += 1
                    return pT[ptc[0] % 2]
                nblk = (NT - tlo) // 2

                def blk_t0(b):
                    return (tlo + 2 * b) * 128

                def loadblk(b):
                    t0 = blk_t0(b)
                    S.dma("sp", [(xt[b % 3].t[:], (xsrc[t0:t0 + 256, :]).rearrange("(s p) c -> p s c", p=128))], xt[b % 3].r, True)
                    S.dma("sp", [(ot[b % 2].t[:, i], do[t0:t0 + 256, :].rearrange("(s p) c -> p s c", p=128)) for i, do in enumerate((OA, OB, OC))],
                          ot[b % 2].r, True)

                def front(b):
                    t0 = blk_t0(b)
                    r = 1 if t0 < 256 else 0
                    xb = xt[b % 3]
                    ob_ = ot[b % 2]
                    for s in range(2):
                        yield from norm_mod_gen(xb.t[:, s, :], xb.r, tmp, hT[b % 2], s * 128, lambda j: G1.t[:, l, r, j:j + 1],
                                                lambda j: MF.t[:, l, r, j:j + 1], G1.r, nextpT(), pad=2)
                    for i in range(3):
                        for s in range(2):
                            pt_ = nextpT()

                            def tro(e, i=i, s=s, pt_=pt_):
                                for k in range(4):
                                    ins = e.transpose(pt_.t[:, k, :], ob_.t[:, i, s, k * 128:(k + 1) * 128], ident.t[:])
                                return ins
                            S.op("pe", tro, [ob_.r, ident.r], [pt_.r])
                            yield
                            S.op("act", lambda e, i=i, s=s, pt_=pt_: e.activation(out=oT[b % 2].t[:, i, :, s * 128:(s + 1) * 128], in_=pt_.t[:, 0:4, :],
                                                                                 func=AF.Copy), [pt_.r], [oT[b % 2].r])
                            yield

                def mainc(b):
                    hTb, oTb, yTb = hT[b % 2], oT[b % 2], yT[b % 2]
                    c_ = 0
                    for ft in range(8):
                        gtb = gt[ft % 2]
                        banks = []
                        for i in range(3):
                            pg_, pb_ = PG[c_ % 2], PB[c_ % 2]
                            c_ += 1
                            banks.append(pb_)

                            def mg(e, i=i, ft=ft, pg_=pg_):
                                for j in range(8):
                                    ins = e.matmul(pg_.t[:, 0:256], lhsT=wG.t[:, j, i * D + ft * 128:i * D + (ft + 1) * 128], rhs=hTb.t[:, j, :],
                                                   start=(j == 0), stop=(j == 7))
                                return ins
                            S.op("pe", mg, [wG_r[ft // 2], hTb.r], [pg_.r])
                            yield
                            S.op("act", lambda e, i=i, ft=ft, gtb=gtb, pg_=pg_: e.activation(out=gtb.t[:, i, :], in_=pg_.t[:, 0:256], func=AF.Sigmoid,
                                                                                            bias=bgT.t[:, l, i * 8 + ft:i * 8 + ft + 1]), [pg_.r, bgT.r], [gtb.r])
                            yield

                            def mb(e, i=i, ft=ft, pb_=pb_):
                                for k in range(4):
                                    ins = e.matmul(pb_.t[:, 0:256], lhsT=wbr.t[:, i, k, ft * 128:(ft + 1) * 128], rhs=oTb.t[:, i, k, :],
                                                   start=(k == 0), stop=(k == 3))
                                return ins
                            S.op("pe", mb, [wbr_r[ft // 2], oTb.r], [pb_.r])
                            yield
                            if i == 0:
                                S.op("dve", lambda e, gtb=gtb, pb_=pb_: e.tensor_tensor(out=ya.t[:], in0=pb_.t[:, 0:256], in1=gtb.t[:, 0, :], op=ALU.mult),
                                     [pb_.r, gtb.r], [ya.r])
                                yield
                            elif i == 1:
                                S.op("dve", lambda e, gtb=gtb, pb_=pb_: e.tensor_tensor(out=yb.t[:], in0=pb_.t[:, 0:256], in1=gtb.t[:, 1, :], op=ALU.mult),
                                     [pb_.r, gtb.r], [yb.r])
                                yield
                                S.op("pool", lambda e: e.tensor_tensor(out=ya.t[:], in0=ya.t[:], in1=yb.t[:], op=ALU.add), [ya.r, yb.r], [ya.r])
                                yield
                            else:
                                S.op("dve", lambda e, gtb=gtb, pb_=pb_: e.tensor_tensor(out=yb.t[:], in0=pb_.t[:, 0:256], in1=gtb.t[:, 2, :], op=ALU.mult),
                                     [pb_.r, gtb.r], [yb.r])
                                yield
                                S.op("pool", lambda e, ft=ft: e.tensor_tensor(out=yTb.t[:, ft, :], in0=ya.t[:], in1=yb.t[:], op=ALU.add), [ya.r, yb.r], [yTb.r])
                                yield

                def tail(b):
                    t0 = blk_t0(b)
                    r = 1 if t0 < 256 else 0
                    xb = xt[b % 3]
                    yTb = yT[b % 2]
                    for s in range(2):
                        for c in range(2):
                            pb_ = PO[(s * 2 + c) % 2]

                            def mo(e, s=s, c=c, pb_=pb_):
                                for j in range(8):
                                    ins = e.matmul(pb_.t[:, :], lhsT=yTb.t[:, j, s * 128:(s + 1) * 128], rhs=wo.t[:, j, c * 512:(c + 1) * 512],
                                                   start=(j == 0), stop=(j == 7))
                                return ins
                            S.op("pe", mo, [yTb.r, wo.r], [pb_.r])
                            yield
                            yield
                            S.op("dve", lambda e, c=c, pb_=pb_: e.tensor_tensor(out=rj.t[:, 0:512], in0=pb_.t[:, :], in1=m2b.t[:, r, c * 512:(c + 1) * 512],
                                                                                op=ALU.mult), [pb_.r, m2b.r], [rj.r])
                            yield
                            S.op("pool", lambda e, s=s, c=c: e.tensor_tensor(out=xb.t[:, s, c * 512:(c + 1) * 512], in0=xb.t[:, s, c * 512:(c + 1) * 512],
                                                                             in1=rj.t[:, 0:512], op=ALU.add), [xb.r, rj.r], [xb.r])
                            yield
                    S.dma("sp", [(x1s[t0:t0 + 256, :].rearrange("(s p) c -> p s c", p=128), xb.t[:])], xb.r, False)
                    for s in range(2):
                        yield from norm_mod_gen(xb.t[:, s, :], xb.r, tmp, h2T, s * 128, lambda j: G2.t[:, l, r, j:j + 1],
                                                lambda j: MF.t[:, l, r, 24 + j:25 + j], G2.r, nextpT(), pad=2)
                    S.dma("sp", [(H2T.rearrange("j p t -> p j t")[:, :, t0:t0 + 256], h2T.t[:])], h2T.r, False)
                    yield

                def chain(*gs):
                    for g_ in gs:
                        yield from g_

                def rr(gens):
                    gens = list(gens)
                    while gens:
                        for g_ in list(gens):
                            try:
                                next(g_)
                            except StopIteration:
                                gens.remove(g_)

                loadblk(0)
                load_weights_C1()
                rr([front(0)])
                if nblk > 1:
                    loadblk(1)
                for b in range(nblk):
                    side = []
                    if b >= 1:
                        side.append(tail(b - 1))
                    if b + 1 < nblk:
                        side.append(front(b + 1))
                    rr([mainc(b), chain(*side)])
                    if b + 2 < nblk:
                        loadblk(b + 2)
                rr([tail(nblk - 1)])
                S.barrier()

            with contextlib.ExitStack() as es:
                wu = mk(es, "wu", [128, 8, 2 * DFF], BF16)
                wd = mk(es, "wd", [128, 22, D], BF16)
                wu_r = [S.newres("wu%d" % q) for q in range(4)]
                wd_r = [S.newres("wd%d" % q) for q in range(2)]

                def load_weights_C2():
                  for q in range(4):
                    f0, f1 = q * 6 * 128, min(22, (q + 1) * 6) * 128
                    S.dma("sp", [(wu.t[:, :, h_ * DFF + f0:h_ * DFF + f1], wub[l].rearrange("(j p) c -> p j c", p=128)[:, :, h_ * DFF + f0:h_ * DFF + f1])
                                 for h_ in range(2)], wu_r[q], True)
                  for q in range(2):
                    S.dma("sp", [(wd.t[:, q * 11:(q + 1) * 11, :], wdb[l, q * 11 * 128:(q + 1) * 11 * 128, :].rearrange("(j p) c -> p j c", p=128))], wd_r[q], True)
                m5b = mk(es, "m5b", [128, 2, D], F32)
                S.dma("sp", [(m5b.t[:, r, :], mscr[l, r, 5120:6144].unsqueeze(0).partition_broadcast(128)) for r in range(2)], m5b.r, True)
                hb = [mk(es, "fh%d" % i, [128, 8, 256], BF16) for i in range(2)]
                cab = [mk(es, "fca%d" % i, [128, 256], F32) for i in range(2)]
                cgb = [mk(es, "fcg%d" % i, [128, 256], F32) for i in range(2)]
                sab = [mk(es, "fsa%d" % i, [128, 256], F32) for i in range(2)]
                aT = mk(es, "faT", [128, 22, 256], BF16)
                x1 = [mk(es, "fx1%d" % i, [128, D], F32) for i in range(2)]
                xo = [mk(es, "fxo%d" % i, [128, D], F32) for i in range(2)]
                fj = mk(es, "fj", [128, D], F32)
                fss = mk(es, "fss", [128, 1], F32)
                fsd = mk(es, "fsd", [128, 1], F32)
                frs = mk(es, "frs", [128, 1], F32)
                PU = [mkp(es, "fPU%d" % i, [128, 512], F32) for i in range(6)]
                PD = [mkp(es, "fPD%d" % i, [128, 512], F32) for i in range(2)]
                seqs = ([] if last else [(0, CTX, 1)]) + [(CTX, SEQ, 0)]
                blks = []
                for (s0, sl, r) in seqs:
                    o = 0
                    while o < sl:
                        n = min(254, sl - o)
                        blks.append((s0, sl, r, o, n))
                        o += n

                def loadh(bi):
                    s0, sl, r, o, n = blks[bi]
                    hbb = hb[bi % 2]
                    lo_ = max(o - 1, 0)
                    hi_ = min(o + n + 1, sl)
                    c0 = lo_ - (o - 1)
                    if o == 0:
                        S.op("dve", lambda e: e.memset(hbb.t[:, :, 0:1], 0.0), [], [hbb.r])
                    if o + n == sl:
                        S.op("dve", lambda e: e.memset(hbb.t[:, :, n + 1:n + 2], 0.0), [], [hbb.r])
                    S.dma("sp", [(hbb.t[:, :, c0:c0 + hi_ - lo_], H2T.rearrange("j p t -> p j t")[:, :, s0 + lo_:s0 + hi_])], hbb.r, True)
                loadh(0)
                load_weights_C2()
                xc = 0
                for bi, (s0, sl, r, o, n) in enumerate(blks):
                    hbb = hb[bi % 2]
                    if bi + 1 < len(blks):
                        loadh(bi + 1)
                    N = n + 2
                    for ft in range(22):
                        ca, cg, sa = cab[ft % 2], cgb[ft % 2], sab[ft % 2]
                        pa = PU[(2 * ft) % 6]
                        pg = PU[(2 * ft + 1) % 6]
                        for (pp, col) in ((pa, ft * 128), (pg, DFF + ft * 128)):
                            def mu(e, pp=pp, col=col):
                                for j in range(8):
                                    ins = e.matmul(pp.t[:, 0:N], lhsT=wu.t[:, j, col:col + 128], rhs=hbb.t[:, j, 0:N], start=(j == 0), stop=(j == 7))
                                return ins
                            S.op("pe", mu, [wu_r[ft // 6], hbb.r], [pp.r])
                        for (pp, dst, fi) in ((pa, ca, ft), (pg, cg, 22 + ft)):
                            S.op("act", lambda e, pp=pp, dst=dst, fi=fi: e.activation(out=dst.t[:, 0:n], in_=pp.t[:, 1:n + 1], func=AF.Identity,
                                                                                     scale=cvw.t[:, l, 1, fi:fi + 1], bias=cvw.t[:, l, 3, fi:fi + 1]),
                                 [pp.r, cvw.r], [dst.r])
                            S.op("dve", lambda e, pp=pp, dst=dst, fi=fi: e.scalar_tensor_tensor(out=dst.t[:, 0:n], in0=pp.t[:, 0:n], scalar=cvw.t[:, l, 0, fi:fi + 1],
                                                                                                in1=dst.t[:, 0:n], op0=ALU.mult, op1=ALU.add),
                                 [pp.r, cvw.r, dst.r], [dst.r])
                            S.op("dve", lambda e, pp=pp, dst=dst, fi=fi: e.scalar_tensor_tensor(out=dst.t[:, 0:n], in0=pp.t[:, 2:n + 2], scalar=cvw.t[:, l, 2, fi:fi + 1],
                                                                                                in1=dst.t[:, 0:n], op0=ALU.mult, op1=ALU.add),
                                 [pp.r, cvw.r, dst.r], [dst.r])
                        S.op("act", lambda e: e.activation(out=sa.t[:, 0:n], in_=ca.t[:, 0:n], func=AF.Silu), [ca.r], [sa.r])
                        S.op("pool", lambda e, ft=ft, sa=sa, cg=cg: e.tensor_tensor(out=aT.t[:, ft, 0:n], in0=sa.t[:, 0:n], in1=cg.t[:, 0:n], op=ALU.mult), [sa.r, cg.r], [aT.r])
                    q = 0
                    while q < n:
                        m = min(128, n - q)
                        tok0 = s0 + o + q
                        x1b = x1[xc % 2]
                        xob = xo[xc % 2]
                        S.dma("sp", [(x1b.t[0:m, :], x1s[tok0:tok0 + m, :])], x1b.r, True)
                        for c in range(2):
                            pb_ = PD[c]

                            def md(e, c=c, pb_=pb_, q=q, m=m):
                                for j in range(22):
                                    ins = e.matmul(pb_.t[0:m, :], lhsT=aT.t[:, j, q:q + m], rhs=wd.t[:, j, c * 512:(c + 1) * 512], start=(j == 0), stop=(j == 21))
                                return ins
                            S.op("pe", md, [aT.r, wd_r[0], wd_r[1]], [pb_.r])
                            S.op("dve", lambda e, c=c, pb_=pb_, m=m: e.tensor_tensor(out=fj.t[0:m, 0:512], in0=pb_.t[0:m, :], in1=m5b.t[0:m, r, c * 512:(c + 1) * 512],
                                                                                     op=ALU.mult), [pb_.r, m5b.r], [fj.r])
                            S.op("dve", lambda e, c=c, m=m, x1b=x1b, xob=xob: e.tensor_tensor(out=xob.t[0:m, c * 512:(c + 1) * 512], in0=x1b.t[0:m, c * 512:(c + 1) * 512],
                                                                                              in1=fj.t[0:m, 0:512], op=ALU.add), [x1b.r, fj.r], [xob.r])
                        if not last:
                            S.dma("pool", [(xres[tok0:tok0 + m, :], xob.t[0:m, :])], xob.r, False)
                        else:
                            S.op("act", lambda e, m=m, xob=xob: e.activation(out=fj.t[0:m, :], in_=xob.t[0:m, :], func=AF.Square, accum_out=fss.t[0:m, :]),
                                 [xob.r], [fj.r, fss.r])
                            S.op("dve", lambda e, m=m: e.tensor_scalar(out=fss.t[0:m, :], in0=fss.t[0:m, :], scalar1=1.0 / D, scalar2=EPS, op0=ALU.mult, op1=ALU.add),
                                 [fss.r], [fss.r])
                            S.op("act", lambda e, m=m: e.activation(out=fsd.t[0:m, :], in_=fss.t[0:m, :], func=AF.Sqrt), [fss.r], [fsd.r])
                            S.op("dve", lambda e, m=m: e.reciprocal(out=frs.t[0:m, :], in_=fsd.t[0:m, :]), [fsd.r], [frs.r])
                            S.op("dve", lambda e, m=m, xob=xob: e.scalar_tensor_tensor(out=xob.t[0:m, :], in0=xob.t[0:m, :], scalar=frs.t[0:m, 0:1], in1=gfb.t[0:m, :],
                                                                                       op0=ALU.mult, op1=ALU.mult), [xob.r, frs.r, gfb.r], [xob.r])
                            S.dma("pool", [(out_d[tok0 - CTX:tok0 - CTX + m, :], xob.t[0:m, :])], xob.r, False)
                        xc += 1
                        q += m
                S.barrier()
    return nc


def _host_consts(SEQ):
    T = CTX + SEQ
    rows = SEQ // 64

    def table(hd):
        nf = hd // 4
        row = np.repeat(np.arange(rows, dtype=np.float32), 64)
        col = np.tile(np.arange(64, dtype=np.float32), rows)
        inv = (np.float32(10000.0) ** (-np.arange(nf, dtype=np.float32) / np.float32(nf))).astype(np.float32)
        ang = np.stack([row[:, None] * inv, col[:, None] * inv], axis=1).astype(np.float32)
        cos = np.cos(ang).astype(np.float32)
        sin = np.sin(ang).astype(np.float32)
        cf = np.ones((T, 2, 2, nf), np.float32)
        sf = np.zeros((T, 2, 2, nf), np.float32)
        cf[CTX:, :, 0, :] = cos
        cf[CTX:, :, 1, :] = cos
        sf[CTX:, :, 0, :] = -sin
        sf[CTX:, :, 1, :] = sin
        return np.stack([cf.reshape(T, hd), sf.reshape(T, hd)], axis=1)
    p = np.arange(128)[:, None]
    f = np.arange(128)[None, :]
    masks = np.stack([(f <= p), (p <= f)]).astype(np.float32).astype(ml_dtypes.bfloat16)
    return dict(ropeH=np.ascontiguousarray(table(64)), ropeA=np.ascontiguousarray(table(32)),
                ident=np.eye(128, dtype=np.float32).astype(ml_dtypes.bfloat16), masks=masks)


_NC_CACHE = {}


def _prep_inputs(inp, SEQ, nb):
    f = lambda a: np.ascontiguousarray(np.asarray(a, dtype=np.float32))
    w_in = f(inp["w_in"])
    perm = np.array([(g * 4 + j) * 64 + d for j in range(4) for g in range(2) for d in range(64)])
    offs = np.cumsum([0, 256, 128, 32, 512, 128, 128, 512, 128, 128])
    colsA = np.concatenate([np.arange(0, 416), offs[3] + perm, np.arange(offs[4], offs[6]), offs[6] + perm, np.arange(offs[7], offs[9])])
    assert colsA.size == NA
    w_inA = np.ascontiguousarray(w_in[:, :, colsA])
    w_inG = np.ascontiguousarray(w_in[:, :, offs[9]:])
    gq = f(inp["g_qn"])
    gk = f(inp["g_kn"])

    def sw(g):
        return g.reshape(L, 2, 2, 16)[:, :, ::-1, :].reshape(L, 64)
    g_qk = np.ascontiguousarray(np.stack([gq, sw(gq), gk, sw(gk)], axis=1))
    shared = dict(
        w_mod=f(inp["w_mod"]), b_mod=f(inp["b_mod"]), g_norm1=f(inp["g_norm1"]), g_norm2=f(inp["g_norm2"]),
        w_inA=w_inA, w_inG=w_inG, b_gate=f(inp["b_gate"]), g_q_a=f(inp["g_q_a"]), w_q_b=f(inp["w_q_b"]),
        g_kv_a=f(inp["g_kv_a"]), w_kv_b=f(inp["w_kv_b"]), g_qk=g_qk, sink=f(inp["sink"]),
        w_branch=f(inp["w_branch"]), w_out=f(inp["w_out"]), w_up=f(inp["w_up"]), w_conv=f(inp["w_conv"]),
        b_conv=f(inp["b_conv"]), w_down=f(inp["w_down"]), g_final=f(inp["g_final"]).reshape(1, D))
    shared.update(_host_consts(SEQ))
    x = f(inp["x"])
    ctx = f(inp["ctx"])
    c = f(inp["c"])
    c_ctx = f(inp["c_ctx"])
    maps = []
    for b in range(nb):
        m = dict(shared)
        m["xin"] = np.ascontiguousarray(np.concatenate([ctx[b], x[b]], axis=0))
        m["cvec"] = np.ascontiguousarray(np.stack([c[b], c_ctx], axis=0))
        maps.append(m)
    return maps


def kernel(**inputs):
    x = np.asarray(inputs["x"])
    nb, SEQ = x.shape[0], x.shape[1]
    if SEQ not in _NC_CACHE:
        _NC_CACHE[SEQ] = build_nc(SEQ)
    nc = _NC_CACHE[SEQ]
    maps = _prep_inputs(inputs, SEQ, nb)
    res = run_bass_kernel_spmd(nc, maps, core_ids=list(range(nb)))
    return np.stack([np.asarray(r["out"], dtype=np.float32) for r in res.results], axis=0)
```
